# Optimizing a Trainium2 kernel written in Bass

```python
import math
import jax
import jax.numpy as jnp
from jax import lax
import numpy as np

D_MODEL = 1024
BATCH = 4
SEQ = 4096
DEPTH = 2

D_FF = 2816
D_RNN = 1024
RNN_BLOCKS = 16
RNN_BLOCK_W = D_RNN // RNN_BLOCKS
RNN_CONV_W = 4
RG_LRU_C = 8.0
D_SCONV = 1024
SCONV_W = 3
N_HEADS = 8
N_KV_HEADS = 2
HEAD_DIM = 128
N_IDX_HEADS = 8
IDX_DIM = 64
INDEX_TOPK_MAX = 256
Q_BLOCK = 128
ROPE_THETA = 500000.0
ROPE_FRACTION_DEN = 4
N_BRANCHES = 3
RMS_EPS = 1e-6

kernel_name = "hybrid_rglru_shortconv_dsa_macaron"

IN_SIZES = (D_RNN, D_RNN,
            D_SCONV, D_SCONV, D_SCONV,
            N_HEADS * HEAD_DIM, N_KV_HEADS * HEAD_DIM, N_KV_HEADS * HEAD_DIM,
            N_IDX_HEADS * IDX_DIM, IDX_DIM, N_IDX_HEADS,
            N_BRANCHES * D_MODEL)
D_IN = sum(IN_SIZES)


def rms_norm(x, g):
    xf = x.astype(jnp.float32)
    y = xf * lax.rsqrt(jnp.mean(xf * xf, axis=-1, keepdims=True) + RMS_EPS)
    return (y * g.astype(jnp.float32)).astype(x.dtype)


def swiglu(x, w_gate_up, w_down):
    g, u = jnp.split(x @ w_gate_up, 2, axis=-1)
    return (jax.nn.silu(g) * u) @ w_down


def causal_dwconv(x, w):
    width = w.shape[0]
    s = x.shape[1]
    xp = jnp.pad(x, ((0, 0), (width - 1, 0), (0, 0)))
    return sum(xp[:, k:k + s] * w[k] for k in range(width))


def partial_rope(x, positions):
    dh = x.shape[-1]
    rot = dh // ROPE_FRACTION_DEN
    half = rot // 2
    inv_freq = ROPE_THETA ** (-jnp.arange(0, rot, 2, dtype=jnp.float32) / rot)
    ang = positions.astype(jnp.float32)[..., None] * inv_freq
    cos = jnp.cos(ang)[:, :, None, :]
    sin = jnp.sin(ang)[:, :, None, :]
    xf = x.astype(jnp.float32)
    x1, x2, xp = xf[..., :half], xf[..., half:rot], xf[..., rot:]
    out = jnp.concatenate([x1 * cos - x2 * sin, x2 * cos + x1 * sin, xp], axis=-1)
    return out.astype(x.dtype)


def rg_lru(x, wa, ba, wx, bx, lam):
    b, s, c = x.shape
    xb = x.reshape(b, s, RNN_BLOCKS, RNN_BLOCK_W)
    r = jax.nn.sigmoid(jnp.einsum('bsni,nij->bsnj', xb, wa).reshape(b, s, c) + ba)
    i = jax.nn.sigmoid(jnp.einsum('bsni,nij->bsnj', xb, wx).reshape(b, s, c) + bx)
    log_a = -RG_LRU_C * r.astype(jnp.float32) * jax.nn.softplus(-lam.astype(jnp.float32))
    a = jnp.exp(log_a)
    in_scale = jnp.sqrt(-jnp.expm1(2.0 * log_a))
    u = in_scale * (i * x).astype(jnp.float32)

    def combine(left, right):
        a1, b1 = left
        a2, b2 = right
        return a1 * a2, a2 * b1 + b2

    _, h = lax.associative_scan(combine, (a, u), axis=1)
    return h.astype(x.dtype)


def dsa_attention(q, k, v, q_idx, k_idx, w_idx):
    b, s, h, dh = q.shape
    g = k.shape[2]
    hpg = h // g
    top_k = min(INDEX_TOPK_MAX, s // 4)
    n_blk = s // Q_BLOCK
    key_pos = jnp.arange(s)
    w_scale = (N_IDX_HEADS ** -0.5) * (IDX_DIM ** -0.5)
    attn_scale = dh ** -0.5

    def to_blocks(t):
        t = t.reshape((b, n_blk, Q_BLOCK) + t.shape[2:])
        return jnp.moveaxis(t, 1, 0)

    k_idx_f = k_idx.astype(jnp.float32)

    def one_block(args):
        qb, qib, wb, start = args
        q_pos = start + jnp.arange(Q_BLOCK)
        rel = jax.nn.relu(jnp.einsum('bqhd,bsd->bqhs', qib.astype(jnp.float32), k_idx_f))
        score = jnp.einsum('bqh,bqhs->bqs', wb.astype(jnp.float32) * w_scale, rel)
        causal = key_pos[None, :] <= q_pos[:, None]
        score = jnp.where(causal[None], score, -jnp.inf)
        _, idx = lax.top_k(score, top_k)
        sel_ok = idx <= q_pos[None, :, None]
        kg = jax.vmap(lambda kk, ii: kk[ii])(k, idx)
        vg = jax.vmap(lambda vv, ii: vv[ii])(v, idx)
        qg = qb.reshape(b, Q_BLOCK, g, hpg, dh)
        logits = jnp.einsum('bqgjd,bqkgd->bqgjk', qg, kg).astype(jnp.float32) * attn_scale
        logits = jnp.where(sel_ok[:, :, None, None, :], logits, -jnp.inf)
        p = jax.nn.softmax(logits, axis=-1).astype(vg.dtype)
        o = jnp.einsum('bqgjk,bqkgd->bqgjd', p, vg)
        return o.reshape(b, Q_BLOCK, h * dh)

    starts = jnp.arange(n_blk) * Q_BLOCK
    out = lax.map(one_block, (to_blocks(q), to_blocks(q_idx), to_blocks(w_idx), starts))
    return jnp.moveaxis(out, 0, 1).reshape(b, s, h * dh)


def token_mixing(u, positions, w_in, rnn_conv_w, rnn_conv_b, rnn_gate_a_w, rnn_gate_a_b,
                 rnn_gate_x_w, rnn_gate_x_b, rnn_lambda, rnn_w_out, sconv_w, sconv_w_out,
                 attn_w_out, w_o):
    b, s, _ = u.shape
    proj = u @ w_in
    cuts = []
    acc = 0
    for sz in IN_SIZES[:-1]:
        acc += sz
        cuts.append(acc)
    (rnn_x, rnn_g, c_b, c_c, c_h, q, k, v, qi, ki, wi, gates) = jnp.split(proj, cuts, axis=-1)

    xa = causal_dwconv(rnn_x, rnn_conv_w) + rnn_conv_b
    ha = rg_lru(xa, rnn_gate_a_w, rnn_gate_a_b, rnn_gate_x_w, rnn_gate_x_b, rnn_lambda)
    ya = (ha * jax.nn.gelu(rnn_g)) @ rnn_w_out

    yb = (c_b * causal_dwconv(c_c * c_h, sconv_w)) @ sconv_w_out

    q = partial_rope(q.reshape(b, s, N_HEADS, HEAD_DIM), positions)
    k = partial_rope(k.reshape(b, s, N_KV_HEADS, HEAD_DIM), positions)
    v = v.reshape(b, s, N_KV_HEADS, HEAD_DIM)
    qi = partial_rope(qi.reshape(b, s, N_IDX_HEADS, IDX_DIM), positions)
    ki = partial_rope(ki.reshape(b, s, 1, IDX_DIM), positions)[:, :, 0]
    yc = dsa_attention(q, k, v, qi, ki, wi) @ attn_w_out

    gt = jax.nn.sigmoid(gates.reshape(b, s, N_BRANCHES, D_MODEL))
    merged = gt[:, :, 0] * ya + gt[:, :, 1] * yb + gt[:, :, 2] * yc
    return merged @ w_o


def setup_inputs(seed: int = 0) -> dict:
    key = jax.random.key(seed)
    ks = iter(jax.random.split(key, 32))
    f32 = jnp.float32

    def nrm(shape, fan_in):
        return jax.random.normal(next(ks), shape, f32) * (fan_in ** -0.5)

    def gain(shape):
        return 1.0 + 0.02 * jax.random.normal(next(ks), shape, f32)

    def bias(shape):
        return 0.01 * jax.random.normal(next(ks), shape, f32)

    L = DEPTH
    x = jax.random.normal(next(ks), (BATCH, SEQ, D_MODEL), f32)
    offs = jax.random.randint(next(ks), (BATCH, 1), 0, 1024, dtype=jnp.int32)
    positions = offs + jnp.arange(SEQ, dtype=jnp.int32)[None, :]
    a0 = jax.random.uniform(next(ks), (L, D_RNN), f32, 0.9, 0.999)
    sig = a0 ** (1.0 / RG_LRU_C)
    rnn_lambda = jnp.log(sig) - jnp.log1p(-sig)
    return {
        "x": x,
        "positions": positions,
        "ffn1_norm": gain((L, D_MODEL)),
        "ffn1_w_gate_up": nrm((L, D_MODEL, 2 * D_FF), D_MODEL),
        "ffn1_w_down": nrm((L, D_FF, D_MODEL), D_FF),
        "mix_norm": gain((L, D_MODEL)),
        "w_in": nrm((L, D_MODEL, D_IN), D_MODEL),
        "rnn_conv_w": nrm((L, RNN_CONV_W, D_RNN), RNN_CONV_W),
        "rnn_conv_b": bias((L, D_RNN)),
        "rnn_gate_a_w": nrm((L, RNN_BLOCKS, RNN_BLOCK_W, RNN_BLOCK_W), RNN_BLOCK_W),
        "rnn_gate_a_b": bias((L, D_RNN)),
        "rnn_gate_x_w": nrm((L, RNN_BLOCKS, RNN_BLOCK_W, RNN_BLOCK_W), RNN_BLOCK_W),
        "rnn_gate_x_b": bias((L, D_RNN)),
        "rnn_lambda": rnn_lambda,
        "rnn_w_out": nrm((L, D_RNN, D_MODEL), D_RNN),
        "sconv_w": nrm((L, SCONV_W, D_SCONV), SCONV_W),
        "sconv_w_out": nrm((L, D_SCONV, D_MODEL), D_SCONV),
        "attn_w_out": nrm((L, N_HEADS * HEAD_DIM, D_MODEL), N_HEADS * HEAD_DIM),
        "w_o": nrm((L, D_MODEL, D_MODEL), D_MODEL),
        "ffn2_norm": gain((L, D_MODEL)),
        "ffn2_w_gate_up": nrm((L, D_MODEL, 2 * D_FF), D_MODEL),
        "ffn2_w_down": nrm((L, D_FF, D_MODEL), D_FF),
        "final_norm": gain((D_MODEL,)),
    }


def reference(x, positions, ffn1_norm, ffn1_w_gate_up, ffn1_w_down, mix_norm, w_in,
              rnn_conv_w, rnn_conv_b, rnn_gate_a_w, rnn_gate_a_b, rnn_gate_x_w, rnn_gate_x_b,
              rnn_lambda, rnn_w_out, sconv_w, sconv_w_out, attn_w_out, w_o,
              ffn2_norm, ffn2_w_gate_up, ffn2_w_down, final_norm):
    for l in range(DEPTH):
        x = x + 0.5 * swiglu(rms_norm(x, ffn1_norm[l]), ffn1_w_gate_up[l], ffn1_w_down[l])
        x = x + token_mixing(rms_norm(x, mix_norm[l]), positions, w_in[l],
                             rnn_conv_w[l], rnn_conv_b[l], rnn_gate_a_w[l], rnn_gate_a_b[l],
                             rnn_gate_x_w[l], rnn_gate_x_b[l], rnn_lambda[l], rnn_w_out[l],
                             sconv_w[l], sconv_w_out[l], attn_w_out[l], w_o[l])
        x = x + 0.5 * swiglu(rms_norm(x, ffn2_norm[l]), ffn2_w_gate_up[l], ffn2_w_down[l])
    return rms_norm(x, final_norm)
```

```python
import math
from contextlib import ExitStack
import numpy as np
import concourse.bass as bass
import concourse.mybir as mybir
from concourse.bass_utils import run_bass_kernel_spmd

F32 = mybir.dt.float32
BF16 = mybir.dt.bfloat16
F32R = mybir.dt.float32r
I32 = mybir.dt.int32
AF = mybir.ActivationFunctionType
ALU = mybir.AluOpType

T = 4096
D = 1024
DFF = 2816
NK = 8
NH = DFF // 128
TT = 512
NT = T // TT
TQ = 256
NQ = T // TQ
DIN = 10312
L = 2
NVEC = 112
NIT = 19
ARENA_R = 29000
ARENA_F = 15100
TWO_PI = 2.0 * math.pi


class Buf:
    _n = 0

    def __init__(self, name=None):
        Buf._n += 1
        self.name = name or ("b%d" % Buf._n)
        self.w = None
        self.r = {}


class Prog:
    CE = ("pe", "act", "dve", "pool")
    ALLE = ("pe", "act", "dve", "pool", "sp")

    def __init__(self):
        self.streams = {e: [] for e in self.ALLE}
        self.cnt = {}
        self.known = {e: {} for e in self.ALLE}

    def op(self, eng, fn, reads=(), writes=(), dma_key=None):
        waits = {}

        def need(tok):
            if tok is None:
                return
            k, v = tok
            if eng == "pe" and k == "pe":
                return
            if self.known[eng].get(k, 0) >= v:
                return
            if waits.get(k, 0) < v:
                waits[k] = v

        for b in reads:
            need(b.w)
        for b in writes:
            need(b.w)
            for k, v in b.r.items():
                need((k, v))
        for k, v in waits.items():
            self.known[eng][k] = v
        key = dma_key if dma_key is not None else eng
        step = 16 if dma_key is not None else 1
        self.cnt[key] = self.cnt.get(key, 0) + step
        tok = (key, self.cnt[key])
        self.streams[eng].append((fn, sorted(waits.items()), tok, step))
        for b in reads:
            if b.r.get(key, 0) < tok[1]:
                b.r[key] = tok[1]
        for b in writes:
            b.w = tok
            b.r = {}
        return tok

    def barrier(self):
        snap = dict(self.cnt)
        for e in self.ALLE:
            waits = []
            for k, v in sorted(snap.items()):
                if e == "pe" and k == "pe":
                    continue
                if self.known[e].get(k, 0) >= v:
                    continue
                self.known[e][k] = v
                waits.append((k, v))
            if waits:
                self.streams[e].append(("wait", waits, None, 0))

    def build(self, nc):
        keys = sorted(self.cnt.keys())
        with ExitStack() as es:
            sems = {}
            for i, k in enumerate(keys):
                sems[k] = es.enter_context(nc.semaphore("s%d" % i))
            block = es.enter_context(nc.Block())
            emap = {"pe": block.tensor, "act": block.scalar, "dve": block.vector,
                    "pool": block.gpsimd, "sp": block.sync}
            for e in self.ALLE:
                stream = self.streams[e]
                if not stream:
                    continue

                def body(engine, stream=stream):
                    for fn, waits, tok, step in stream:
                        for k, v in waits:
                            engine.wait_ge(sems[k], v)
                        if fn == "wait":
                            continue
                        inst = fn(engine)
                        inst.then_inc(sems[tok[0]], step)
                emap[e](body)


class Arena:
    def __init__(self, tensor, size, base):
        self.t = tensor
        self.off = 0
        self.size = size
        self.base = base

    def alloc(self, shape, dtype=F32, name=None):
        n = 1
        for s in shape:
            n *= s
        words = n if dtype in (F32, F32R, I32) else (n + 1) // 2
        words = (words + 7) // 8 * 8
        assert self.off + words <= self.size, ("arena overflow", self.off, words, name)
        ap = self.t[:, self.off:self.off + words]
        self.off += words
        if dtype != self.base:
            ap = ap.bitcast(dtype)
        ap = ap[:, 0:n]
        if len(shape) == 2:
            ap = ap.rearrange("p (a b) -> p a b", a=shape[0], b=shape[1])
        elif len(shape) == 3:
            ap = ap.rearrange("p (a b c) -> p a b c", a=shape[0], b=shape[1], c=shape[2])
        return ap, Buf(name)


def f32(ap):
    return ap.bitcast(F32)


_USED = {}


def build_program(phases="all", dbg_out=("out",), dbg_in=(), nlayers=L):
    nc = bass.Bass("TRN2", target_bir_lowering=False)
    P = Prog()

    used_inputs = []

    class LazyIn:
        def __init__(self, name, shape, dt=F32):
            self.name, self.shape, self.dt, self._ap = name, list(shape), dt, None

        def get(self):
            if self._ap is None:
                self._ap = nc.dram_tensor(self.name, self.shape, self.dt, kind="ExternalInput").ap()
                used_inputs.append(self.name)
            return self._ap

        def __getitem__(self, idx):
            return self.get()[idx]

        def rearrange(self, *a, **k):
            return self.get().rearrange(*a, **k)

        @property
        def tensor(self):
            return self.get().tensor

    def din(name, shape, dt=F32):
        return LazyIn(name, shape, dt)

    class LazyScr(LazyIn):
        def get(self):
            if self._ap is None:
                if self.name in dbg_in:
                    kind = "ExternalInput"
                    used_inputs.append(self.name)
                elif self.name in dbg_out:
                    kind = "ExternalOutput"
                else:
                    kind = "Internal"
                self._ap = nc.dram_tensor(self.name, self.shape, self.dt, kind=kind).ap()
            return self._ap

    def dscr(name, shape, dt=F32):
        return LazyScr(name, shape, dt)

    x_in = din("x", [T, D])
    pos_in = din("positions", [1, T], I32)
    w_gu = [din("ffn1_w_gate_up", [L, D, 2 * DFF]), din("ffn2_w_gate_up", [L, D, 2 * DFF])]
    w_dn = [din("ffn1_w_down", [L, DFF, D]), din("ffn2_w_down", [L, DFF, D])]
    w_in = din("w_in", [L, D, DIN])
    w_ga = din("rnn_gate_a_w", [L, 16, 64, 64])
    w_gx = din("rnn_gate_x_w", [L, 16, 64, 64])
    w_ro = din("rnn_w_out", [L, D, D])
    w_so = din("sconv_w_out", [L, D, D])
    w_ao = din("attn_w_out", [L, D, D])
    w_o = din("w_o", [L, D, D])
    vec_in = din("vec", [L, 128, NVEC])
    fin_in = din("fin", [128, NK])
    cst_in = din("cst", [128, 16])
    ident_in = din("ident", [128, 128])
    sel_in = din("sel", [8, 8 * 128])
    pm_in = din("pm", [2, 128, 128])
    out_l = LazyScr("out", [T, D], F32)

    xT = dscr("xT", [D, T])
    RX = dscr("RX", [D, T])
    GG = dscr("GG", [D, T])
    CB = dscr("CB", [D, T])
    CCH = dscr("CCH", [D, T])
    QT = dscr("QT", [D, T])
    KT = dscr("KT", [256, T])
    VV = dscr("VV", [T, 256])
    QIT = dscr("QIT", [512, T])
    KIT = dscr("KIT", [128, T])
    WIT = dscr("WIT", [8, T])
    GT = dscr("GT", [3 * D, T])
    HG = dscr("HG", [D, T])
    YB = dscr("YB", [D, T])
    AT = dscr("AT", [D, T])
    ROPE = dscr("ROPE", [4, 128, T])

    def cm(ap, c0=None):
        return ap.rearrange("(c p) t -> p c t", p=128)

    with ExitStack() as es:
        arena_r = es.enter_context(nc.sbuf_tensor("arena_r", [128, ARENA_R], F32R))
        arena_f = es.enter_context(nc.sbuf_tensor("arena_f", [128, ARENA_F], F32))
        AR_ = Arena(arena_r, ARENA_R, F32R)
        AF_ = Arena(arena_f, ARENA_F, F32)
        psum = [es.enter_context(nc.psum_tensor("ps%d" % i, [128, 512], F32)) for i in range(8)]
        PS = [Buf("ps%d" % i) for i in range(8)]

        keyn = [0]
        fillreg = {}

        def newkey(pfx):
            keyn[0] += 1
            return "%s%d" % (pfx, keyn[0])

        class Slot:
            def __init__(self, shape, dtype=F32, name=None, region=None):
                if region is None:
                    region = "r" if dtype == F32R else "f"
                self.ap, self.buf = (AR_ if region == "r" else AF_).alloc(shape, dtype, name)
                self.lk = None
                self.sk = None

        def load(slot, src, eng="sp", dst=None):
            if slot.lk is None:
                slot.lk = newkey("L")
            d = slot.ap if dst is None else dst
            if d.dtype == F32R:
                eng = "pool"
            P.op(eng, lambda e: e.dma_start(out=d, in_=src), writes=[slot.buf], dma_key=slot.lk)

        def store(slot, dstdram, src=None, eng="sp"):
            if slot.sk is None:
                slot.sk = newkey("S")
            s = slot.ap if src is None else src
            if s.dtype == F32R:
                s = s.bitcast(F32)
            P.op(eng, lambda e: e.dma_start(out=dstdram, in_=s), reads=[slot.buf], dma_key=slot.sk)

        def mm_group(bank, items, reads, n=512, m=128):
            def fn(e):
                inst = None
                for i, (l, r) in enumerate(items):
                    inst = e.matmul(psum[bank][0:m, 0:n], lhsT=l, rhs=r, start=(i == 0), stop=(i == len(items) - 1))
                return inst
            P.op("pe", fn, reads=reads, writes=[PS[bank]])

        ones_r = Slot([128], F32R, "ones_r")
        ones_b = Slot([128], BF16, "ones_b")
        ident = Slot([128], F32, "ident")
        cst = Slot([16], F32, "cst")
        fin = Slot([NK], F32, "fin")
        vec = [Slot([NVEC], F32, "vec%d" % l) for l in range(L)]
        ones_f = Slot([128], F32, "ones_f")
        P.op("dve", lambda e: e.memset(ones_f.ap, 1.0), writes=[ones_f.buf])
        P.op("act", lambda e: e.copy(out=ones_r.ap, in_=ones_f.ap), reads=[ones_f.buf], writes=[ones_r.buf])
        P.op("dve", lambda e: e.tensor_copy(out=ones_b.ap, in_=ones_f.ap), reads=[ones_f.buf], writes=[ones_b.buf])
        load(ident, ident_in.get())
        load(cst, cst_in.get())
        load(fin, fin_in.get())
        for l in range(L):
            load(vec[l], vec_in[l])
        base_off = (AR_.off, AF_.off)
        base_key = keyn[0]

        def new_phase():
            P.barrier()
            AR_.off, AF_.off = base_off
            keyn[0] = base_key

        def rmsnorm(xt, gcol, gslot, outp, sq, rstd, width, bank=0, cs=slice(None)):
            for c in range(NK):
                s = sq[c % 2]
                P.op("act", lambda e, s=s, c=c: e.activation(out=s.ap, in_=xt.ap[:, c, cs], func=AF.Square),
                     reads=[xt.buf], writes=[s.buf])
                def fn(e, s=s, c=c):
                    return e.matmul(psum[bank][:, 0:width], lhsT=ones_r.ap, rhs=s.ap, start=(c == 0), stop=(c == NK - 1))
                P.op("pe", fn, reads=[s.buf, ones_r.buf], writes=[PS[bank]])
            P.op("act", lambda e: e.activation(out=rstd.ap, in_=psum[bank][:, 0:width], func=AF.Sqrt,
                                               bias=1e-6, scale=1.0 / D),
                 reads=[PS[bank]], writes=[rstd.buf])
            P.op("dve", lambda e: e.reciprocal(out=rstd.ap, in_=rstd.ap), reads=[rstd.buf], writes=[rstd.buf])
            for c in range(NK):
                P.op("dve", lambda e, c=c: e.scalar_tensor_tensor(
                    out=outp.ap[:, c, cs], in0=xt.ap[:, c, cs], scalar=gslot.ap[:, gcol + c:gcol + c + 1],
                    in1=rstd.ap, op0=ALU.mult, op1=ALU.mult),
                    reads=[xt.buf, rstd.buf, gslot.buf], writes=[outp.buf])

        def phase_x0():
            new_phase()
            xin = [Slot([4, D], F32, "xin")] * 2
            xo = [Slot([NK, TT], F32, "xo")] * 2
            xr = x_in.rearrange("(i s p) d -> i p s d", s=4, p=128)
            n = 0
            for i in range(NT):
                xi = xin[i % 2]
                o = xo[i % 2]
                load(xi, xr[i])
                for c in range(NK):
                    bank = n % 2
                    n += 1
                    def fn(e, xi=xi, c=c, bank=bank):
                        inst = None
                        for s in range(4):
                            inst = e.transpose(psum[bank][:, s * 128:(s + 1) * 128],
                                               xi.ap[:, s, c * 128:(c + 1) * 128], ident.ap)
                        return inst
                    P.op("pe", fn, reads=[xi.buf, ident.buf], writes=[PS[bank]])
                    if c % 2 == 0:
                        P.op("act", lambda e, o=o, c=c, bank=bank: e.copy(out=o.ap[:, c, :], in_=psum[bank][:, :]),
                             reads=[PS[bank]], writes=[o.buf])
                    else:
                        P.op("dve", lambda e, o=o, c=c, bank=bank: e.tensor_copy(out=o.ap[:, c, :], in_=psum[bank][:, :]),
                             reads=[PS[bank]], writes=[o.buf])
                store(o, cm(xT)[:, :, i * TT:(i + 1) * TT])

        def phase_out():
            new_phase()
            xt = [Slot([NK, TT], F32, "xt")] * 2
            yn = Slot([NK, TT], F32, "yn")
            sq = [Slot([TT], F32R, "sq%d" % i) for i in range(2)]
            rstd = Slot([TT], F32, "rstd")
            yo = [Slot([4, D], F32, "yo")] * 2
            orr = out_l.rearrange("(i s p) d -> i p s d", s=4, p=128)
            n = 0
            for i in range(NT):
                load(xt[i % 2], cm(xT)[:, :, i * TT:(i + 1) * TT])
                x = xt[i % 2]
                rmsnorm(x, 0, fin, yn, sq, rstd, TT, bank=0)
                o = yo[i % 2]
                for s in range(4):
                    for half in range(2):
                        bank = 1 + n % 2
                        n += 1
                        def fn(e, s=s, half=half, bank=bank):
                            inst = None
                            for cc in range(4):
                                c = half * 4 + cc
                                inst = e.transpose(psum[bank][:, cc * 128:(cc + 1) * 128],
                                                   yn.ap[:, c, s * 128:(s + 1) * 128], ident.ap)
                            return inst
                        P.op("pe", fn, reads=[yn.buf, ident.buf], writes=[PS[bank]])
                        if half == 0:
                            P.op("act", lambda e, o=o, s=s, half=half, bank=bank: e.copy(
                                out=o.ap[:, s, half * 512:(half + 1) * 512], in_=psum[bank][:, :]),
                                reads=[PS[bank]], writes=[o.buf])
                        else:
                            P.op("dve", lambda e, o=o, s=s, half=half, bank=bank: e.tensor_copy(
                                out=o.ap[:, s, half * 512:(half + 1) * 512], in_=psum[bank][:, :]),
                                reads=[PS[bank]], writes=[o.buf])
                store(o, orr[i])

        def phase_ffn(l, which):
            new_phase()
            TF = 1024
            NHH = NH // 2
            wgu = w_gu[which][l]
            wdn = w_dn[which][l]
            gcol = 0 if which == 0 else 16
            xt = Slot([NK, TF], F32, "xt")
            xn = Slot([NK, TF], F32R, "xn")
            sq = [Slot([TT], F32R, "sq%d" % i) for i in range(2)]
            rstd = Slot([TT], F32, "rstd")
            h = Slot([NHH, TF], F32R, "h")
            wg = [Slot([NK, 128], F32R, "wg%d" % i) for i in range(2)]
            wu = [Slot([NK, 128], F32R, "wu%d" % i) for i in range(2)]
            wd = [Slot([NHH, 128], F32R, "wd%d" % i) for i in range(2)]
            sg = [Slot([TT], F32, "sg%d" % i) for i in range(2)]
            wgu_r = wgu.rearrange("(k p) n -> p k n", p=128)
            wdn_r = wdn.rearrange("(j p) n -> p j n", p=128)
            cnt = {"n": 0, "d": 0}

            def tile_body(i):
                x = xt
                load(x, cm(xT)[:, :, i * TF:(i + 1) * TF])
                for s in range(2):
                    rmsnorm(x, gcol, vec[l], xn, sq, rstd, TT, bank=0, cs=slice(s * TT, (s + 1) * TT))
                for half in range(2):
                    for jj in range(NHH):
                        j = half * NHH + jj
                        g_, u_ = wg[j % 2], wu[j % 2]
                        load(g_, wgu_r[:, :, j * 128:(j + 1) * 128])
                        load(u_, wgu_r[:, :, DFF + j * 128:DFF + (j + 1) * 128])
                        for s in range(2):
                            cs = slice(s * TT, (s + 1) * TT)
                            n = cnt["n"]
                            cnt["n"] += 1
                            bg, bu = 1 + n % 2, 3 + n % 2
                            mm_group(bg, [(g_.ap[:, k, :], xn.ap[:, k, cs]) for k in range(NK)], [g_.buf, xn.buf])
                            mm_group(bu, [(u_.ap[:, k, :], xn.ap[:, k, cs]) for k in range(NK)], [u_.buf, xn.buf])
                            s_ = sg[n % 2]
                            P.op("act", lambda e, s_=s_, bg=bg: e.activation(out=s_.ap, in_=psum[bg][:, :], func=AF.Silu),
                                 reads=[PS[bg]], writes=[s_.buf])
                            P.op("dve", lambda e, s_=s_, bu=bu, jj=jj, cs=cs: e.tensor_tensor(
                                out=h.ap[:, jj, cs], in0=s_.ap, in1=psum[bu][:, :], op=ALU.mult),
                                reads=[s_.buf, PS[bu]], writes=[h.buf])
                    for c in range(NK):
                        w_ = wd[c % 2]
                        load(w_, wdn_r[:, half * NHH:(half + 1) * NHH, c * 128:(c + 1) * 128])
                        for s in range(2):
                            cs = slice(s * TT, (s + 1) * TT)
                            d = cnt["d"]
                            cnt["d"] += 1
                            bo = 5 + d % 2
                            mm_group(bo, [(w_.ap[:, jj, :], h.ap[:, jj, cs]) for jj in range(NHH)], [w_.buf, h.buf])
                            P.op("dve", lambda e, c=c, bo=bo, cs=cs: e.scalar_tensor_tensor(
                                out=x.ap[:, c, cs], in0=psum[bo][:, :], scalar=0.5, in1=x.ap[:, c, cs],
                                op0=ALU.mult, op1=ALU.add),
                                reads=[PS[bo], x.buf], writes=[x.buf])
                store(x, cm(xT)[:, :, i * TF:(i + 1) * TF])

            for i in range(T // TF):
                tile_body(i)

        def phase_rope():
            new_phase()
            posi = Slot([TT], I32, "posi")
            posf = Slot([TT], F32, "posf")
            a2 = Slot([TT], F32, "a2")
            ki_ = Slot([TT], I32, "ki")
            kf = Slot([TT], F32, "kf")
            y = Slot([TT], F32, "y")
            m = Slot([TT], F32, "m")
            tab = [Slot([TT], F32, "tab%d" % i) for i in range(2)]
            n = 0
            for i in range(NT):
                load(posi, bass.AP(pos_in.tensor, i * TT, [[0, 128], [1, TT]]), eng="pool")
                P.op("dve", lambda e: e.tensor_copy(out=posf.ap, in_=posi.ap), reads=[posi.buf], writes=[posf.buf])
                for ty in range(2):
                    for cs in range(2):
                        shift = math.pi / 2 if cs == 0 else 0.0
                        P.op("dve", lambda e, ty=ty, shift=shift: e.tensor_scalar(
                            out=a2.ap, in0=posf.ap, scalar1=cst.ap[:, 2 * ty:2 * ty + 1], scalar2=shift,
                            op0=ALU.mult, op1=ALU.add), reads=[posf.buf, cst.buf], writes=[a2.buf])
                        P.op("dve", lambda e: e.tensor_scalar(
                            out=ki_.ap, in0=a2.ap, scalar1=1.0 / TWO_PI, scalar2=None, op0=ALU.mult),
                            reads=[a2.buf], writes=[ki_.buf])
                        P.op("dve", lambda e: e.tensor_copy(out=kf.ap, in_=ki_.ap), reads=[ki_.buf], writes=[kf.buf])
                        P.op("dve", lambda e: e.scalar_tensor_tensor(
                            out=y.ap, in0=kf.ap, scalar=-TWO_PI, in1=a2.ap, op0=ALU.mult, op1=ALU.add),
                            reads=[kf.buf, a2.buf], writes=[y.buf])
                        P.op("dve", lambda e: e.tensor_scalar(
                            out=m.ap, in0=y.ap, scalar1=math.pi, scalar2=-TWO_PI, op0=ALU.is_gt, op1=ALU.mult),
                            reads=[y.buf], writes=[m.buf])
                        P.op("dve", lambda e: e.tensor_tensor(out=y.ap, in0=y.ap, in1=m.ap, op=ALU.add),
                             reads=[y.buf, m.buf], writes=[y.buf])
                        P.op("dve", lambda e: e.tensor_scalar(
                            out=m.ap, in0=y.ap, scalar1=-math.pi, scalar2=TWO_PI, op0=ALU.is_lt, op1=ALU.mult),
                            reads=[y.buf], writes=[m.buf])
                        P.op("dve", lambda e: e.tensor_tensor(out=y.ap, in0=y.ap, in1=m.ap, op=ALU.add),
                             reads=[y.buf, m.buf], writes=[y.buf])
                        P.op("dve", lambda e: e.tensor_scalar(
                            out=y.ap, in0=y.ap, scalar1=3.1415925, scalar2=-3.1415925, op0=ALU.min, op1=ALU.max),
                            reads=[y.buf], writes=[y.buf])
                        tb = tab[n % 2]
                        n += 1
                        P.op("act", lambda e, tb=tb: e.activation(out=tb.ap, in_=y.ap, func=AF.Sin),
                             reads=[y.buf], writes=[tb.buf])
                        store(tb, ROPE[2 * ty + cs, :, i * TT:(i + 1) * TT])

        def phase_proj(l):
            new_phase()
            wi = w_in[l].rearrange("(k p) n -> p k n", p=128)
            xt = [Slot([NK, TT], F32, "xt")] * 2
            un = Slot([NK, TT], F32R, "un")
            sq = [Slot([TT], F32R, "sq%d" % i) for i in range(2)]
            rstd = Slot([TT], F32, "rstd")
            rope = [Slot([4, TT], F32, "rope")] * 2
            W = [Slot([NK, 512], F32R, "W%d" % i) for i in range(3)]
            Wk = Slot([NK, 128], F32R, "Wk")
            Ww = Slot([NK, 128], F32R, "Ww")
            pmat = Slot([2, 128], F32R, "pmat")
            ot = [Slot([TT], F32, "ot%d" % i) for i in range(4)]
            otr = [Slot([TT], F32R, "otr%d" % i) for i in range(4)]
            xs = [Slot([TT], F32R, "xs%d" % i) for i in range(2)]
            t1 = [Slot([TT], F32, "t1_%d" % i) for i in range(2)]
            t2 = [Slot([TT], F32, "t2_%d" % i) for i in range(2)]
            st = {"w": 0, "o": 0, "b": 0, "t": 0, "x": 0}
            load(pmat, pm_in.get().rearrange("a p m -> p a m"))

            def nextW():
                st["w"] += 1
                return W[st["w"] % 3]

            def nexto(r=False):
                st["o"] += 1
                return (otr if r else ot)[st["o"] % 4]

            def nextbank():
                st["b"] += 1
                return 1 + st["b"] % 7

            def nextt():
                st["t"] += 1
                return t1[st["t"] % 2], t2[st["t"] % 2]

            def nextx():
                st["x"] += 1
                return xs[st["x"] % 2]

            def loadblk(col0, ncols=512):
                w_ = nextW()
                load(w_, wi[:, :, col0:col0 + ncols], dst=w_.ap[:, :, 0:ncols])
                return w_

            def proj(w_, c0, m=128):
                bank = nextbank()
                mm_group(bank, [(w_.ap[:, k, c0:c0 + m], un.ap[:, k, :]) for k in range(NK)], [w_.buf, un.buf], m=m)
                return bank

            def tile_body(i):
                tsl = slice(i * TT, (i + 1) * TT)
                x = xt[i % 2]
                rp = rope[i % 2]
                load(x, cm(xT)[:, :, tsl])
                load(rp, ROPE[:, :, tsl].rearrange("a p t -> p a t"))
                rmsnorm(x, 8, vec[l], un, sq, rstd, TT, bank=0)

                def plain(w_, c0, dst, row0, func=None):
                    bank = proj(w_, c0)
                    o = nexto()
                    if func is None:
                        P.op("act", lambda e: e.copy(out=o.ap, in_=psum[bank][:, :]), reads=[PS[bank]], writes=[o.buf])
                    else:
                        P.op("act", lambda e: e.activation(out=o.ap, in_=psum[bank][:, :], func=func),
                             reads=[PS[bank]], writes=[o.buf])
                    store(o, dst[row0:row0 + 128, tsl])

                def gelu(w_, c0, dst, row0):
                    bank = proj(w_, c0)
                    a_, b_ = nextt()
                    o = nexto()
                    P.op("act", lambda e: e.activation(out=a_.ap, in_=psum[bank][:, :], func=AF.Square),
                         reads=[PS[bank]], writes=[a_.buf])
                    P.op("pool", lambda e: e.tensor_scalar(out=a_.ap, in0=a_.ap, scalar1=0.044715, scalar2=1.0,
                                                           op0=ALU.mult, op1=ALU.add),
                         reads=[a_.buf], writes=[a_.buf])
                    P.op("dve", lambda e: e.tensor_tensor(out=b_.ap, in0=a_.ap, in1=psum[bank][:, :], op=ALU.mult),
                         reads=[a_.buf, PS[bank]], writes=[b_.buf])
                    P.op("act", lambda e: e.activation(out=b_.ap, in_=b_.ap, func=AF.Sigmoid, scale=1.5957691216057308),
                         reads=[b_.buf], writes=[b_.buf])
                    P.op("dve", lambda e: e.tensor_tensor(out=o.ap, in0=b_.ap, in1=psum[bank][:, :], op=ALU.mult),
                         reads=[b_.buf, PS[bank]], writes=[o.buf])
                    store(o, dst[row0:row0 + 128, tsl])

                def prod(w1, w2, c0, dst, row0):
                    b1 = proj(w1, c0)
                    b2 = proj(w2, c0)
                    a_, b_ = nextt()
                    o = nexto()
                    P.op("act", lambda e: e.copy(out=a_.ap, in_=psum[b1][:, :]), reads=[PS[b1]], writes=[a_.buf])
                    P.op("dve", lambda e: e.tensor_tensor(out=o.ap, in0=a_.ap, in1=psum[b2][:, :], op=ALU.mult),
                         reads=[a_.buf, PS[b2]], writes=[o.buf])
                    store(o, dst[row0:row0 + 128, tsl])

                def roped(w_, c0, ty, dst, row0):
                    b1 = proj(w_, c0)
                    x_ = nextx()
                    P.op("act", lambda e: e.copy(out=x_.ap, in_=psum[b1][:, :]), reads=[PS[b1]], writes=[x_.buf])
                    b2 = nextbank()
                    mm_group(b2, [(pmat.ap[:, ty, :], x_.ap)], [pmat.buf, x_.buf])
                    a_, b_ = nextt()
                    o = nexto(True)
                    P.op("pool", lambda e: e.tensor_tensor(out=a_.ap, in0=f32(x_.ap), in1=rp.ap[:, 2 * ty, :], op=ALU.mult),
                         reads=[x_.buf, rp.buf], writes=[a_.buf])
                    P.op("dve", lambda e: e.tensor_tensor(out=b_.ap, in0=psum[b2][:, :], in1=rp.ap[:, 2 * ty + 1, :], op=ALU.mult),
                         reads=[PS[b2], rp.buf], writes=[b_.buf])
                    P.op("dve", lambda e: e.tensor_tensor(out=o.ap, in0=a_.ap, in1=b_.ap, op=ALU.add),
                         reads=[a_.buf, b_.buf], writes=[o.buf])
                    store(o, dst[row0:row0 + 128, tsl])

                for blk in range(2):
                    w_ = loadblk(blk * 512)
                    for c in range(4):
                        plain(w_, c * 128, RX, (blk * 4 + c) * 128)
                for blk in range(2):
                    w_ = loadblk(1024 + blk * 512)
                    for c in range(4):
                        gelu(w_, c * 128, GG, (blk * 4 + c) * 128)
                for blk in range(2):
                    w_ = loadblk(2048 + blk * 512)
                    for c in range(4):
                        plain(w_, c * 128, CB, (blk * 4 + c) * 128)
                for blk in range(2):
                    w1 = loadblk(3072 + blk * 512)
                    w2 = loadblk(4096 + blk * 512)
                    for c in range(4):
                        prod(w1, w2, c * 128, CCH, (blk * 4 + c) * 128)
                for blk in range(2):
                    w_ = loadblk(5120 + blk * 512)
                    for c in range(4):
                        roped(w_, c * 128, 0, QT, (blk * 4 + c) * 128)
                w_ = loadblk(6144)
                for g in range(2):
                    roped(w_, g * 128, 0, KT, g * 128)
                for s in range(4):
                    bank = nextbank()
                    mm_group(bank, [(un.ap[:, k, s * 128:(s + 1) * 128], w_.ap[:, k, 256:512]) for k in range(NK)],
                             [w_.buf, un.buf], n=256)
                    o = nexto(True)
                    P.op("act", lambda e, o=o, bank=bank: e.copy(out=o.ap[:, 0:256], in_=psum[bank][:, 0:256]),
                         reads=[PS[bank]], writes=[o.buf])
                    store(o, VV[i * TT + s * 128:i * TT + (s + 1) * 128, :], src=o.ap[:, 0:256])
                w_ = loadblk(6656)
                for c in range(4):
                    roped(w_, c * 128, 1, QIT, c * 128)
                load(Wk, wi[:, :, 7168:7232], dst=Wk.ap[:, :, 0:64])
                load(Wk, wi[:, :, 7168:7232], dst=Wk.ap[:, :, 64:128])
                roped(Wk, 0, 1, KIT, 0)
                load(Ww, wi[:, :, 7232:7360])
                bank = proj(Ww, 0)
                o = nexto(True)
                wsc = (8 ** -0.5) * (64 ** -0.5)
                P.op("act", lambda e, o=o, bank=bank: e.activation(out=o.ap[0:8, :], in_=psum[bank][0:8, :], func=AF.Copy, scale=wsc),
                     reads=[PS[bank]], writes=[o.buf])
                store(o, WIT[:, tsl], src=o.ap[0:8, :])
                for blk in range(6):
                    w_ = loadblk(7240 + blk * 512)
                    for c in range(4):
                        plain(w_, c * 128, GT, (blk * 4 + c) * 128, func=AF.Sigmoid)

            for i in range(NT):
                tile_body(i)

        def phase_rnn(l):
            new_phase()
            v = vec[l]
            bda = Slot([128], F32R, "bda")
            bdx = Slot([128], F32R, "bdx")
            c1 = Slot([1], F32, "c1")
            tmpc = Slot([1], F32, "tmpc")
            rx = [Slot([TT + 3], F32, "rx%d" % i) for i in range(2)]
            gg = [Slot([TT], F32, "gg%d" % i) for i in range(2)]
            xa = Slot([TT], F32, "xa")
            xar = Slot([TT], F32R, "xar")
            r_ = Slot([TT], F32, "r")
            ig = Slot([TT], F32, "ig")
            a_ = Slot([TT], F32, "a")
            a2 = Slot([TT], F32, "a2")
            u_ = Slot([TT], F32, "u")
            hs = [Slot([TT], F32, "h%d" % i) for i in range(2)]
            hg = [Slot([TT], F32R, "hg%d" % i) for i in range(2)]
            nst = [0]

            def rnn_chunk(c):
                rows = slice(c * 128, (c + 1) * 128)
                P.op("act", lambda e: e.activation(out=bda.ap, in_=ones_f.ap, func=AF.Copy, scale=0.0),
                     reads=[ones_f.buf], writes=[bda.buf])
                P.op("act", lambda e: e.activation(out=bdx.ap, in_=ones_f.ap, func=AF.Copy, scale=0.0),
                     reads=[ones_f.buf], writes=[bdx.buf])
                for r in range(2):
                    load(bda, w_ga[l, 2 * c + r], eng="pool", dst=bda.ap[r * 64:(r + 1) * 64, r * 64:(r + 1) * 64])
                    load(bdx, w_gx[l, 2 * c + r], eng="pool", dst=bdx.ap[r * 64:(r + 1) * 64, r * 64:(r + 1) * 64])
                P.op("act", lambda e, c=c: e.activation(out=tmpc.ap, in_=v.ap[:, 48 + c:49 + c], func=AF.Exp, scale=-1.0),
                     reads=[v.buf], writes=[tmpc.buf])
                P.op("act", lambda e: e.activation(out=tmpc.ap, in_=tmpc.ap, func=AF.Ln, bias=1.0, scale=1.0),
                     reads=[tmpc.buf], writes=[tmpc.buf])
                P.op("dve", lambda e: e.tensor_scalar(out=c1.ap, in0=tmpc.ap, scalar1=-8.0, scalar2=None, op0=ALU.mult),
                     reads=[tmpc.buf], writes=[c1.buf])
                for i in range(NT):
                    rnn_tile(c, i, rows)

            def rnn_tile(c, i, rows):
                if True:
                    tsl = slice(i * TT, (i + 1) * TT)
                    n = nst[0]
                    x_ = rx[n % 2]
                    g_ = gg[n % 2]
                    hcur = hs[n % 2]
                    hprev = hs[(n + 1) % 2]
                    ho = hg[n % 2]
                    nst[0] += 1
                    if i == 0:
                        P.op("dve", lambda e, x_=x_: e.memset(x_.ap[:, 0:3], 0.0), writes=[x_.buf])
                        load(x_, RX[rows, 0:TT], dst=x_.ap[:, 3:TT + 3])
                    else:
                        load(x_, RX[rows, i * TT - 3:(i + 1) * TT])
                    load(g_, GG[rows, tsl])
                    cw = lambda k: v.ap[:, 56 + k * 8 + c:57 + k * 8 + c]
                    P.op("dve", lambda e, x_=x_: e.tensor_scalar(out=xa.ap, in0=x_.ap[:, 0:TT], scalar1=cw(0),
                                                                scalar2=v.ap[:, 24 + c:25 + c], op0=ALU.mult, op1=ALU.add),
                         reads=[x_.buf, v.buf], writes=[xa.buf])
                    for k in (1, 2):
                        P.op("dve", lambda e, x_=x_, k=k: e.scalar_tensor_tensor(
                            out=xa.ap, in0=x_.ap[:, k:k + TT], scalar=cw(k), in1=xa.ap, op0=ALU.mult, op1=ALU.add),
                            reads=[x_.buf, v.buf, xa.buf], writes=[xa.buf])
                    P.op("dve", lambda e, x_=x_: e.scalar_tensor_tensor(
                        out=xa.ap, in0=x_.ap[:, 3:3 + TT], scalar=cw(3), in1=xa.ap, op0=ALU.mult, op1=ALU.add),
                        reads=[x_.buf, v.buf, xa.buf], writes=[xa.buf])
                    P.op("act", lambda e: e.copy(out=xar.ap, in_=xa.ap), reads=[xa.buf], writes=[xar.buf])
                    mm_group(1, [(bda.ap, xar.ap)], [bda.buf, xar.buf])
                    mm_group(2, [(bdx.ap, xar.ap)], [bdx.buf, xar.buf])
                    P.op("act", lambda e: e.activation(out=r_.ap, in_=psum[1][:, :], func=AF.Sigmoid,
                                                       bias=v.ap[:, 32 + c:33 + c], scale=1.0),
                         reads=[PS[1], v.buf], writes=[r_.buf])
                    P.op("act", lambda e: e.activation(out=ig.ap, in_=psum[2][:, :], func=AF.Sigmoid,
                                                       bias=v.ap[:, 40 + c:41 + c], scale=1.0),
                         reads=[PS[2], v.buf], writes=[ig.buf])
                    P.op("act", lambda e: e.activation(out=a_.ap, in_=r_.ap, func=AF.Exp, scale=c1.ap[:, 0:1]),
                         reads=[r_.buf, c1.buf], writes=[a_.buf])
                    P.op("dve", lambda e: e.tensor_tensor(out=a2.ap, in0=a_.ap, in1=a_.ap, op=ALU.mult),
                         reads=[a_.buf], writes=[a2.buf])
                    P.op("act", lambda e: e.activation(out=a2.ap, in_=a2.ap, func=AF.Sqrt, bias=1.0, scale=-1.0),
                         reads=[a2.buf], writes=[a2.buf])
                    P.op("dve", lambda e: e.tensor_tensor(out=u_.ap, in0=ig.ap, in1=xa.ap, op=ALU.mult),
                         reads=[ig.buf, xa.buf], writes=[u_.buf])
                    P.op("dve", lambda e: e.tensor_tensor(out=u_.ap, in0=u_.ap, in1=a2.ap, op=ALU.mult),
                         reads=[u_.buf, a2.buf], writes=[u_.buf])
                    if i == 0:
                        P.op("dve", lambda e, hcur=hcur: e.tensor_tensor_scan(
                            out=hcur.ap, data0=a_.ap, data1=u_.ap, initial=0.0, op0=ALU.mult, op1=ALU.add),
                            reads=[a_.buf, u_.buf], writes=[hcur.buf])
                    else:
                        P.op("dve", lambda e, hcur=hcur, hprev=hprev: e.tensor_tensor_scan(
                            out=hcur.ap, data0=a_.ap, data1=u_.ap, initial=hprev.ap[:, TT - 1:TT],
                            op0=ALU.mult, op1=ALU.add),
                            reads=[a_.buf, u_.buf, hprev.buf], writes=[hcur.buf])
                    P.op("dve", lambda e, hcur=hcur, g_=g_, ho=ho: e.tensor_tensor(
                        out=ho.ap, in0=hcur.ap, in1=g_.ap, op=ALU.mult),
                        reads=[hcur.buf, g_.buf], writes=[ho.buf])
                    store(ho, HG[rows, tsl])

            for c in range(NK):
                rnn_chunk(c)

        def phase_sconv(l):
            new_phase()
            v = vec[l]
            cx = [Slot([TT + 2], F32, "cx%d" % i) for i in range(2)]
            cb = [Slot([TT], F32, "cb%d" % i) for i in range(2)]
            y = Slot([TT], F32, "y")
            yo = [Slot([TT], F32R, "yo%d" % i) for i in range(2)]
            nst = [0]

            def sconv_tile(c, i):
                if True:
                    rows = slice(c * 128, (c + 1) * 128)
                    tsl = slice(i * TT, (i + 1) * TT)
                    n = nst[0]
                    x_ = cx[n % 2]
                    b_ = cb[n % 2]
                    o = yo[n % 2]
                    nst[0] += 1
                    if i == 0:
                        P.op("dve", lambda e, x_=x_: e.memset(x_.ap[:, 0:2], 0.0), writes=[x_.buf])
                        load(x_, CCH[rows, 0:TT], dst=x_.ap[:, 2:TT + 2])
                    else:
                        load(x_, CCH[rows, i * TT - 2:(i + 1) * TT])
                    load(b_, CB[rows, tsl])
                    sw = lambda k: v.ap[:, 88 + k * 8 + c:89 + k * 8 + c]
                    P.op("dve", lambda e, x_=x_: e.tensor_scalar(out=y.ap, in0=x_.ap[:, 0:TT], scalar1=sw(0), scalar2=None,
                                                                op0=ALU.mult), reads=[x_.buf, v.buf], writes=[y.buf])
                    for k in (1, 2):
                        P.op("dve", lambda e, x_=x_, k=k: e.scalar_tensor_tensor(
                            out=y.ap, in0=x_.ap[:, k:k + TT], scalar=sw(k), in1=y.ap, op0=ALU.mult, op1=ALU.add),
                            reads=[x_.buf, v.buf, y.buf], writes=[y.buf])
                    P.op("dve", lambda e, b_=b_, o=o: e.tensor_tensor(out=o.ap, in0=y.ap, in1=b_.ap, op=ALU.mult),
                         reads=[y.buf, b_.buf], writes=[o.buf])
                    store(o, YB[rows, tsl])

            for c in range(NK):
                for i in range(NT):
                    sconv_tile(c, i)

        def phase_attn(l):
            new_phase()
            kt = Slot([2, T], F32R, "kt")
            vt = Slot([T // 128, 256], F32R, "vt")
            kit = Slot([T], F32R, "kit")
            sel = Slot([8 * 128], F32R, "sel")
            sc = Slot([T // 128, TQ], F32, "sc")
            scb = [Buf("scb%d" % i) for i in range(T // 256)]
            qt = Slot([8, TQ], F32R, "qt")
            qit = Slot([4, TQ], F32R, "qit")
            wit = [Slot([TQ], F32R, "wit%d" % i) for i in range(2)]
            qp = Slot([4, TQ], F32R, "qp")
            qm = Slot([4, TQ], F32R, "qm")
            wp = [Slot([TQ], F32, "wp")] * 2
            wm = [Slot([TQ], F32, "wm")] * 2
            acc2 = Slot([2, TQ], F32, "acc2")
            rtmp = [Slot([2, TQ], F32, "rtmp%d" % i) for i in range(2)]
            lo = Slot([TQ], F32, "lo")
            mid = Slot([TQ], F32, "mid")
            tmp = Slot([TQ], F32, "tmp")
            CBN = 8
            cmpb = [Slot([CBN, TQ], BF16, "cmp%d" % i) for i in range(2)]
            ee = [Slot([2, TQ], F32, "ee%d" % i) for i in range(2)]
            pm_ = [Slot([2, TQ], F32R, "pm%d" % i) for i in range(2)]
            rden = Slot([TQ], F32, "rden")
            oh = [Slot([TQ], F32R, "oh%d" % i) for i in range(2)]
            load(kt, KT.rearrange("(g p) t -> p g t", p=128))
            load(vt, VV.rearrange("(j p) d -> p j d", p=128))
            load(kit, KIT.get())
            load(sel, sel_in.get(), eng="pool", dst=sel.ap[0:8, :])
            ascale = 128 ** -0.5
            cst_ = {"cn": 0, "en": 0, "on": 0}

            def bcast(slot, nb):
                a = slot.ap
                return bass.AP(a.tensor, a.offset, [list(a.ap[0]), [0, nb], [1, TQ]])

            def attn_q(q):
                t0 = q * TQ
                nkt = 2 * (q + 1)
                Q_, QI, WI = qt, qit, wit[q % 2]
                load(Q_, cm(QT)[:, :, t0:t0 + TQ])
                load(QI, cm(QIT)[:, :, t0:t0 + TQ])
                load(WI, WIT[:, t0:t0 + TQ], dst=WI.ap[0:8, :])
                for hh in range(8):
                    c, r = divmod(hh, 2)
                    bank = 4 + hh % 2
                    mm_group(bank, [(sel.ap[0:8, hh * 128:(hh + 1) * 128], WI.ap[0:8, :])], [sel.buf, WI.buf], n=TQ)
                    p_, m_ = wp[hh % 2], wm[hh % 2]
                    P.op("dve", lambda e, p_=p_, bank=bank: e.tensor_scalar(out=p_.ap, in0=psum[bank][:, 0:TQ], scalar1=0.0,
                                                                            scalar2=None, op0=ALU.max),
                         reads=[PS[bank]], writes=[p_.buf])
                    P.op("dve", lambda e, m_=m_, bank=bank: e.tensor_scalar(out=m_.ap, in0=psum[bank][:, 0:TQ], scalar1=0.0,
                                                                            scalar2=None, op0=ALU.min),
                         reads=[PS[bank]], writes=[m_.buf])
                    rs = slice(r * 64, (r + 1) * 64)
                    P.op("pool", lambda e, p_=p_, c=c, rs=rs: e.tensor_tensor(
                        out=qp.ap[rs, c, :], in0=f32(QI.ap)[rs, c, :], in1=p_.ap[rs, :], op=ALU.mult),
                        reads=[QI.buf, p_.buf], writes=[qp.buf])
                    P.op("pool", lambda e, m_=m_, c=c, rs=rs: e.tensor_tensor(
                        out=qm.ap[rs, c, :], in0=f32(QI.ap)[rs, c, :], in1=m_.ap[rs, :], op=ALU.mult),
                        reads=[QI.buf, m_.buf], writes=[qm.buf])
                for j in range(0, nkt, 2):
                    scj = sc.ap[:, j:j + 2, :]
                    for hh in range(8):
                        c, r = divmod(hh, 2)
                        rs = slice(r * 64, (r + 1) * 64)
                        b1, b2 = 2 * (hh % 2), 2 * (hh % 2) + 1
                        for (bk, qq) in ((b1, qp), (b2, qm)):
                            def fn(e, bk=bk, qq=qq, rs=rs, c=c, j=j):
                                inst = None
                                for jj in range(2):
                                    inst = e.matmul(psum[bk][:, jj * TQ:(jj + 1) * TQ],
                                                    lhsT=kit.ap[rs, (j + jj) * 128:(j + jj + 1) * 128],
                                                    rhs=qq.ap[rs, c, :], start=True, stop=True)
                                return inst
                            P.op("pe", fn, reads=[kit.buf, qq.buf], writes=[PS[bk]])
                        ps1 = psum[b1][:, :].rearrange("p (a b) -> p a b", a=2)
                        ps2 = psum[b2][:, :].rearrange("p (a b) -> p a b", a=2)
                        if hh == 0:
                            P.op("dve", lambda e, ps1=ps1, scj=scj: e.tensor_scalar(
                                out=scj, in0=ps1, scalar1=0.0, scalar2=None, op0=ALU.max),
                                reads=[PS[b1]], writes=[scb[j // 2]])
                        else:
                            P.op("dve", lambda e, ps1=ps1, scj=scj: e.scalar_tensor_tensor(
                                out=scj, in0=ps1, scalar=0.0, in1=scj, op0=ALU.max, op1=ALU.add),
                                reads=[PS[b1], scb[j // 2]], writes=[scb[j // 2]])
                        rt = rtmp[hh % 2]
                        P.op("act", lambda e, ps2=ps2, rt=rt: e.activation(out=rt.ap, in_=ps2, func=AF.Relu, scale=-1.0),
                             reads=[PS[b2]], writes=[rt.buf])
                        if hh == 0:
                            P.op("pool", lambda e, rt=rt: e.tensor_copy(out=acc2.ap, in_=rt.ap),
                                 reads=[rt.buf], writes=[acc2.buf])
                        else:
                            P.op("pool", lambda e, rt=rt: e.tensor_tensor(out=acc2.ap, in0=acc2.ap, in1=rt.ap, op=ALU.add),
                                 reads=[rt.buf, acc2.buf], writes=[acc2.buf])
                    P.op("dve", lambda e, scj=scj: e.tensor_tensor(out=scj, in0=scj, in1=acc2.ap, op=ALU.subtract),
                         reads=[scb[j // 2], acc2.buf], writes=[scb[j // 2]])
                    if j >= 2 * q:
                        def fsel(e, j=j, scj=scj):
                            if "r" not in fillreg:
                                fillreg["r"] = e.to_reg(-1.0e30)
                            return e.affine_select(
                                out=scj, in_=scj, pattern=[[-128, 2], [1, TQ]], compare_op=ALU.is_ge,
                                fill=fillreg["r"], base=t0 - j * 128, channel_multiplier=-1)
                        P.op("pool", fsel, reads=[scb[j // 2]], writes=[scb[j // 2]])
                P.op("dve", lambda e: e.memset(lo.ap, -16.0), writes=[lo.buf])
                for it in range(NIT):
                    dstep = 16.0 / (2 ** it)
                    P.op("dve", lambda e, dstep=dstep: e.tensor_scalar(out=mid.ap, in0=lo.ap, scalar1=dstep, scalar2=None,
                                                                      op0=ALU.add), reads=[lo.buf], writes=[mid.buf])
                    cb_ = 6 + it % 2
                    for j0 in range(0, nkt, CBN):
                        nb = min(CBN, nkt - j0)
                        cp = cmpb[cst_["cn"] % 2]
                        cst_["cn"] += 1
                        P.op("dve", lambda e, cp=cp, j0=j0, nb=nb: e.tensor_tensor(
                            out=cp.ap[:, 0:nb, :], in0=sc.ap[:, j0:j0 + nb, :], in1=bcast(mid, nb), op=ALU.is_gt),
                            reads=scb[j0 // 2:(j0 + nb) // 2] + [mid.buf], writes=[cp.buf])
                        def fn(e, cp=cp, j0=j0, nb=nb, cb_=cb_):
                            inst = None
                            for jj in range(nb):
                                inst = e.matmul(psum[cb_][:, 0:TQ], lhsT=ones_b.ap, rhs=cp.ap[:, jj, :],
                                                start=(j0 + jj == 0), stop=(j0 + jj == nkt - 1))
                            return inst
                        P.op("pe", fn, reads=[cp.buf, ones_b.buf], writes=[PS[cb_]])
                    P.op("dve", lambda e, cb_=cb_, dstep=dstep: e.tensor_scalar(
                        out=tmp.ap, in0=psum[cb_][:, 0:TQ], scalar1=255.5, scalar2=dstep, op0=ALU.is_ge, op1=ALU.mult),
                        reads=[PS[cb_]], writes=[tmp.buf])
                    P.op("dve", lambda e: e.tensor_tensor(out=lo.ap, in0=lo.ap, in1=tmp.ap, op=ALU.add),
                         reads=[lo.buf, tmp.buf], writes=[lo.buf])
                for j0 in range(0, nkt, CBN):
                    nb = min(CBN, nkt - j0)
                    P.op("dve", lambda e, j0=j0, nb=nb: e.tensor_tensor(
                        out=sc.ap[:, j0:j0 + nb, :], in0=sc.ap[:, j0:j0 + nb, :], in1=bcast(lo, nb), op=ALU.is_gt),
                        reads=scb[j0 // 2:(j0 + nb) // 2] + [lo.buf], writes=scb[j0 // 2:(j0 + nb) // 2])
                for hh in range(8):
                    g = hh // 4
                    bo, bd = (2, 3) if hh % 2 == 0 else (4, 5)
                    for j in range(0, nkt, 2):
                        en = cst_["en"]
                        cst_["en"] += 1
                        bs = en % 2
                        E = ee[en % 2]
                        Pm = pm_[en % 2]
                        def fs(e, j=j, bs=bs, g=g, hh=hh):
                            inst = None
                            for jj in range(2):
                                inst = e.matmul(psum[bs][:, jj * TQ:(jj + 1) * TQ],
                                                lhsT=kt.ap[:, g, (j + jj) * 128:(j + jj + 1) * 128],
                                                rhs=Q_.ap[:, hh, :], start=True, stop=True)
                            return inst
                        P.op("pe", fs, reads=[kt.buf, Q_.buf], writes=[PS[bs]])
                        P.op("act", lambda e, E=E, bs=bs: e.activation(
                            out=E.ap, in_=psum[bs][:, :].rearrange("p (a b) -> p a b", a=2), func=AF.Exp, scale=ascale),
                            reads=[PS[bs]], writes=[E.buf])
                        P.op("dve", lambda e, E=E, Pm=Pm, j=j: e.tensor_tensor(
                            out=Pm.ap, in0=E.ap, in1=sc.ap[:, j:j + 2, :], op=ALU.mult),
                            reads=[E.buf, scb[j // 2]], writes=[Pm.buf])
                        def fo(e, Pm=Pm, j=j, g=g, bo=bo):
                            inst = None
                            for jj in range(2):
                                inst = e.matmul(psum[bo][:, 0:TQ], lhsT=vt.ap[:, j + jj, g * 128:(g + 1) * 128],
                                                rhs=Pm.ap[:, jj, :], start=(j + jj == 0), stop=(j + jj == nkt - 1))
                            return inst
                        P.op("pe", fo, reads=[vt.buf, Pm.buf], writes=[PS[bo]])
                        def fd(e, Pm=Pm, j=j, bd=bd):
                            inst = None
                            for jj in range(2):
                                inst = e.matmul(psum[bd][:, 0:TQ], lhsT=ones_r.ap, rhs=Pm.ap[:, jj, :],
                                                start=(j + jj == 0), stop=(j + jj == nkt - 1))
                            return inst
                        P.op("pe", fd, reads=[ones_r.buf, Pm.buf], writes=[PS[bd]])
                    P.op("dve", lambda e, bd=bd: e.reciprocal(out=rden.ap, in_=psum[bd][:, 0:TQ]),
                         reads=[PS[bd]], writes=[rden.buf])
                    o_ = oh[cst_["on"] % 2]
                    cst_["on"] += 1
                    P.op("dve", lambda e, bo=bo, o_=o_: e.tensor_tensor(
                        out=o_.ap, in0=psum[bo][:, 0:TQ], in1=rden.ap, op=ALU.mult),
                        reads=[PS[bo], rden.buf], writes=[o_.buf])
                    store(o_, AT[hh * 128:(hh + 1) * 128, t0:t0 + TQ])

            for q in range(NQ):
                attn_q(q)

        def phase_merge(l):
            new_phase()
            xt = Slot([NK, TT], F32, "xt")
            br = [Slot([NK, TT], F32R, "br%d" % i) for i in range(2)]
            gt = [Slot([4, TT], F32, "gt%d" % i) for i in range(2)]
            mg = Slot([NK, TT], F32, "mg")
            mgr = Slot([NK, TT], F32R, "mgr")
            tmp = [Slot([TT], F32, "tmp%d" % i) for i in range(2)]
            W = [Slot([NK, 512], F32R, "W%d" % i) for i in range(3)]
            srcs = [(HG, w_ro[l]), (YB, w_so[l]), (AT, w_ao[l])]
            st = {"w": 0, "b": 0, "t": 0, "g": 0}

            def tile_body(i):
                tsl = slice(i * TT, (i + 1) * TT)
                x = xt
                load(x, cm(xT)[:, :, tsl])
                load(br[0], cm(srcs[0][0])[:, :, tsl])
                for b, (src, wmat) in enumerate(srcs):
                    s_ = br[b % 2]
                    if b + 1 < 3:
                        load(br[(b + 1) % 2], cm(srcs[b + 1][0])[:, :, tsl])
                    wr = wmat.rearrange("(k p) n -> p k n", p=128)
                    for blk in range(2):
                        w_ = W[st["w"] % 3]
                        st["w"] += 1
                        load(w_, wr[:, :, blk * 512:(blk + 1) * 512])
                        g_ = gt[st["g"] % 2]
                        st["g"] += 1
                        load(g_, cm(GT[b * D + blk * 512:b * D + (blk + 1) * 512, :])[:, :, tsl])
                        for cc in range(4):
                            c = blk * 4 + cc
                            bank = 1 + st["b"] % 6
                            st["b"] += 1
                            mm_group(bank, [(w_.ap[:, k, cc * 128:(cc + 1) * 128], s_.ap[:, k, :]) for k in range(NK)],
                                     [w_.buf, s_.buf])
                            if b == 0:
                                P.op("dve", lambda e, bank=bank, g_=g_, c=c, cc=cc: e.tensor_tensor(
                                    out=mg.ap[:, c, :], in0=psum[bank][:, :], in1=g_.ap[:, cc, :], op=ALU.mult),
                                    reads=[PS[bank], g_.buf], writes=[mg.buf])
                            else:
                                t_ = tmp[st["t"] % 2]
                                st["t"] += 1
                                dst = mg.ap[:, c, :] if b < 2 else mgr.ap[:, c, :]
                                dbuf = mg.buf if b < 2 else mgr.buf
                                P.op("dve", lambda e, bank=bank, g_=g_, c=c, cc=cc, t_=t_: e.tensor_tensor(
                                    out=t_.ap, in0=psum[bank][:, :], in1=g_.ap[:, cc, :], op=ALU.mult),
                                    reads=[PS[bank], g_.buf], writes=[t_.buf])
                                P.op("pool", lambda e, t_=t_, c=c, dst=dst: e.tensor_tensor(
                                    out=dst, in0=t_.ap, in1=mg.ap[:, c, :], op=ALU.add),
                                    reads=[t_.buf, mg.buf], writes=[dbuf])
                wr = w_o[l].rearrange("(k p) n -> p k n", p=128)
                for blk in range(2):
                    w_ = W[st["w"] % 3]
                    st["w"] += 1
                    load(w_, wr[:, :, blk * 512:(blk + 1) * 512])
                    for cc in range(4):
                        c = blk * 4 + cc
                        bank = 1 + st["b"] % 6
                        st["b"] += 1
                        mm_group(bank, [(w_.ap[:, k, cc * 128:(cc + 1) * 128], mgr.ap[:, k, :]) for k in range(NK)],
                                 [w_.buf, mgr.buf])
                        P.op("dve", lambda e, bank=bank, c=c: e.tensor_tensor(
                            out=x.ap[:, c, :], in0=psum[bank][:, :], in1=x.ap[:, c, :], op=ALU.add),
                            reads=[PS[bank], x.buf], writes=[x.buf])
                store(x, cm(xT)[:, :, tsl])

            for i in range(NT):
                tile_body(i)

        sched = []
        sched.append(("x0", phase_x0))
        sched.append(("rope", phase_rope))
        for l in range(nlayers):
            sched.append(("ffn1_%d" % l, lambda l=l: phase_ffn(l, 0)))
            sched.append(("proj_%d" % l, lambda l=l: phase_proj(l)))
            sched.append(("rnn_%d" % l, lambda l=l: phase_rnn(l)))
            sched.append(("sconv_%d" % l, lambda l=l: phase_sconv(l)))
            sched.append(("attn_%d" % l, lambda l=l: phase_attn(l)))
            sched.append(("merge_%d" % l, lambda l=l: phase_merge(l)))
            sched.append(("ffn2_%d" % l, lambda l=l: phase_ffn(l, 1)))
        sched.append(("out", phase_out))
        for name, fn in sched:
            if phases == "all" or name in phases:
                fn()
        P.barrier()
        P.build(nc)
    _USED[id(nc)] = list(used_inputs)
    return nc


def make_consts():
    theta = 500000.0
    cst = np.zeros((128, 16), np.float32)
    p = np.arange(128)
    invA = (theta ** (-(np.arange(0, 32, 2, dtype=np.float32)) / 32)).astype(np.float32)
    invB = (theta ** (-(np.arange(0, 16, 2, dtype=np.float32)) / 16)).astype(np.float32)
    cst[:32, 0] = invA[p[:32] % 16]
    cst[:, 1] = np.where(p < 16, -1.0, 1.0)
    pm = p % 64
    cst[:, 2] = np.where(pm < 16, invB[pm % 8], 0.0)
    cst[:, 3] = np.where(pm < 8, -1.0, 1.0)
    ident = np.eye(128, dtype=np.float32)
    sel = np.zeros((8, 8 * 128), np.float32)
    for h in range(8):
        sel[h, h * 128:(h + 1) * 128] = 1.0
    pm = np.zeros((2, 128, 128), np.float32)
    for i in range(16):
        pm[0, i + 16, i] = -1.0
        pm[0, i, i + 16] = 1.0
    for hb in (0, 64):
        for i in range(8):
            pm[1, hb + i + 8, hb + i] = -1.0
            pm[1, hb + i, hb + i + 8] = 1.0
    return cst, ident, sel, pm


def pack_vec(inp):
    def col(v):
        return np.ascontiguousarray(v.reshape(8, 128).T)
    vec = np.zeros((L, 128, NVEC), np.float32)
    for l in range(L):
        vec[l, :, 0:8] = col(inp["ffn1_norm"][l])
        vec[l, :, 8:16] = col(inp["mix_norm"][l])
        vec[l, :, 16:24] = col(inp["ffn2_norm"][l])
        vec[l, :, 24:32] = col(inp["rnn_conv_b"][l])
        vec[l, :, 32:40] = col(inp["rnn_gate_a_b"][l])
        vec[l, :, 40:48] = col(inp["rnn_gate_x_b"][l])
        vec[l, :, 48:56] = col(inp["rnn_lambda"][l])
        for k in range(4):
            vec[l, :, 56 + 8 * k:64 + 8 * k] = col(inp["rnn_conv_w"][l, k])
        for k in range(3):
            vec[l, :, 88 + 8 * k:96 + 8 * k] = col(inp["sconv_w"][l, k])
    fin = col(inp["final_norm"])
    return vec, fin


def make_in_maps(inp, batches, used=None, extra=None):
    cst, ident, sel, pm = make_consts()
    vec, fin = pack_vec(inp)
    shared = {k: np.ascontiguousarray(np.asarray(inp[k], dtype=np.float32)) for k in (
        "ffn1_w_gate_up", "ffn2_w_gate_up", "ffn1_w_down", "ffn2_w_down", "w_in", "rnn_gate_a_w",
        "rnn_gate_x_w", "rnn_w_out", "sconv_w_out", "attn_w_out", "w_o")}
    shared.update({"vec": vec, "fin": fin, "cst": cst, "ident": ident, "sel": sel, "pm": pm})
    maps = []
    for b in batches:
        m = dict(shared)
        m["x"] = np.ascontiguousarray(np.asarray(inp["x"][b], dtype=np.float32))
        m["positions"] = np.ascontiguousarray(np.asarray(inp["positions"][b:b + 1], dtype=np.int32))
        if extra:
            m.update(extra)
        if used is not None:
            m = {k: v for k, v in m.items() if k in used}
        maps.append(m)
    return maps


_NC_CACHE = {}


def kernel(**inputs):
    inp = {k: np.asarray(v) for k, v in inputs.items()}
    if "full" not in _NC_CACHE:
        _NC_CACHE["full"] = build_program()
    nc = _NC_CACHE["full"]
    maps = make_in_maps(inp, range(4), used=_USED[id(nc)])
    res = run_bass_kernel_spmd(nc, maps, core_ids=list(range(4)))
    return np.stack([np.asarray(r["out"], dtype=np.float32) for r in res.results], axis=0)
```

```python
import math
from contextlib import ExitStack
import numpy as np
import concourse.bass as bass
import concourse.mybir as mybir
from concourse.bass_utils import run_bass_kernel_spmd

F32 = mybir.dt.float32
BF16 = mybir.dt.bfloat16
F32R = mybir.dt.float32r
I32 = mybir.dt.int32
AF = mybir.ActivationFunctionType
ALU = mybir.AluOpType

T = 4096
D = 1024
DFF = 2816
NK = 8
NH = DFF // 128
TT = 512
NT = T // TT
TQ = 256
NQ = T // TQ
DIN = 10312
L = 2
NVEC = 112
import os
NIT = int(os.environ.get('K_NIT', '18'))
SKIP_HEADS = os.environ.get('K_SKIPH') == '1'
SKIP_IDX = os.environ.get('K_SKIPI') == '1'
ARENA_R = 29100
ARENA_F = 15200
TWO_PI = 2.0 * math.pi


class Buf:
    _n = 0

    def __init__(self, name=None):
        Buf._n += 1
        self.name = name or ("b%d" % Buf._n)
        self.w = None
        self.r = {}


class Prog:
    CE = ("pe", "act", "dve", "pool")
    ALLE = ("pe", "act", "dve", "pool", "sp")

    def __init__(self):
        self.streams = {e: [] for e in self.ALLE}
        self.cnt = {}
        self.known = {e: {} for e in self.ALLE}

    def op(self, eng, fn, reads=(), writes=(), dma_key=None):
        waits = {}

        def need(tok):
            if tok is None:
                return
            k, v = tok
            if eng == "pe" and k == "pe":
                return
            if self.known[eng].get(k, 0) >= v:
                return
            if waits.get(k, 0) < v:
                waits[k] = v

        for b in reads:
            need(b.w)
        for b in writes:
            need(b.w)
            for k, v in b.r.items():
                need((k, v))
        for k, v in waits.items():
            self.known[eng][k] = v
        key = dma_key if dma_key is not None else eng
        step = 16 if dma_key is not None else 1
        self.cnt[key] = self.cnt.get(key, 0) + step
        tok = (key, self.cnt[key])
        self.streams[eng].append((fn, sorted(waits.items()), tok, step))
        for b in reads:
            if b.r.get(key, 0) < tok[1]:
                b.r[key] = tok[1]
        for b in writes:
            b.w = tok
            b.r = {}
        return tok

    def barrier(self):
        snap = dict(self.cnt)
        for e in self.ALLE:
            waits = []
            for k, v in sorted(snap.items()):
                if e == "pe" and k == "pe":
                    continue
                if self.known[e].get(k, 0) >= v:
                    continue
                self.known[e][k] = v
                waits.append((k, v))
            if waits:
                self.streams[e].append(("wait", waits, None, 0))

    def build(self, nc):
        keys = sorted(self.cnt.keys())
        with ExitStack() as es:
            sems = {}
            for i, k in enumerate(keys):
                sems[k] = es.enter_context(nc.semaphore("s%d" % i))
            block = es.enter_context(nc.Block())
            emap = {"pe": block.tensor, "act": block.scalar, "dve": block.vector,
                    "pool": block.gpsimd, "sp": block.sync}
            for e in self.ALLE:
                stream = self.streams[e]
                if not stream:
                    continue

                def body(engine, stream=stream):
                    for fn, waits, tok, step in stream:
                        for k, v in waits:
                            engine.wait_ge(sems[k], v)
                        if fn == "wait":
                            continue
                        inst = fn(engine)
                        inst.then_inc(sems[tok[0]], step)
                emap[e](body)


class Arena:
    def __init__(self, tensor, size, base):
        self.t = tensor
        self.off = 0
        self.size = size
        self.base = base

    def alloc(self, shape, dtype=F32, name=None):
        n = 1
        for s in shape:
            n *= s
        words = n if dtype in (F32, F32R, I32) else (n + 1) // 2
        words = (words + 7) // 8 * 8
        assert self.off + words <= self.size, ("arena overflow", self.off, words, name)
        ap = self.t[:, self.off:self.off + words]
        self.off += words
        if dtype != self.base:
            ap = ap.bitcast(dtype)
        ap = ap[:, 0:n]
        if len(shape) == 2:
            ap = ap.rearrange("p (a b) -> p a b", a=shape[0], b=shape[1])
        elif len(shape) == 3:
            ap = ap.rearrange("p (a b c) -> p a b c", a=shape[0], b=shape[1], c=shape[2])
        return ap, Buf(name)


def f32(ap):
    return ap.bitcast(F32)


_USED = {}


def build_program(phases="all", dbg_out=("out",), dbg_in=(), nlayers=L):
    nc = bass.Bass("TRN2", target_bir_lowering=False)
    P = Prog()

    used_inputs = []

    class LazyIn:
        def __init__(self, name, shape, dt=F32):
            self.name, self.shape, self.dt, self._ap = name, list(shape), dt, None

        def get(self):
            if self._ap is None:
                self._ap = nc.dram_tensor(self.name, self.shape, self.dt, kind="ExternalInput").ap()
                used_inputs.append(self.name)
            return self._ap

        def __getitem__(self, idx):
            return self.get()[idx]

        def rearrange(self, *a, **k):
            return self.get().rearrange(*a, **k)

        @property
        def tensor(self):
            return self.get().tensor

    def din(name, shape, dt=F32):
        return LazyIn(name, shape, dt)

    class LazyScr(LazyIn):
        def get(self):
            if self._ap is None:
                if self.name in dbg_in:
                    kind = "ExternalInput"
                    used_inputs.append(self.name)
                elif self.name in dbg_out:
                    kind = "ExternalOutput"
                else:
                    kind = "Internal"
                self._ap = nc.dram_tensor(self.name, self.shape, self.dt, kind=kind).ap()
            return self._ap

    def dscr(name, shape, dt=F32):
        return LazyScr(name, shape, dt)

    x_in = din("x", [T, D])
    pos_in = din("positions", [1, T], I32)
    w_gu = [din("ffn1_w_gate_up", [L, D, 2 * DFF]), din("ffn2_w_gate_up", [L, D, 2 * DFF])]
    w_dn = [din("ffn1_w_down", [L, DFF, D]), din("ffn2_w_down", [L, DFF, D])]
    w_in = din("w_in", [L, D, DIN])
    w_ga = din("rnn_gate_a_w", [L, 16, 64, 64])
    w_gx = din("rnn_gate_x_w", [L, 16, 64, 64])
    w_ro = din("rnn_w_out", [L, D, D])
    w_so = din("sconv_w_out", [L, D, D])
    w_ao = din("attn_w_out", [L, D, D])
    w_o = din("w_o", [L, D, D])
    vec_in = din("vec", [L, 128, NVEC])
    fin_in = din("fin", [128, NK])
    cst_in = din("cst", [128, 16])
    ident_in = din("ident", [128, 128])
    sel_in = din("sel", [8, 8 * 128])
    pm_in = din("pm", [2, 128, 128])
    out_l = LazyScr("out", [T, D], F32)

    xT = dscr("xT", [D, T])
    RX = dscr("RX", [D, T])
    GG = dscr("GG", [D, T])
    CB = dscr("CB", [D, T])
    CCH = dscr("CCH", [D, T])
    QT = dscr("QT", [D, T])
    KT = dscr("KT", [256, T])
    VV = dscr("VV", [T, 256])
    QIT = dscr("QIT", [512, T])
    KIT = dscr("KIT", [128, T])
    WIT = dscr("WIT", [8, T])
    GT = dscr("GT", [3 * D, T])
    HG = dscr("HG", [D, T])
    YB = dscr("YB", [D, T])
    AT = dscr("AT", [D, T])
    ROPE = dscr("ROPE", [4, 128, T])

    def cm(ap, c0=None):
        return ap.rearrange("(c p) t -> p c t", p=128)

    with ExitStack() as es:
        arena_r = es.enter_context(nc.sbuf_tensor("arena_r", [128, ARENA_R], F32R))
        arena_f = es.enter_context(nc.sbuf_tensor("arena_f", [128, ARENA_F], F32))
        AR_ = Arena(arena_r, ARENA_R, F32R)
        AF_ = Arena(arena_f, ARENA_F, F32)
        psum = [es.enter_context(nc.psum_tensor("ps%d" % i, [128, 512], F32)) for i in range(8)]
        PS = [Buf("ps%d" % i) for i in range(8)]

        keyn = [0]
        fillreg = {}

        def newkey(pfx):
            keyn[0] += 1
            return "%s%d" % (pfx, keyn[0])

        class Slot:
            def __init__(self, shape, dtype=F32, name=None, region=None):
                if region is None:
                    region = "r" if dtype == F32R else "f"
                self.ap, self.buf = (AR_ if region == "r" else AF_).alloc(shape, dtype, name)
                self.lk = None
                self.sk = None

        def load(slot, src, eng="sp", dst=None):
            if slot.lk is None:
                slot.lk = newkey("L")
            d = slot.ap if dst is None else dst
            if d.dtype == F32R:
                eng = "pool"
            P.op(eng, lambda e: e.dma_start(out=d, in_=src), writes=[slot.buf], dma_key=slot.lk)

        def store(slot, dstdram, src=None, eng="sp"):
            if slot.sk is None:
                slot.sk = newkey("S")
            s = slot.ap if src is None else src
            if s.dtype == F32R:
                s = s.bitcast(F32)
            P.op(eng, lambda e: e.dma_start(out=dstdram, in_=s), reads=[slot.buf], dma_key=slot.sk)

        def mm_group(bank, items, reads, n=512, m=128):
            def fn(e):
                inst = None
                for i, (l, r) in enumerate(items):
                    inst = e.matmul(psum[bank][0:m, 0:n], lhsT=l, rhs=r, start=(i == 0), stop=(i == len(items) - 1))
                return inst
            P.op("pe", fn, reads=reads, writes=[PS[bank]])

        ones_r = Slot([128], F32R, "ones_r")
        ones_b = Slot([128], BF16, "ones_b")
        ident = Slot([128], F32, "ident")
        cst = Slot([16], F32, "cst")
        fin = Slot([NK], F32, "fin")
        vec = [Slot([NVEC], F32, "vec%d" % l) for l in range(L)]
        ones_f = Slot([128], F32, "ones_f")
        P.op("dve", lambda e: e.memset(ones_f.ap, 1.0), writes=[ones_f.buf])
        P.op("act", lambda e: e.copy(out=ones_r.ap, in_=ones_f.ap), reads=[ones_f.buf], writes=[ones_r.buf])
        P.op("dve", lambda e: e.tensor_copy(out=ones_b.ap, in_=ones_f.ap), reads=[ones_f.buf], writes=[ones_b.buf])
        load(ident, ident_in.get())
        load(cst, cst_in.get())
        load(fin, fin_in.get())
        for l in range(L):
            load(vec[l], vec_in[l])
        base_off = (AR_.off, AF_.off)
        base_key = keyn[0]

        def new_phase():
            P.barrier()
            AR_.off, AF_.off = base_off
            keyn[0] = base_key

        def rmsnorm(xt, gcol, gslot, outp, sq, rstd, width, bank=0, cs=slice(None)):
            for c in range(NK):
                s = sq[c % 2]
                P.op("act", lambda e, s=s, c=c: e.activation(out=s.ap, in_=xt.ap[:, c, cs], func=AF.Square),
                     reads=[xt.buf], writes=[s.buf])
                def fn(e, s=s, c=c):
                    return e.matmul(psum[bank][:, 0:width], lhsT=ones_r.ap, rhs=s.ap, start=(c == 0), stop=(c == NK - 1))
                P.op("pe", fn, reads=[s.buf, ones_r.buf], writes=[PS[bank]])
            P.op("act", lambda e: e.activation(out=rstd.ap, in_=psum[bank][:, 0:width], func=AF.Sqrt,
                                               bias=1e-6, scale=1.0 / D),
                 reads=[PS[bank]], writes=[rstd.buf])
            P.op("dve", lambda e: e.reciprocal(out=rstd.ap, in_=rstd.ap), reads=[rstd.buf], writes=[rstd.buf])
            for c in range(NK):
                P.op("dve", lambda e, c=c: e.scalar_tensor_tensor(
                    out=outp.ap[:, c, cs], in0=xt.ap[:, c, cs], scalar=gslot.ap[:, gcol + c:gcol + c + 1],
                    in1=rstd.ap, op0=ALU.mult, op1=ALU.mult),
                    reads=[xt.buf, rstd.buf, gslot.buf], writes=[outp.buf])

        def phase_x0():
            new_phase()
            xin = [Slot([4, D], F32, "xin")] * 2
            xo = [Slot([NK, TT], F32, "xo")] * 2
            xr = x_in.rearrange("(i s p) d -> i p s d", s=4, p=128)
            n = 0
            for i in range(NT):
                xi = xin[i % 2]
                o = xo[i % 2]
                load(xi, xr[i])
                for c in range(NK):
                    bank = n % 2
                    n += 1
                    def fn(e, xi=xi, c=c, bank=bank):
                        inst = None
                        for s in range(4):
                            inst = e.transpose(psum[bank][:, s * 128:(s + 1) * 128],
                                               xi.ap[:, s, c * 128:(c + 1) * 128], ident.ap)
                        return inst
                    P.op("pe", fn, reads=[xi.buf, ident.buf], writes=[PS[bank]])
                    if c % 2 == 0:
                        P.op("act", lambda e, o=o, c=c, bank=bank: e.copy(out=o.ap[:, c, :], in_=psum[bank][:, :]),
                             reads=[PS[bank]], writes=[o.buf])
                    else:
                        P.op("dve", lambda e, o=o, c=c, bank=bank: e.tensor_copy(out=o.ap[:, c, :], in_=psum[bank][:, :]),
                             reads=[PS[bank]], writes=[o.buf])
                store(o, cm(xT)[:, :, i * TT:(i + 1) * TT])

        def phase_out():
            new_phase()
            xt = [Slot([NK, TT], F32, "xt")] * 2
            yn = Slot([NK, TT], F32, "yn")
            sq = [Slot([TT], F32R, "sq%d" % i) for i in range(2)]
            rstd = Slot([TT], F32, "rstd")
            yo = [Slot([4, D], F32, "yo")] * 2
            orr = out_l.rearrange("(i s p) d -> i p s d", s=4, p=128)
            n = 0
            for i in range(NT):
                load(xt[i % 2], cm(xT)[:, :, i * TT:(i + 1) * TT])
                x = xt[i % 2]
                rmsnorm(x, 0, fin, yn, sq, rstd, TT, bank=0)
                o = yo[i % 2]
                for s in range(4):
                    for half in range(2):
                        bank = 1 + n % 2
                        n += 1
                        def fn(e, s=s, half=half, bank=bank):
                            inst = None
                            for cc in range(4):
                                c = half * 4 + cc
                                inst = e.transpose(psum[bank][:, cc * 128:(cc + 1) * 128],
                                                   yn.ap[:, c, s * 128:(s + 1) * 128], ident.ap)
                            return inst
                        P.op("pe", fn, reads=[yn.buf, ident.buf], writes=[PS[bank]])
                        if half == 0:
                            P.op("act", lambda e, o=o, s=s, half=half, bank=bank: e.copy(
                                out=o.ap[:, s, half * 512:(half + 1) * 512], in_=psum[bank][:, :]),
                                reads=[PS[bank]], writes=[o.buf])
                        else:
                            P.op("dve", lambda e, o=o, s=s, half=half, bank=bank: e.tensor_copy(
                                out=o.ap[:, s, half * 512:(half + 1) * 512], in_=psum[bank][:, :]),
                                reads=[PS[bank]], writes=[o.buf])
                store(o, orr[i])

        def phase_ffn(l, which):
            new_phase()
            TF = 1024
            NHH = NH // 2
            wgu = w_gu[which][l]
            wdn = w_dn[which][l]
            gcol = 0 if which == 0 else 16
            xt = Slot([NK, TF], F32, "xt")
            xn = Slot([NK, TF], F32R, "xn")
            sq = [Slot([TT], F32R, "sq%d" % i) for i in range(2)]
            rstd = Slot([TT], F32, "rstd")
            h = Slot([NHH, TF], F32R, "h")
            wg = [Slot([NK, 128], F32R, "wg%d" % i) for i in range(2)]
            wu = [Slot([NK, 128], F32R, "wu%d" % i) for i in range(2)]
            wd = [Slot([NHH, 128], F32R, "wd%d" % i) for i in range(2)]
            sg = [Slot([TT], F32, "sg%d" % i) for i in range(2)]
            wgu_r = wgu.rearrange("(k p) n -> p k n", p=128)
            wdn_r = wdn.rearrange("(j p) n -> p j n", p=128)
            cnt = {"n": 0, "d": 0}

            def tile_body(i):
                x = xt
                load(x, cm(xT)[:, :, i * TF:(i + 1) * TF])
                for s in range(2):
                    rmsnorm(x, gcol, vec[l], xn, sq, rstd, TT, bank=0, cs=slice(s * TT, (s + 1) * TT))
                for half in range(2):
                    for jj in range(NHH):
                        j = half * NHH + jj
                        g_, u_ = wg[j % 2], wu[j % 2]
                        load(g_, wgu_r[:, :, j * 128:(j + 1) * 128])
                        load(u_, wgu_r[:, :, DFF + j * 128:DFF + (j + 1) * 128])
                        for s in range(2):
                            cs = slice(s * TT, (s + 1) * TT)
                            n = cnt["n"]
                            cnt["n"] += 1
                            bg, bu = 1 + n % 2, 3 + n % 2
                            mm_group(bg, [(g_.ap[:, k, :], xn.ap[:, k, cs]) for k in range(NK)], [g_.buf, xn.buf])
                            mm_group(bu, [(u_.ap[:, k, :], xn.ap[:, k, cs]) for k in range(NK)], [u_.buf, xn.buf])
                            s_ = sg[n % 2]
                            P.op("act", lambda e, s_=s_, bg=bg: e.activation(out=s_.ap, in_=psum[bg][:, :], func=AF.Silu),
                                 reads=[PS[bg]], writes=[s_.buf])
                            P.op("dve", lambda e, s_=s_, bu=bu, jj=jj, cs=cs: e.tensor_tensor(
                                out=h.ap[:, jj, cs], in0=s_.ap, in1=psum[bu][:, :], op=ALU.mult),
                                reads=[s_.buf, PS[bu]], writes=[h.buf])
                    for c in range(NK):
                        w_ = wd[c % 2]
                        load(w_, wdn_r[:, half * NHH:(half + 1) * NHH, c * 128:(c + 1) * 128])
                        for s in range(2):
                            cs = slice(s * TT, (s + 1) * TT)
                            d = cnt["d"]
                            cnt["d"] += 1
                            bo = 5 + d % 2
                            mm_group(bo, [(w_.ap[:, jj, :], h.ap[:, jj, cs]) for jj in range(NHH)], [w_.buf, h.buf])
                            P.op("dve", lambda e, c=c, bo=bo, cs=cs: e.scalar_tensor_tensor(
                                out=x.ap[:, c, cs], in0=psum[bo][:, :], scalar=0.5, in1=x.ap[:, c, cs],
                                op0=ALU.mult, op1=ALU.add),
                                reads=[PS[bo], x.buf], writes=[x.buf])
                store(x, cm(xT)[:, :, i * TF:(i + 1) * TF])

            for i in range(T // TF):
                tile_body(i)

        def phase_rope():
            new_phase()
            posi = Slot([TT], I32, "posi")
            posf = Slot([TT], F32, "posf")
            a2 = Slot([TT], F32, "a2")
            ki_ = Slot([TT], I32, "ki")
            kf = Slot([TT], F32, "kf")
            y = Slot([TT], F32, "y")
            m = Slot([TT], F32, "m")
            tab = [Slot([TT], F32, "tab%d" % i) for i in range(2)]
            n = 0
            for i in range(NT):
                load(posi, bass.AP(pos_in.tensor, i * TT, [[0, 128], [1, TT]]), eng="pool")
                P.op("dve", lambda e: e.tensor_copy(out=posf.ap, in_=posi.ap), reads=[posi.buf], writes=[posf.buf])
                for ty in range(2):
                    for cs in range(2):
                        shift = math.pi / 2 if cs == 0 else 0.0
                        P.op("dve", lambda e, ty=ty, shift=shift: e.tensor_scalar(
                            out=a2.ap, in0=posf.ap, scalar1=cst.ap[:, 2 * ty:2 * ty + 1], scalar2=shift,
                            op0=ALU.mult, op1=ALU.add), reads=[posf.buf, cst.buf], writes=[a2.buf])
                        P.op("dve", lambda e: e.tensor_scalar(
                            out=ki_.ap, in0=a2.ap, scalar1=1.0 / TWO_PI, scalar2=None, op0=ALU.mult),
                            reads=[a2.buf], writes=[ki_.buf])
                        P.op("dve", lambda e: e.tensor_copy(out=kf.ap, in_=ki_.ap), reads=[ki_.buf], writes=[kf.buf])
                        P.op("dve", lambda e: e.scalar_tensor_tensor(
                            out=y.ap, in0=kf.ap, scalar=-TWO_PI, in1=a2.ap, op0=ALU.mult, op1=ALU.add),
                            reads=[kf.buf, a2.buf], writes=[y.buf])
                        P.op("dve", lambda e: e.tensor_scalar(
                            out=m.ap, in0=y.ap, scalar1=math.pi, scalar2=-TWO_PI, op0=ALU.is_gt, op1=ALU.mult),
                            reads=[y.buf], writes=[m.buf])
                        P.op("dve", lambda e: e.tensor_tensor(out=y.ap, in0=y.ap, in1=m.ap, op=ALU.add),
                             reads=[y.buf, m.buf], writes=[y.buf])
                        P.op("dve", lambda e: e.tensor_scalar(
                            out=m.ap, in0=y.ap, scalar1=-math.pi, scalar2=TWO_PI, op0=ALU.is_lt, op1=ALU.mult),
                            reads=[y.buf], writes=[m.buf])
                        P.op("dve", lambda e: e.tensor_tensor(out=y.ap, in0=y.ap, in1=m.ap, op=ALU.add),
                             reads=[y.buf, m.buf], writes=[y.buf])
                        P.op("dve", lambda e: e.tensor_scalar(
                            out=y.ap, in0=y.ap, scalar1=3.1415925, scalar2=-3.1415925, op0=ALU.min, op1=ALU.max),
                            reads=[y.buf], writes=[y.buf])
                        tb = tab[n % 2]
                        n += 1
                        P.op("act", lambda e, tb=tb: e.activation(out=tb.ap, in_=y.ap, func=AF.Sin),
                             reads=[y.buf], writes=[tb.buf])
                        store(tb, ROPE[2 * ty + cs, :, i * TT:(i + 1) * TT])

        def phase_proj(l):
            new_phase()
            wi = w_in[l].rearrange("(k p) n -> p k n", p=128)
            xt = [Slot([NK, TT], F32, "xt")] * 2
            un = Slot([NK, TT], F32R, "un")
            sq = [Slot([TT], F32R, "sq%d" % i) for i in range(2)]
            rstd = Slot([TT], F32, "rstd")
            rope = [Slot([4, TT], F32, "rope")] * 2
            W = [Slot([NK, 512], F32R, "W%d" % i) for i in range(3)]
            Wk = Slot([NK, 128], F32R, "Wk")
            Ww = Slot([NK, 128], F32R, "Ww")
            pmat = Slot([2, 128], F32R, "pmat")
            ot = [Slot([TT], F32, "ot%d" % i) for i in range(4)]
            otr = [Slot([TT], F32R, "otr%d" % i) for i in range(4)]
            xs = [Slot([TT], F32R, "xs%d" % i) for i in range(2)]
            t1 = [Slot([TT], F32, "t1_%d" % i) for i in range(2)]
            t2 = [Slot([TT], F32, "t2_%d" % i) for i in range(2)]
            st = {"w": 0, "o": 0, "b": 0, "t": 0, "x": 0}
            load(pmat, pm_in.get().rearrange("a p m -> p a m"))

            def nextW():
                st["w"] += 1
                return W[st["w"] % 3]

            def nexto(r=False):
                st["o"] += 1
                return (otr if r else ot)[st["o"] % 4]

            def nextbank():
                st["b"] += 1
                return 1 + st["b"] % 7

            def nextt():
                st["t"] += 1
                return t1[st["t"] % 2], t2[st["t"] % 2]

            def nextx():
                st["x"] += 1
                return xs[st["x"] % 2]

            def loadblk(col0, ncols=512):
                w_ = nextW()
                load(w_, wi[:, :, col0:col0 + ncols], dst=w_.ap[:, :, 0:ncols])
                return w_

            def proj(w_, c0, m=128):
                bank = nextbank()
                mm_group(bank, [(w_.ap[:, k, c0:c0 + m], un.ap[:, k, :]) for k in range(NK)], [w_.buf, un.buf], m=m)
                return bank

            def tile_body(i):
                tsl = slice(i * TT, (i + 1) * TT)
                x = xt[i % 2]
                rp = rope[i % 2]
                load(x, cm(xT)[:, :, tsl])
                load(rp, ROPE[:, :, tsl].rearrange("a p t -> p a t"))
                rmsnorm(x, 8, vec[l], un, sq, rstd, TT, bank=0)

                def plain(w_, c0, dst, row0, func=None):
                    bank = proj(w_, c0)
                    o = nexto()
                    if func is None:
                        P.op("act", lambda e: e.copy(out=o.ap, in_=psum[bank][:, :]), reads=[PS[bank]], writes=[o.buf])
                    else:
                        P.op("act", lambda e: e.activation(out=o.ap, in_=psum[bank][:, :], func=func),
                             reads=[PS[bank]], writes=[o.buf])
                    store(o, dst[row0:row0 + 128, tsl])

                def gelu(w_, c0, dst, row0):
                    bank = proj(w_, c0)
                    a_, b_ = nextt()
                    o = nexto()
                    P.op("act", lambda e: e.activation(out=a_.ap, in_=psum[bank][:, :], func=AF.Square),
                         reads=[PS[bank]], writes=[a_.buf])
                    P.op("pool", lambda e: e.tensor_scalar(out=a_.ap, in0=a_.ap, scalar1=0.044715, scalar2=1.0,
                                                           op0=ALU.mult, op1=ALU.add),
                         reads=[a_.buf], writes=[a_.buf])
                    P.op("dve", lambda e: e.tensor_tensor(out=b_.ap, in0=a_.ap, in1=psum[bank][:, :], op=ALU.mult),
                         reads=[a_.buf, PS[bank]], writes=[b_.buf])
                    P.op("act", lambda e: e.activation(out=b_.ap, in_=b_.ap, func=AF.Sigmoid, scale=1.5957691216057308),
                         reads=[b_.buf], writes=[b_.buf])
                    P.op("dve", lambda e: e.tensor_tensor(out=o.ap, in0=b_.ap, in1=psum[bank][:, :], op=ALU.mult),
                         reads=[b_.buf, PS[bank]], writes=[o.buf])
                    store(o, dst[row0:row0 + 128, tsl])

                def prod(w1, w2, c0, dst, row0):
                    b1 = proj(w1, c0)
                    b2 = proj(w2, c0)
                    a_, b_ = nextt()
                    o = nexto()
                    P.op("act", lambda e: e.copy(out=a_.ap, in_=psum[b1][:, :]), reads=[PS[b1]], writes=[a_.buf])
                    P.op("dve", lambda e: e.tensor_tensor(out=o.ap, in0=a_.ap, in1=psum[b2][:, :], op=ALU.mult),
                         reads=[a_.buf, PS[b2]], writes=[o.buf])
                    store(o, dst[row0:row0 + 128, tsl])

                def roped(w_, c0, ty, dst, row0):
                    b1 = proj(w_, c0)
                    x_ = nextx()
                    P.op("act", lambda e: e.copy(out=x_.ap, in_=psum[b1][:, :]), reads=[PS[b1]], writes=[x_.buf])
                    b2 = nextbank()
                    mm_group(b2, [(pmat.ap[:, ty, :], x_.ap)], [pmat.buf, x_.buf])
                    a_, b_ = nextt()
                    o = nexto(True)
                    P.op("pool", lambda e: e.tensor_tensor(out=a_.ap, in0=f32(x_.ap), in1=rp.ap[:, 2 * ty, :], op=ALU.mult),
                         reads=[x_.buf, rp.buf], writes=[a_.buf])
                    P.op("dve", lambda e: e.tensor_tensor(out=b_.ap, in0=psum[b2][:, :], in1=rp.ap[:, 2 * ty + 1, :], op=ALU.mult),
                         reads=[PS[b2], rp.buf], writes=[b_.buf])
                    P.op("dve", lambda e: e.tensor_tensor(out=o.ap, in0=a_.ap, in1=b_.ap, op=ALU.add),
                         reads=[a_.buf, b_.buf], writes=[o.buf])
                    store(o, dst[row0:row0 + 128, tsl])

                for blk in range(2):
                    w_ = loadblk(blk * 512)
                    for c in range(4):
                        plain(w_, c * 128, RX, (blk * 4 + c) * 128)
                for blk in range(2):
                    w_ = loadblk(1024 + blk * 512)
                    for c in range(4):
                        gelu(w_, c * 128, GG, (blk * 4 + c) * 128)
                for blk in range(2):
                    w_ = loadblk(2048 + blk * 512)
                    for c in range(4):
                        plain(w_, c * 128, CB, (blk * 4 + c) * 128)
                for blk in range(2):
                    w1 = loadblk(3072 + blk * 512)
                    w2 = loadblk(4096 + blk * 512)
                    for c in range(4):
                        prod(w1, w2, c * 128, CCH, (blk * 4 + c) * 128)
                for blk in range(2):
                    w_ = loadblk(5120 + blk * 512)
                    for c in range(4):
                        roped(w_, c * 128, 0, QT, (blk * 4 + c) * 128)
                w_ = loadblk(6144)
                for g in range(2):
                    roped(w_, g * 128, 0, KT, g * 128)
                for s in range(4):
                    bank = nextbank()
                    mm_group(bank, [(un.ap[:, k, s * 128:(s + 1) * 128], w_.ap[:, k, 256:512]) for k in range(NK)],
                             [w_.buf, un.buf], n=256)
                    o = nexto(True)
                    P.op("act", lambda e, o=o, bank=bank: e.copy(out=o.ap[:, 0:256], in_=psum[bank][:, 0:256]),
                         reads=[PS[bank]], writes=[o.buf])
                    store(o, VV[i * TT + s * 128:i * TT + (s + 1) * 128, :], src=o.ap[:, 0:256])
                w_ = loadblk(6656)
                for c in range(4):
                    roped(w_, c * 128, 1, QIT, c * 128)
                load(Wk, wi[:, :, 7168:7232], dst=Wk.ap[:, :, 0:64])
                load(Wk, wi[:, :, 7168:7232], dst=Wk.ap[:, :, 64:128])
                roped(Wk, 0, 1, KIT, 0)
                load(Ww, wi[:, :, 7232:7360])
                bank = proj(Ww, 0)
                o = nexto(True)
                wsc = (8 ** -0.5) * (64 ** -0.5)
                P.op("act", lambda e, o=o, bank=bank: e.activation(out=o.ap[0:8, :], in_=psum[bank][0:8, :], func=AF.Copy, scale=wsc),
                     reads=[PS[bank]], writes=[o.buf])
                store(o, WIT[:, tsl], src=o.ap[0:8, :])
                for blk in range(6):
                    w_ = loadblk(7240 + blk * 512)
                    for c in range(4):
                        plain(w_, c * 128, GT, (blk * 4 + c) * 128, func=AF.Sigmoid)

            for i in range(NT):
                tile_body(i)

        def phase_rnn(l):
            new_phase()
            v = vec[l]
            S2 = range(2)
            bda = [Slot([128], F32R, "bda%d" % k) for k in S2]
            bdx = [Slot([128], F32R, "bdx%d" % k) for k in S2]
            c1 = [Slot([1], F32, "c1_%d" % k) for k in S2]
            tmpc = [Slot([1], F32, "tmpc%d" % k) for k in S2]
            rx = [[Slot([TT + 3], F32, "rx%d_%d" % (k, i)) for i in range(2)] for k in S2]
            gg = [[Slot([TT], F32, "gg%d" % k)] * 2 for k in S2]
            xa = [Slot([TT], F32, "xa%d" % k) for k in S2]
            xar = [Slot([TT], F32R, "xar%d" % k) for k in S2]
            r_ = [Slot([TT], F32, "r%d" % k) for k in S2]
            ig = [Slot([TT], F32, "ig%d" % k) for k in S2]
            a_ = [Slot([TT], F32, "a%d" % k) for k in S2]
            a2 = [Slot([TT], F32, "a2%d" % k) for k in S2]
            u_ = [Slot([TT], F32, "u%d" % k) for k in S2]
            hs = [[Slot([TT], F32, "h%d_%d" % (k, i)) for i in range(2)] for k in S2]
            hg = [[Slot([TT], F32R, "hg%d_%d" % (k, i)) for i in range(2)] for k in S2]
            cx = [[Slot([TT + 2], F32, "cx%d" % k)] * 2 for k in S2]
            cb = [[Slot([TT], F32, "cb%d" % k)] * 2 for k in S2]
            y = [Slot([TT], F32, "y%d" % k) for k in S2]
            yo = [[Slot([TT], F32R, "yo%d_%d" % (k, i)) for i in range(2)] for k in S2]

            def rnn_setup(c, k):
                P.op("act", lambda e: e.activation(out=bda[k].ap, in_=ones_f.ap, func=AF.Copy, scale=0.0),
                     reads=[ones_f.buf], writes=[bda[k].buf])
                P.op("act", lambda e: e.activation(out=bdx[k].ap, in_=ones_f.ap, func=AF.Copy, scale=0.0),
                     reads=[ones_f.buf], writes=[bdx[k].buf])
                for r in range(2):
                    load(bda[k], w_ga[l, 2 * c + r], eng="pool", dst=bda[k].ap[r * 64:(r + 1) * 64, r * 64:(r + 1) * 64])
                    load(bdx[k], w_gx[l, 2 * c + r], eng="pool", dst=bdx[k].ap[r * 64:(r + 1) * 64, r * 64:(r + 1) * 64])
                P.op("act", lambda e: e.activation(out=tmpc[k].ap, in_=v.ap[:, 48 + c:49 + c], func=AF.Exp, scale=-1.0),
                     reads=[v.buf], writes=[tmpc[k].buf])
                P.op("act", lambda e: e.activation(out=tmpc[k].ap, in_=tmpc[k].ap, func=AF.Ln, bias=1.0, scale=1.0),
                     reads=[tmpc[k].buf], writes=[tmpc[k].buf])
                P.op("dve", lambda e: e.tensor_scalar(out=c1[k].ap, in0=tmpc[k].ap, scalar1=-8.0, scalar2=None, op0=ALU.mult),
                     reads=[tmpc[k].buf], writes=[c1[k].buf])

            def rnn_tile(c, i, k):
                rows = slice(c * 128, (c + 1) * 128)
                tsl = slice(i * TT, (i + 1) * TT)
                x_, g_ = rx[k][i % 2], gg[k][i % 2]
                hcur, hprev, ho = hs[k][i % 2], hs[k][(i + 1) % 2], hg[k][i % 2]
                xa_, xar_, rr, ig_, aa, a2_, uu = xa[k], xar[k], r_[k], ig[k], a_[k], a2[k], u_[k]
                b1, b2 = 1 + 2 * k, 2 + 2 * k
                if i == 0:
                    P.op("dve", lambda e: e.memset(x_.ap[:, 0:3], 0.0), writes=[x_.buf])
                    load(x_, RX[rows, 0:TT], dst=x_.ap[:, 3:TT + 3])
                else:
                    load(x_, RX[rows, i * TT - 3:(i + 1) * TT])
                load(g_, GG[rows, tsl])
                cw = [v.ap[:, 56 + kk * 8 + c:57 + kk * 8 + c] for kk in range(4)]
                P.op("dve", lambda e: e.tensor_scalar(out=xa_.ap, in0=x_.ap[:, 0:TT], scalar1=cw[0],
                                                      scalar2=v.ap[:, 24 + c:25 + c], op0=ALU.mult, op1=ALU.add),
                     reads=[x_.buf, v.buf], writes=[xa_.buf])
                for kk in (1, 2, 3):
                    P.op("dve", lambda e, kk=kk: e.scalar_tensor_tensor(
                        out=xa_.ap, in0=x_.ap[:, kk:kk + TT], scalar=cw[kk], in1=xa_.ap, op0=ALU.mult, op1=ALU.add),
                        reads=[x_.buf, v.buf, xa_.buf], writes=[xa_.buf])
                P.op("act", lambda e: e.copy(out=xar_.ap, in_=xa_.ap), reads=[xa_.buf], writes=[xar_.buf])
                mm_group(b1, [(bda[k].ap, xar_.ap)], [bda[k].buf, xar_.buf])
                mm_group(b2, [(bdx[k].ap, xar_.ap)], [bdx[k].buf, xar_.buf])
                P.op("act", lambda e: e.activation(out=rr.ap, in_=psum[b1][:, :], func=AF.Sigmoid,
                                                   bias=v.ap[:, 32 + c:33 + c], scale=1.0),
                     reads=[PS[b1], v.buf], writes=[rr.buf])
                P.op("act", lambda e: e.activation(out=ig_.ap, in_=psum[b2][:, :], func=AF.Sigmoid,
                                                   bias=v.ap[:, 40 + c:41 + c], scale=1.0),
                     reads=[PS[b2], v.buf], writes=[ig_.buf])
                P.op("act", lambda e: e.activation(out=aa.ap, in_=rr.ap, func=AF.Exp, scale=c1[k].ap[:, 0:1]),
                     reads=[rr.buf, c1[k].buf], writes=[aa.buf])
                P.op("dve", lambda e: e.tensor_tensor(out=a2_.ap, in0=aa.ap, in1=aa.ap, op=ALU.mult),
                     reads=[aa.buf], writes=[a2_.buf])
                P.op("act", lambda e: e.activation(out=a2_.ap, in_=a2_.ap, func=AF.Sqrt, bias=1.0, scale=-1.0),
                     reads=[a2_.buf], writes=[a2_.buf])
                P.op("dve", lambda e: e.tensor_tensor(out=uu.ap, in0=ig_.ap, in1=xa_.ap, op=ALU.mult),
                     reads=[ig_.buf, xa_.buf], writes=[uu.buf])
                P.op("dve", lambda e: e.tensor_tensor(out=uu.ap, in0=uu.ap, in1=a2_.ap, op=ALU.mult),
                     reads=[uu.buf, a2_.buf], writes=[uu.buf])
                if i == 0:
                    P.op("dve", lambda e: e.tensor_tensor_scan(
                        out=hcur.ap, data0=aa.ap, data1=uu.ap, initial=0.0, op0=ALU.mult, op1=ALU.add),
                        reads=[aa.buf, uu.buf], writes=[hcur.buf])
                else:
                    P.op("dve", lambda e: e.tensor_tensor_scan(
                        out=hcur.ap, data0=aa.ap, data1=uu.ap, initial=hprev.ap[:, TT - 1:TT],
                        op0=ALU.mult, op1=ALU.add),
                        reads=[aa.buf, uu.buf, hprev.buf], writes=[hcur.buf])
                P.op("dve", lambda e: e.tensor_tensor(out=ho.ap, in0=hcur.ap, in1=g_.ap, op=ALU.mult),
                     reads=[hcur.buf, g_.buf], writes=[ho.buf])
                store(ho, HG[rows, tsl], eng="pool")

            def sconv_tile(c, i, k):
                rows = slice(c * 128, (c + 1) * 128)
                tsl = slice(i * TT, (i + 1) * TT)
                x_, b_, o, y_ = cx[k][i % 2], cb[k][i % 2], yo[k][i % 2], y[k]
                if i == 0:
                    P.op("dve", lambda e: e.memset(x_.ap[:, 0:2], 0.0), writes=[x_.buf])
                    load(x_, CCH[rows, 0:TT], dst=x_.ap[:, 2:TT + 2])
                else:
                    load(x_, CCH[rows, i * TT - 2:(i + 1) * TT])
                load(b_, CB[rows, tsl])
                sw = [v.ap[:, 88 + kk * 8 + c:89 + kk * 8 + c] for kk in range(3)]
                P.op("dve", lambda e: e.tensor_scalar(out=y_.ap, in0=x_.ap[:, 0:TT], scalar1=sw[0], scalar2=None,
                                                      op0=ALU.mult), reads=[x_.buf, v.buf], writes=[y_.buf])
                for kk in (1, 2):
                    P.op("dve", lambda e, kk=kk: e.scalar_tensor_tensor(
                        out=y_.ap, in0=x_.ap[:, kk:kk + TT], scalar=sw[kk], in1=y_.ap, op0=ALU.mult, op1=ALU.add),
                        reads=[x_.buf, v.buf, y_.buf], writes=[y_.buf])
                P.op("dve", lambda e: e.tensor_tensor(out=o.ap, in0=y_.ap, in1=b_.ap, op=ALU.mult),
                     reads=[y_.buf, b_.buf], writes=[o.buf])
                store(o, YB[rows, tsl], eng="pool")

            for cp_ in range(NK // 2):
                for k in S2:
                    rnn_setup(2 * cp_ + k, k)
                for i in range(NT):
                    for k in S2:
                        rnn_tile(2 * cp_ + k, i, k)
                    for k in S2:
                        sconv_tile(2 * cp_ + k, i, k)

        def phase_sconv(l):
            pass

        def phase_attn(l):
            new_phase()
            kt = Slot([2, T], F32R, "kt")
            vt = Slot([T // 128, 256], F32R, "vt")
            kit = Slot([T], F32R, "kit")
            sel = Slot([8 * 128], F32R, "sel")
            sc = Slot([T // 128, TQ], F32, "sc")
            scb = [Buf("scb%d" % i) for i in range(T // 256)]
            qt = Slot([8, TQ], F32R, "qt")
            qit = Slot([4, TQ], F32R, "qit")
            wit = [Slot([TQ], F32R, "wit")] * 2
            qp = Slot([4, TQ], F32R, "qp")
            qm = Slot([4, TQ], F32R, "qm")
            wp = [Slot([TQ], F32, "wp")] * 2
            wm = [Slot([TQ], F32, "wm")] * 2
            acc2 = Slot([2, TQ], F32, "acc2")
            rtmp = [Slot([2, TQ], F32, "rtmp%d" % i) for i in range(2)]
            lo = Slot([TQ], F32, "lo")
            mid = Slot([TQ], F32, "mid")
            tmp = Slot([TQ], F32, "tmp")
            CBN = 8
            cmpb = [Slot([CBN, TQ], BF16, "cmp%d" % i) for i in range(2)]
            ee = [Slot([2, TQ], F32, "ee%d" % i) for i in range(2)]
            pm_ = [Slot([2, TQ], F32R, "pm%d" % i) for i in range(2)]
            rden = Slot([2, TQ], F32, "rden")
            oh = [Slot([2, TQ], F32R, "oh%d" % i) for i in range(2)]
            load(kt, KT.rearrange("(g p) t -> p g t", p=128))
            load(vt, VV.rearrange("(j p) d -> p j d", p=128))
            load(kit, KIT.get())
            load(sel, sel_in.get(), eng="pool", dst=sel.ap[0:8, :])
            ascale = 128 ** -0.5
            cst_ = {"cn": 0, "en": 0, "on": 0}

            def bcast(slot, nb):
                a = slot.ap
                return bass.AP(a.tensor, a.offset, [list(a.ap[0]), [0, nb], [1, TQ]])

            def attn_q(q):
                t0 = q * TQ
                nkt = 2 * (q + 1)
                Q_, QI, WI = qt, qit, wit[q % 2]
                load(Q_, cm(QT)[:, :, t0:t0 + TQ])
                load(QI, cm(QIT)[:, :, t0:t0 + TQ])
                load(WI, WIT[:, t0:t0 + TQ], dst=WI.ap[0:8, :])
                for hh in range(8):
                    c, r = divmod(hh, 2)
                    bank = 4 + hh % 2
                    mm_group(bank, [(sel.ap[0:8, hh * 128:(hh + 1) * 128], WI.ap[0:8, :])], [sel.buf, WI.buf], n=TQ)
                    p_, m_ = wp[hh % 2], wm[hh % 2]
                    P.op("dve", lambda e, p_=p_, bank=bank: e.tensor_scalar(out=p_.ap, in0=psum[bank][:, 0:TQ], scalar1=0.0,
                                                                            scalar2=None, op0=ALU.max),
                         reads=[PS[bank]], writes=[p_.buf])
                    P.op("dve", lambda e, m_=m_, bank=bank: e.tensor_scalar(out=m_.ap, in0=psum[bank][:, 0:TQ], scalar1=0.0,
                                                                            scalar2=None, op0=ALU.min),
                         reads=[PS[bank]], writes=[m_.buf])
                    rs = slice(r * 64, (r + 1) * 64)
                    P.op("pool", lambda e, p_=p_, c=c, rs=rs: e.tensor_tensor(
                        out=qp.ap[rs, c, :], in0=f32(QI.ap)[rs, c, :], in1=p_.ap[rs, :], op=ALU.mult),
                        reads=[QI.buf, p_.buf], writes=[qp.buf])
                    P.op("pool", lambda e, m_=m_, c=c, rs=rs: e.tensor_tensor(
                        out=qm.ap[rs, c, :], in0=f32(QI.ap)[rs, c, :], in1=m_.ap[rs, :], op=ALU.mult),
                        reads=[QI.buf, m_.buf], writes=[qm.buf])
                for j in range(0, nkt, 2):
                    scj = sc.ap[:, j:j + 2, :]
                    for hh in range(8):
                        c, r = divmod(hh, 2)
                        rs = slice(r * 64, (r + 1) * 64)
                        b1, b2 = 2 * (hh % 2), 2 * (hh % 2) + 1
                        for (bk, qq) in ((b1, qp), (b2, qm)):
                            def fn(e, bk=bk, qq=qq, rs=rs, c=c, j=j):
                                inst = None
                                for jj in range(2):
                                    inst = e.matmul(psum[bk][:, jj * TQ:(jj + 1) * TQ],
                                                    lhsT=kit.ap[rs, (j + jj) * 128:(j + jj + 1) * 128],
                                                    rhs=qq.ap[rs, c, :], start=True, stop=True)
                                return inst
                            P.op("pe", fn, reads=[kit.buf, qq.buf], writes=[PS[bk]])
                        ps1 = psum[b1][:, :].rearrange("p (a b) -> p a b", a=2)
                        ps2 = psum[b2][:, :].rearrange("p (a b) -> p a b", a=2)
                        if hh == 0:
                            P.op("dve", lambda e, ps1=ps1, scj=scj: e.tensor_scalar(
                                out=scj, in0=ps1, scalar1=0.0, scalar2=None, op0=ALU.max),
                                reads=[PS[b1]], writes=[scb[j // 2]])
                        else:
                            P.op("dve", lambda e, ps1=ps1, scj=scj: e.scalar_tensor_tensor(
                                out=scj, in0=ps1, scalar=0.0, in1=scj, op0=ALU.max, op1=ALU.add),
                                reads=[PS[b1], scb[j // 2]], writes=[scb[j // 2]])
                        if hh < 2:
                            P.op("dve", lambda e, ps2=ps2, scj=scj: e.scalar_tensor_tensor(
                                out=scj, in0=ps2, scalar=0.0, in1=scj, op0=ALU.min, op1=ALU.add),
                                reads=[PS[b2], scb[j // 2]], writes=[scb[j // 2]])
                            continue
                        rt = rtmp[hh % 2]
                        P.op("act", lambda e, ps2=ps2, rt=rt: e.activation(out=rt.ap, in_=ps2, func=AF.Relu, scale=-1.0),
                             reads=[PS[b2]], writes=[rt.buf])
                        if hh == 2:
                            P.op("pool", lambda e, rt=rt: e.tensor_copy(out=acc2.ap, in_=rt.ap),
                                 reads=[rt.buf], writes=[acc2.buf])
                        else:
                            P.op("pool", lambda e, rt=rt: e.tensor_tensor(out=acc2.ap, in0=acc2.ap, in1=rt.ap, op=ALU.add),
                                 reads=[rt.buf, acc2.buf], writes=[acc2.buf])
                    P.op("dve", lambda e, scj=scj: e.tensor_tensor(out=scj, in0=scj, in1=acc2.ap, op=ALU.subtract),
                         reads=[scb[j // 2], acc2.buf], writes=[scb[j // 2]])
                    if j >= 2 * q:
                        def fsel(e, j=j, scj=scj):
                            if "r" not in fillreg:
                                fillreg["r"] = e.to_reg(-1.0e30)
                            return e.affine_select(
                                out=scj, in_=scj, pattern=[[-128, 2], [1, TQ]], compare_op=ALU.is_ge,
                                fill=fillreg["r"], base=t0 - j * 128, channel_multiplier=-1)
                        P.op("pool", fsel, reads=[scb[j // 2]], writes=[scb[j // 2]])
                P.op("dve", lambda e: e.memset(lo.ap, -8.0), writes=[lo.buf])
                for it in range(NIT):
                    dstep = 8.0 / (2 ** it)
                    P.op("dve", lambda e, dstep=dstep: e.tensor_scalar(out=mid.ap, in0=lo.ap, scalar1=dstep, scalar2=None,
                                                                      op0=ALU.add), reads=[lo.buf], writes=[mid.buf])
                    cb_ = 6 + it % 2
                    npool = 0
                    chunks = []
                    j0 = 0
                    while j0 < nkt - npool:
                        nb = min(CBN, nkt - npool - j0)
                        chunks.append(("dve", j0, nb))
                        j0 += nb
                    if npool:
                        chunks.insert(min(1, len(chunks)), ("pool", nkt - npool, npool))
                    for ci, (ceng, j0, nb) in enumerate(chunks):
                        cp = cmpb[cst_["cn"] % 2]
                        cst_["cn"] += 1
                        P.op(ceng, lambda e, cp=cp, j0=j0, nb=nb: e.tensor_tensor(
                            out=cp.ap[:, 0:nb, :], in0=sc.ap[:, j0:j0 + nb, :], in1=bcast(mid, nb), op=ALU.is_gt),
                            reads=scb[j0 // 2:(j0 + nb) // 2] + [mid.buf], writes=[cp.buf])
                        def fn(e, cp=cp, nb=nb, cb_=cb_, first=(ci == 0), last=(ci == len(chunks) - 1)):
                            inst = None
                            for jj in range(nb):
                                inst = e.matmul(psum[cb_][:, 0:TQ], lhsT=ones_b.ap, rhs=cp.ap[:, jj, :],
                                                start=(first and jj == 0), stop=(last and jj == nb - 1))
                            return inst
                        P.op("pe", fn, reads=[cp.buf, ones_b.buf], writes=[PS[cb_]])
                    P.op("dve", lambda e, cb_=cb_, dstep=dstep: e.tensor_scalar(
                        out=tmp.ap, in0=psum[cb_][:, 0:TQ], scalar1=255.5, scalar2=dstep, op0=ALU.is_ge, op1=ALU.mult),
                        reads=[PS[cb_]], writes=[tmp.buf])
                    P.op("dve", lambda e: e.tensor_tensor(out=lo.ap, in0=lo.ap, in1=tmp.ap, op=ALU.add),
                         reads=[lo.buf, tmp.buf], writes=[lo.buf])
                for j0 in range(0, nkt, CBN):
                    nb = min(CBN, nkt - j0)
                    P.op("dve", lambda e, j0=j0, nb=nb: e.tensor_tensor(
                        out=sc.ap[:, j0:j0 + nb, :], in0=sc.ap[:, j0:j0 + nb, :], in1=bcast(lo, nb), op=ALU.is_gt),
                        reads=scb[j0 // 2:(j0 + nb) // 2] + [lo.buf], writes=scb[j0 // 2:(j0 + nb) // 2])
                steps = [(hp, j) for hp in range(1 if SKIP_HEADS else 4) for j in range(nkt)]

                def issue_s(n):
                    hp, j = steps[n]
                    bs = (cst_["en"] + n) % 2
                    mm_group(bs, [(kt.ap[:, hp // 2, j * 128:(j + 1) * 128], Q_.ap[:, 2 * hp:2 * hp + 2, :])],
                             [kt.buf, Q_.buf])

                issue_s(0)
                for n, (hp, j) in enumerate(steps):
                    g = hp // 2
                    bo, bd = (2, 3) if hp % 2 == 0 else (4, 5)
                    en = cst_["en"] + n
                    bs = en % 2
                    E = ee[en % 2]
                    Pm = pm_[en % 2]
                    P.op("act", lambda e, E=E, bs=bs: e.activation(
                        out=E.ap, in_=psum[bs][:, :].rearrange("p (a b) -> p a b", a=2), func=AF.Exp, scale=ascale),
                        reads=[PS[bs]], writes=[E.buf])
                    if n + 1 < len(steps):
                        issue_s(n + 1)
                    mj = sc.ap[:, j, :]
                    mjb = bass.AP(mj.tensor, mj.offset, [list(mj.ap[0]), [0, 2], [1, TQ]])
                    P.op("dve", lambda e, E=E, Pm=Pm, mjb=mjb: e.tensor_tensor(
                        out=Pm.ap, in0=E.ap, in1=mjb, op=ALU.mult),
                        reads=[E.buf, scb[j // 2]], writes=[Pm.buf])
                    def fo(e, Pm=Pm, j=j, g=g, bo=bo):
                        return e.matmul(psum[bo][:, :], lhsT=vt.ap[:, j, g * 128:(g + 1) * 128],
                                        rhs=Pm.ap, start=(j == 0), stop=(j == nkt - 1))
                    P.op("pe", fo, reads=[vt.buf, Pm.buf], writes=[PS[bo]])
                    def fd(e, Pm=Pm, j=j, bd=bd):
                        return e.matmul(psum[bd][:, :], lhsT=ones_r.ap, rhs=Pm.ap,
                                        start=(j == 0), stop=(j == nkt - 1))
                    P.op("pe", fd, reads=[ones_r.buf, Pm.buf], writes=[PS[bd]])
                    if j == nkt - 1:
                        P.op("dve", lambda e, bd=bd: e.reciprocal(
                            out=rden.ap, in_=psum[bd][:, :].rearrange("p (a b) -> p a b", a=2)),
                            reads=[PS[bd]], writes=[rden.buf])
                        o_ = oh[cst_["on"] % 2]
                        cst_["on"] += 1
                        P.op("dve", lambda e, bo=bo, o_=o_: e.tensor_tensor(
                            out=o_.ap, in0=psum[bo][:, :].rearrange("p (a b) -> p a b", a=2), in1=rden.ap, op=ALU.mult),
                            reads=[PS[bo], rden.buf], writes=[o_.buf])
                        store(o_, AT[2 * hp * 128:(2 * hp + 2) * 128, t0:t0 + TQ].rearrange("(h p) t -> p h t", p=128))
                cst_["en"] += len(steps)

            for q in range(NQ):
                attn_q(q)

        def phase_merge(l):
            new_phase()
            xt = Slot([NK, TT], F32, "xt")
            br = [Slot([NK, TT], F32R, "br%d" % i) for i in range(2)]
            gt = [Slot([4, TT], F32, "gt%d" % i) for i in range(2)]
            mg = Slot([NK, TT], F32, "mg")
            mgr = Slot([NK, TT], F32R, "mgr")
            tmp = [Slot([TT], F32, "tmp%d" % i) for i in range(2)]
            W = [Slot([NK, 512], F32R, "W%d" % i) for i in range(3)]
            srcs = [(HG, w_ro[l]), (YB, w_so[l]), (AT, w_ao[l])]
            st = {"w": 0, "b": 0, "t": 0, "g": 0}

            def tile_body(i):
                tsl = slice(i * TT, (i + 1) * TT)
                x = xt
                load(x, cm(xT)[:, :, tsl])
                load(br[0], cm(srcs[0][0])[:, :, tsl])
                for b, (src, wmat) in enumerate(srcs):
                    s_ = br[b % 2]
                    if b + 1 < 3:
                        load(br[(b + 1) % 2], cm(srcs[b + 1][0])[:, :, tsl])
                    wr = wmat.rearrange("(k p) n -> p k n", p=128)
                    for blk in range(2):
                        w_ = W[st["w"] % 3]
                        st["w"] += 1
                        load(w_, wr[:, :, blk * 512:(blk + 1) * 512])
                        g_ = gt[st["g"] % 2]
                        st["g"] += 1
                        load(g_, cm(GT[b * D + blk * 512:b * D + (blk + 1) * 512, :])[:, :, tsl])
                        for cc in range(4):
                            c = blk * 4 + cc
                            bank = 1 + st["b"] % 6
                            st["b"] += 1
                            mm_group(bank, [(w_.ap[:, k, cc * 128:(cc + 1) * 128], s_.ap[:, k, :]) for k in range(NK)],
                                     [w_.buf, s_.buf])
                            if b == 0:
                                P.op("dve", lambda e, bank=bank, g_=g_, c=c, cc=cc: e.tensor_tensor(
                                    out=mg.ap[:, c, :], in0=psum[bank][:, :], in1=g_.ap[:, cc, :], op=ALU.mult),
                                    reads=[PS[bank], g_.buf], writes=[mg.buf])
                            else:
                                t_ = tmp[st["t"] % 2]
                                st["t"] += 1
                                dst = mg.ap[:, c, :] if b < 2 else mgr.ap[:, c, :]
                                dbuf = mg.buf if b < 2 else mgr.buf
                                P.op("dve", lambda e, bank=bank, g_=g_, c=c, cc=cc, t_=t_: e.tensor_tensor(
                                    out=t_.ap, in0=psum[bank][:, :], in1=g_.ap[:, cc, :], op=ALU.mult),
                                    reads=[PS[bank], g_.buf], writes=[t_.buf])
                                P.op("pool", lambda e, t_=t_, c=c, dst=dst: e.tensor_tensor(
                                    out=dst, in0=t_.ap, in1=mg.ap[:, c, :], op=ALU.add),
                                    reads=[t_.buf, mg.buf], writes=[dbuf])
                wr = w_o[l].rearrange("(k p) n -> p k n", p=128)
                for blk in range(2):
                    w_ = W[st["w"] % 3]
                    st["w"] += 1
                    load(w_, wr[:, :, blk * 512:(blk + 1) * 512])
                    for cc in range(4):
                        c = blk * 4 + cc
                        bank = 1 + st["b"] % 6
                        st["b"] += 1
                        mm_group(bank, [(w_.ap[:, k, cc * 128:(cc + 1) * 128], mgr.ap[:, k, :]) for k in range(NK)],
                                 [w_.buf, mgr.buf])
                        P.op("dve", lambda e, bank=bank, c=c: e.tensor_tensor(
                            out=x.ap[:, c, :], in0=psum[bank][:, :], in1=x.ap[:, c, :], op=ALU.add),
                            reads=[PS[bank], x.buf], writes=[x.buf])
                store(x, cm(xT)[:, :, tsl])

            for i in range(NT):
                tile_body(i)

        sched = []
        sched.append(("x0", phase_x0))
        sched.append(("rope", phase_rope))
        for l in range(nlayers):
            sched.append(("ffn1_%d" % l, lambda l=l: phase_ffn(l, 0)))
            sched.append(("proj_%d" % l, lambda l=l: phase_proj(l)))
            sched.append(("rnn_%d" % l, lambda l=l: phase_rnn(l)))
            sched.append(("sconv_%d" % l, lambda l=l: phase_sconv(l)))
            sched.append(("attn_%d" % l, lambda l=l: phase_attn(l)))
            sched.append(("merge_%d" % l, lambda l=l: phase_merge(l)))
            sched.append(("ffn2_%d" % l, lambda l=l: phase_ffn(l, 1)))
        sched.append(("out", phase_out))
        for name, fn in sched:
            if phases == "all" or name in phases:
                fn()
        P.barrier()
        P.build(nc)
    _USED[id(nc)] = list(used_inputs)
    return nc


def make_consts():
    theta = 500000.0
    cst = np.zeros((128, 16), np.float32)
    p = np.arange(128)
    invA = (theta ** (-(np.arange(0, 32, 2, dtype=np.float32)) / 32)).astype(np.float32)
    invB = (theta ** (-(np.arange(0, 16, 2, dtype=np.float32)) / 16)).astype(np.float32)
    cst[:32, 0] = invA[p[:32] % 16]
    cst[:, 1] = np.where(p < 16, -1.0, 1.0)
    pm = p % 64
    cst[:, 2] = np.where(pm < 16, invB[pm % 8], 0.0)
    cst[:, 3] = np.where(pm < 8, -1.0, 1.0)
    ident = np.eye(128, dtype=np.float32)
    sel = np.zeros((8, 8 * 128), np.float32)
    for h in range(8):
        sel[h, h * 128:(h + 1) * 128] = 1.0
    pm = np.zeros((2, 128, 128), np.float32)
    for i in range(16):
        pm[0, i + 16, i] = -1.0
        pm[0, i, i + 16] = 1.0
    for hb in (0, 64):
        for i in range(8):
            pm[1, hb + i + 8, hb + i] = -1.0
            pm[1, hb + i, hb + i + 8] = 1.0
    return cst, ident, sel, pm


def pack_vec(inp):
    def col(v):
        return np.ascontiguousarray(v.reshape(8, 128).T)
    vec = np.zeros((L, 128, NVEC), np.float32)
    for l in range(L):
        vec[l, :, 0:8] = col(inp["ffn1_norm"][l])
        vec[l, :, 8:16] = col(inp["mix_norm"][l])
        vec[l, :, 16:24] = col(inp["ffn2_norm"][l])
        vec[l, :, 24:32] = col(inp["rnn_conv_b"][l])
        vec[l, :, 32:40] = col(inp["rnn_gate_a_b"][l])
        vec[l, :, 40:48] = col(inp["rnn_gate_x_b"][l])
        vec[l, :, 48:56] = col(inp["rnn_lambda"][l])
        for k in range(4):
            vec[l, :, 56 + 8 * k:64 + 8 * k] = col(inp["rnn_conv_w"][l, k])
        for k in range(3):
            vec[l, :, 88 + 8 * k:96 + 8 * k] = col(inp["sconv_w"][l, k])
    fin = col(inp["final_norm"])
    return vec, fin


def make_in_maps(inp, batches, used=None, extra=None):
    cst, ident, sel, pm = make_consts()
    vec, fin = pack_vec(inp)
    shared = {k: np.ascontiguousarray(np.asarray(inp[k], dtype=np.float32)) for k in (
        "ffn1_w_gate_up", "ffn2_w_gate_up", "ffn1_w_down", "ffn2_w_down", "w_in", "rnn_gate_a_w",
        "rnn_gate_x_w", "rnn_w_out", "sconv_w_out", "attn_w_out", "w_o")}
    shared.update({"vec": vec, "fin": fin, "cst": cst, "ident": ident, "sel": sel, "pm": pm})
    maps = []
    for b in batches:
        m = dict(shared)
        m["x"] = np.ascontiguousarray(np.asarray(inp["x"][b], dtype=np.float32))
        m["positions"] = np.ascontiguousarray(np.asarray(inp["positions"][b:b + 1], dtype=np.int32))
        if extra:
            m.update(extra)
        if used is not None:
            m = {k: v for k, v in m.items() if k in used}
        maps.append(m)
    return maps


_NC_CACHE = {}


def kernel(**inputs):
    inp = {k: np.asarray(v) for k, v in inputs.items()}
    if "full" not in _NC_CACHE:
        _NC_CACHE["full"] = build_program()
    nc = _NC_CACHE["full"]
    maps = make_in_maps(inp, range(4), used=_USED[id(nc)])
    res = run_bass_kernel_spmd(nc, maps, core_ids=list(range(4)))
    return np.stack([np.asarray(r["out"], dtype=np.float32) for r in res.results], axis=0)
```

```python
import math
from contextlib import ExitStack
import numpy as np
import concourse.bass as bass
import concourse.mybir as mybir
from concourse.bass_utils import run_bass_kernel_spmd

F32 = mybir.dt.float32
BF16 = mybir.dt.bfloat16
F32R = mybir.dt.float32r
I32 = mybir.dt.int32
AF = mybir.ActivationFunctionType
ALU = mybir.AluOpType

T = 4096
D = 1024
DFF = 2816
NK = 8
NH = DFF // 128
TT = 512
NT = T // TT
TQ = 256
NQ = T // TQ
DIN = 10312
L = 2
NVEC = 112
import os
NIT = int(os.environ.get('K_NIT', '18'))
SKIP_HEADS = os.environ.get('K_SKIPH') == '1'
SKIP_IDX = os.environ.get('K_SKIPI') == '1'
ARENA_R = 29100
ARENA_F = 15200
TWO_PI = 2.0 * math.pi


class Buf:
    _n = 0

    def __init__(self, name=None):
        Buf._n += 1
        self.name = name or ("b%d" % Buf._n)
        self.w = None
        self.r = {}


class Prog:
    CE = ("pe", "act", "dve", "pool")
    ALLE = ("pe", "act", "dve", "pool", "sp")

    def __init__(self):
        self.streams = {e: [] for e in self.ALLE}
        self.cnt = {}
        self.known = {e: {} for e in self.ALLE}

    def op(self, eng, fn, reads=(), writes=(), dma_key=None):
        waits = {}

        def need(tok):
            if tok is None:
                return
            k, v = tok
            if eng == "pe" and k == "pe":
                return
            if self.known[eng].get(k, 0) >= v:
                return
            if waits.get(k, 0) < v:
                waits[k] = v

        for b in reads:
            need(b.w)
        for b in writes:
            need(b.w)
            for k, v in b.r.items():
                need((k, v))
        for k, v in waits.items():
            self.known[eng][k] = v
        key = dma_key if dma_key is not None else eng
        step = 16 if dma_key is not None else 1
        self.cnt[key] = self.cnt.get(key, 0) + step
        tok = (key, self.cnt[key])
        self.streams[eng].append((fn, sorted(waits.items()), tok, step))
        for b in reads:
            if b.r.get(key, 0) < tok[1]:
                b.r[key] = tok[1]
        for b in writes:
            b.w = tok
            b.r = {}
        return tok

    def barrier(self):
        snap = dict(self.cnt)
        for e in self.ALLE:
            waits = []
            for k, v in sorted(snap.items()):
                if e == "pe" and k == "pe":
                    continue
                if self.known[e].get(k, 0) >= v:
                    continue
                self.known[e][k] = v
                waits.append((k, v))
            if waits:
                self.streams[e].append(("wait", waits, None, 0))

    def build(self, nc):
        keys = sorted(self.cnt.keys())
        with ExitStack() as es:
            sems = {}
            for i, k in enumerate(keys):
                sems[k] = es.enter_context(nc.semaphore("s%d" % i))
            block = es.enter_context(nc.Block())
            emap = {"pe": block.tensor, "act": block.scalar, "dve": block.vector,
                    "pool": block.gpsimd, "sp": block.sync}
            for e in self.ALLE:
                stream = self.streams[e]
                if not stream:
                    continue

                def body(engine, stream=stream):
                    for fn, waits, tok, step in stream:
                        for k, v in waits:
                            engine.wait_ge(sems[k], v)
                        if fn == "wait":
                            continue
                        inst = fn(engine)
                        inst.then_inc(sems[tok[0]], step)
                emap[e](body)


class Arena:
    def __init__(self, tensor, size, base):
        self.t = tensor
        self.off = 0
        self.size = size
        self.base = base

    def alloc(self, shape, dtype=F32, name=None):
        n = 1
        for s in shape:
            n *= s
        words = n if dtype in (F32, F32R, I32) else (n + 1) // 2
        words = (words + 7) // 8 * 8
        assert self.off + words <= self.size, ("arena overflow", self.off, words, name)
        ap = self.t[:, self.off:self.off + words]
        self.off += words
        if dtype != self.base:
            ap = ap.bitcast(dtype)
        ap = ap[:, 0:n]
        if len(shape) == 2:
            ap = ap.rearrange("p (a b) -> p a b", a=shape[0], b=shape[1])
        elif len(shape) == 3:
            ap = ap.rearrange("p (a b c) -> p a b c", a=shape[0], b=shape[1], c=shape[2])
        return ap, Buf(name)


def f32(ap):
    return ap.bitcast(F32)


_USED = {}


def build_program(phases="all", dbg_out=("out",), dbg_in=(), nlayers=L):
    nc = bass.Bass("TRN2", target_bir_lowering=False)
    P = Prog()

    used_inputs = []

    class LazyIn:
        def __init__(self, name, shape, dt=F32):
            self.name, self.shape, self.dt, self._ap = name, list(shape), dt, None

        def get(self):
            if self._ap is None:
                self._ap = nc.dram_tensor(self.name, self.shape, self.dt, kind="ExternalInput").ap()
                used_inputs.append(self.name)
            return self._ap

        def __getitem__(self, idx):
            return self.get()[idx]

        def rearrange(self, *a, **k):
            return self.get().rearrange(*a, **k)

        @property
        def tensor(self):
            return self.get().tensor

    def din(name, shape, dt=F32):
        return LazyIn(name, shape, dt)

    class LazyScr(LazyIn):
        def get(self):
            if self._ap is None:
                if self.name in dbg_in:
                    kind = "ExternalInput"
                    used_inputs.append(self.name)
                elif self.name in dbg_out:
                    kind = "ExternalOutput"
                else:
                    kind = "Internal"
                self._ap = nc.dram_tensor(self.name, self.shape, self.dt, kind=kind).ap()
            return self._ap

    def dscr(name, shape, dt=F32):
        return LazyScr(name, shape, dt)

    x_in = din("x", [T, D])
    pos_in = din("positions", [1, T], I32)
    w_gu = [din("ffn1_w_gate_up", [L, D, 2 * DFF]), din("ffn2_w_gate_up", [L, D, 2 * DFF])]
    w_dn = [din("ffn1_w_down", [L, DFF, D]), din("ffn2_w_down", [L, DFF, D])]
    w_in = din("w_in", [L, D, DIN])
    w_ga = din("rnn_gate_a_w", [L, 16, 64, 64])
    w_gx = din("rnn_gate_x_w", [L, 16, 64, 64])
    w_ro = din("rnn_w_out", [L, D, D])
    w_so = din("sconv_w_out", [L, D, D])
    w_ao = din("attn_w_out", [L, D, D])
    w_o = din("w_o", [L, D, D])
    vec_in = din("vec", [L, 128, NVEC])
    fin_in = din("fin", [128, NK])
    cst_in = din("cst", [128, 16])
    ident_in = din("ident", [128, 128])
    sel_in = din("sel", [8, 8 * 128])
    pm_in = din("pm", [2, 128, 128])
    out_l = LazyScr("out", [T, D], F32)

    xT = dscr("xT", [D, T])
    RX = dscr("RX", [D, T])
    GG = dscr("GG", [D, T])
    CB = dscr("CB", [D, T])
    CCH = dscr("CCH", [D, T])
    QT = dscr("QT", [D, T])
    KT = dscr("KT", [256, T])
    VV = dscr("VV", [T, 256])
    QIT = dscr("QIT", [512, T])
    KIT = dscr("KIT", [128, T])
    WIT = dscr("WIT", [8, T])
    GT = dscr("GT", [3 * D, T])
    HG = dscr("HG", [D, T])
    YB = dscr("YB", [D, T])
    AT = dscr("AT", [D, T])
    ROPE = dscr("ROPE", [4, 128, T])

    def cm(ap, c0=None):
        return ap.rearrange("(c p) t -> p c t", p=128)

    with ExitStack() as es:
        arena_r = es.enter_context(nc.sbuf_tensor("arena_r", [128, ARENA_R], F32R))
        arena_f = es.enter_context(nc.sbuf_tensor("arena_f", [128, ARENA_F], F32))
        AR_ = Arena(arena_r, ARENA_R, F32R)
        AF_ = Arena(arena_f, ARENA_F, F32)
        psum = [es.enter_context(nc.psum_tensor("ps%d" % i, [128, 512], F32)) for i in range(8)]
        PS = [Buf("ps%d" % i) for i in range(8)]

        keyn = [0]
        fillreg = {}

        def newkey(pfx):
            keyn[0] += 1
            return "%s%d" % (pfx, keyn[0])

        class Slot:
            def __init__(self, shape, dtype=F32, name=None, region=None):
                if region is None:
                    region = "r" if dtype == F32R else "f"
                self.ap, self.buf = (AR_ if region == "r" else AF_).alloc(shape, dtype, name)
                self.lk = None
                self.sk = None

        def load(slot, src, eng="sp", dst=None):
            if slot.lk is None:
                slot.lk = newkey("L")
            d = slot.ap if dst is None else dst
            if d.dtype == F32R:
                eng = "pool"
            P.op(eng, lambda e: e.dma_start(out=d, in_=src), writes=[slot.buf], dma_key=slot.lk)

        def store(slot, dstdram, src=None, eng="sp"):
            if slot.sk is None:
                slot.sk = newkey("S")
            s = slot.ap if src is None else src
            if s.dtype == F32R:
                s = s.bitcast(F32)
            P.op(eng, lambda e: e.dma_start(out=dstdram, in_=s), reads=[slot.buf], dma_key=slot.sk)

        def mm_group(bank, items, reads, n=512, m=128):
            def fn(e):
                inst = None
                for i, (l, r) in enumerate(items):
                    inst = e.matmul(psum[bank][0:m, 0:n], lhsT=l, rhs=r, start=(i == 0), stop=(i == len(items) - 1))
                return inst
            P.op("pe", fn, reads=reads, writes=[PS[bank]])

        ones_r = Slot([128], F32R, "ones_r")
        ones_b = Slot([128], BF16, "ones_b")
        ident = Slot([128], F32, "ident")
        cst = Slot([16], F32, "cst")
        fin = Slot([NK], F32, "fin")
        vec = [Slot([NVEC], F32, "vec%d" % l) for l in range(L)]
        ones_f = Slot([128], F32, "ones_f")
        P.op("dve", lambda e: e.memset(ones_f.ap, 1.0), writes=[ones_f.buf])
        P.op("act", lambda e: e.copy(out=ones_r.ap, in_=ones_f.ap), reads=[ones_f.buf], writes=[ones_r.buf])
        P.op("dve", lambda e: e.tensor_copy(out=ones_b.ap, in_=ones_f.ap), reads=[ones_f.buf], writes=[ones_b.buf])
        load(ident, ident_in.get())
        load(cst, cst_in.get())
        load(fin, fin_in.get())
        for l in range(L):
            load(vec[l], vec_in[l])
        base_off = (AR_.off, AF_.off)
        base_key = keyn[0]

        def new_phase():
            P.barrier()
            AR_.off, AF_.off = base_off
            keyn[0] = base_key

        def rmsnorm(xt, gcol, gslot, outp, sq, rstd, width, bank=0, cs=slice(None)):
            for c in range(NK):
                s = sq[c % 2]
                P.op("act", lambda e, s=s, c=c: e.activation(out=s.ap, in_=xt.ap[:, c, cs], func=AF.Square),
                     reads=[xt.buf], writes=[s.buf])
                def fn(e, s=s, c=c):
                    return e.matmul(psum[bank][:, 0:width], lhsT=ones_r.ap, rhs=s.ap, start=(c == 0), stop=(c == NK - 1))
                P.op("pe", fn, reads=[s.buf, ones_r.buf], writes=[PS[bank]])
            P.op("act", lambda e: e.activation(out=rstd.ap, in_=psum[bank][:, 0:width], func=AF.Sqrt,
                                               bias=1e-6, scale=1.0 / D),
                 reads=[PS[bank]], writes=[rstd.buf])
            P.op("dve", lambda e: e.reciprocal(out=rstd.ap, in_=rstd.ap), reads=[rstd.buf], writes=[rstd.buf])
            for c in range(NK):
                P.op("dve", lambda e, c=c: e.scalar_tensor_tensor(
                    out=outp.ap[:, c, cs], in0=xt.ap[:, c, cs], scalar=gslot.ap[:, gcol + c:gcol + c + 1],
                    in1=rstd.ap, op0=ALU.mult, op1=ALU.mult),
                    reads=[xt.buf, rstd.buf, gslot.buf], writes=[outp.buf])

        def phase_x0():
            new_phase()
            xin = [Slot([4, D], F32, "xin")] * 2
            xo = [Slot([NK, TT], F32, "xo")] * 2
            xr = x_in.rearrange("(i s p) d -> i p s d", s=4, p=128)
            n = 0
            for i in range(NT):
                xi = xin[i % 2]
                o = xo[i % 2]
                load(xi, xr[i])
                for c in range(NK):
                    bank = n % 2
                    n += 1
                    def fn(e, xi=xi, c=c, bank=bank):
                        inst = None
                        for s in range(4):
                            inst = e.transpose(psum[bank][:, s * 128:(s + 1) * 128],
                                               xi.ap[:, s, c * 128:(c + 1) * 128], ident.ap)
                        return inst
                    P.op("pe", fn, reads=[xi.buf, ident.buf], writes=[PS[bank]])
                    if c % 2 == 0:
                        P.op("act", lambda e, o=o, c=c, bank=bank: e.copy(out=o.ap[:, c, :], in_=psum[bank][:, :]),
                             reads=[PS[bank]], writes=[o.buf])
                    else:
                        P.op("dve", lambda e, o=o, c=c, bank=bank: e.tensor_copy(out=o.ap[:, c, :], in_=psum[bank][:, :]),
                             reads=[PS[bank]], writes=[o.buf])
                store(o, cm(xT)[:, :, i * TT:(i + 1) * TT])

        def phase_out():
            new_phase()
            xt = [Slot([NK, TT], F32, "xt")] * 2
            yn = Slot([NK, TT], F32, "yn")
            sq = [Slot([TT], F32R, "sq%d" % i) for i in range(2)]
            rstd = Slot([TT], F32, "rstd")
            yo = [Slot([4, D], F32, "yo")] * 2
            orr = out_l.rearrange("(i s p) d -> i p s d", s=4, p=128)
            n = 0
            for i in range(NT):
                load(xt[i % 2], cm(xT)[:, :, i * TT:(i + 1) * TT])
                x = xt[i % 2]
                rmsnorm(x, 0, fin, yn, sq, rstd, TT, bank=0)
                o = yo[i % 2]
                for s in range(4):
                    for half in range(2):
                        bank = 1 + n % 2
                        n += 1
                        def fn(e, s=s, half=half, bank=bank):
                            inst = None
                            for cc in range(4):
                                c = half * 4 + cc
                                inst = e.transpose(psum[bank][:, cc * 128:(cc + 1) * 128],
                                                   yn.ap[:, c, s * 128:(s + 1) * 128], ident.ap)
                            return inst
                        P.op("pe", fn, reads=[yn.buf, ident.buf], writes=[PS[bank]])
                        if half == 0:
                            P.op("act", lambda e, o=o, s=s, half=half, bank=bank: e.copy(
                                out=o.ap[:, s, half * 512:(half + 1) * 512], in_=psum[bank][:, :]),
                                reads=[PS[bank]], writes=[o.buf])
                        else:
                            P.op("dve", lambda e, o=o, s=s, half=half, bank=bank: e.tensor_copy(
                                out=o.ap[:, s, half * 512:(half + 1) * 512], in_=psum[bank][:, :]),
                                reads=[PS[bank]], writes=[o.buf])
                store(o, orr[i])

        def phase_ffn(l, which):
            new_phase()
            TF = 1024
            NHH = NH // 2
            wgu = w_gu[which][l]
            wdn = w_dn[which][l]
            gcol = 0 if which == 0 else 16
            xt = Slot([NK, TF], F32, "xt")
            xn = Slot([NK, TF], F32R, "xn")
            sq = [Slot([TT], F32R, "sq%d" % i) for i in range(2)]
            rstd = Slot([TT], F32, "rstd")
            h = Slot([NHH, TF], F32R, "h")
            wg = [Slot([NK, 128], F32R, "wg%d" % i) for i in range(2)]
            wu = [Slot([NK, 128], F32R, "wu%d" % i) for i in range(2)]
            wd = [Slot([NHH, 128], F32R, "wd%d" % i) for i in range(3)]
            sg = [Slot([TT], F32, "sg%d" % i) for i in range(2)]
            wgu_r = wgu.rearrange("(k p) n -> p k n", p=128)
            wdn_r = wdn.rearrange("(j p) n -> p j n", p=128)
            cnt = {"n": 0, "d": 0}

            def tile_body(i):
                x = xt
                load(x, cm(xT)[:, :, i * TF:(i + 1) * TF])
                for s in range(2):
                    rmsnorm(x, gcol, vec[l], xn, sq, rstd, TT, bank=0, cs=slice(s * TT, (s + 1) * TT))
                for half in range(2):
                    for jj in range(NHH):
                        j = half * NHH + jj
                        g_, u_ = wg[j % 2], wu[j % 2]
                        load(g_, wgu_r[:, :, j * 128:(j + 1) * 128])
                        load(u_, wgu_r[:, :, DFF + j * 128:DFF + (j + 1) * 128])
                        for s in range(2):
                            cs = slice(s * TT, (s + 1) * TT)
                            n = cnt["n"]
                            cnt["n"] += 1
                            bg, bu = 1 + n % 2, 3 + n % 2
                            mm_group(bg, [(g_.ap[:, k, :], xn.ap[:, k, cs]) for k in range(NK)], [g_.buf, xn.buf])
                            mm_group(bu, [(u_.ap[:, k, :], xn.ap[:, k, cs]) for k in range(NK)], [u_.buf, xn.buf])
                            s_ = sg[n % 2]
                            P.op("act", lambda e, s_=s_, bg=bg: e.activation(out=s_.ap, in_=psum[bg][:, :], func=AF.Silu),
                                 reads=[PS[bg]], writes=[s_.buf])
                            P.op("dve", lambda e, s_=s_, bu=bu, jj=jj, cs=cs: e.tensor_tensor(
                                out=h.ap[:, jj, cs], in0=s_.ap, in1=psum[bu][:, :], op=ALU.mult),
                                reads=[s_.buf, PS[bu]], writes=[h.buf])
                    for c in range(NK):
                        w_ = wd[c % 3]
                        load(w_, wdn_r[:, half * NHH:(half + 1) * NHH, c * 128:(c + 1) * 128])
                        for s in range(2):
                            cs = slice(s * TT, (s + 1) * TT)
                            d = cnt["d"]
                            cnt["d"] += 1
                            bo = 5 + d % 2
                            mm_group(bo, [(w_.ap[:, jj, :], h.ap[:, jj, cs]) for jj in range(NHH)], [w_.buf, h.buf])
                            P.op("dve", lambda e, c=c, bo=bo, cs=cs: e.scalar_tensor_tensor(
                                out=x.ap[:, c, cs], in0=psum[bo][:, :], scalar=0.5, in1=x.ap[:, c, cs],
                                op0=ALU.mult, op1=ALU.add),
                                reads=[PS[bo], x.buf], writes=[x.buf])
                store(x, cm(xT)[:, :, i * TF:(i + 1) * TF])

            for i in range(T // TF):
                tile_body(i)

        def phase_rope():
            new_phase()
            posi = Slot([TT], I32, "posi")
            posf = Slot([TT], F32, "posf")
            a2 = Slot([TT], F32, "a2")
            ki_ = Slot([TT], I32, "ki")
            kf = Slot([TT], F32, "kf")
            y = Slot([TT], F32, "y")
            m = Slot([TT], F32, "m")
            tab = [Slot([TT], F32, "tab%d" % i) for i in range(2)]
            n = 0
            for i in range(NT):
                load(posi, bass.AP(pos_in.tensor, i * TT, [[0, 128], [1, TT]]), eng="pool")
                P.op("dve", lambda e: e.tensor_copy(out=posf.ap, in_=posi.ap), reads=[posi.buf], writes=[posf.buf])
                for ty in range(2):
                    for cs in range(2):
                        shift = math.pi / 2 if cs == 0 else 0.0
                        P.op("dve", lambda e, ty=ty, shift=shift: e.tensor_scalar(
                            out=a2.ap, in0=posf.ap, scalar1=cst.ap[:, 2 * ty:2 * ty + 1], scalar2=shift,
                            op0=ALU.mult, op1=ALU.add), reads=[posf.buf, cst.buf], writes=[a2.buf])
                        P.op("dve", lambda e: e.tensor_scalar(
                            out=ki_.ap, in0=a2.ap, scalar1=1.0 / TWO_PI, scalar2=None, op0=ALU.mult),
                            reads=[a2.buf], writes=[ki_.buf])
                        P.op("dve", lambda e: e.tensor_copy(out=kf.ap, in_=ki_.ap), reads=[ki_.buf], writes=[kf.buf])
                        P.op("dve", lambda e: e.scalar_tensor_tensor(
                            out=y.ap, in0=kf.ap, scalar=-TWO_PI, in1=a2.ap, op0=ALU.mult, op1=ALU.add),
                            reads=[kf.buf, a2.buf], writes=[y.buf])
                        P.op("dve", lambda e: e.tensor_scalar(
                            out=m.ap, in0=y.ap, scalar1=math.pi, scalar2=-TWO_PI, op0=ALU.is_gt, op1=ALU.mult),
                            reads=[y.buf], writes=[m.buf])
                        P.op("dve", lambda e: e.tensor_tensor(out=y.ap, in0=y.ap, in1=m.ap, op=ALU.add),
                             reads=[y.buf, m.buf], writes=[y.buf])
                        P.op("dve", lambda e: e.tensor_scalar(
                            out=m.ap, in0=y.ap, scalar1=-math.pi, scalar2=TWO_PI, op0=ALU.is_lt, op1=ALU.mult),
                            reads=[y.buf], writes=[m.buf])
                        P.op("dve", lambda e: e.tensor_tensor(out=y.ap, in0=y.ap, in1=m.ap, op=ALU.add),
                             reads=[y.buf, m.buf], writes=[y.buf])
                        P.op("dve", lambda e: e.tensor_scalar(
                            out=y.ap, in0=y.ap, scalar1=3.1415925, scalar2=-3.1415925, op0=ALU.min, op1=ALU.max),
                            reads=[y.buf], writes=[y.buf])
                        tb = tab[n % 2]
                        n += 1
                        P.op("act", lambda e, tb=tb: e.activation(out=tb.ap, in_=y.ap, func=AF.Sin),
                             reads=[y.buf], writes=[tb.buf])
                        store(tb, ROPE[2 * ty + cs, :, i * TT:(i + 1) * TT])

        def phase_proj(l):
            new_phase()
            wi = w_in[l].rearrange("(k p) n -> p k n", p=128)
            xt = [Slot([NK, TT], F32, "xt")] * 2
            un = Slot([NK, TT], F32R, "un")
            sq = [Slot([TT], F32R, "sq%d" % i) for i in range(2)]
            rstd = Slot([TT], F32, "rstd")
            rope = [Slot([4, TT], F32, "rope")] * 2
            W = [Slot([NK, 512], F32R, "W%d" % i) for i in range(4)]
            Wk = Slot([NK, 128], F32R, "Wk")
            Ww = Slot([NK, 128], F32R, "Ww")
            pmat = Slot([2, 128], F32R, "pmat")
            ot = [Slot([TT], F32, "ot%d" % i) for i in range(4)]
            otr = [Slot([TT], F32R, "otr%d" % i) for i in range(4)]
            xs = [Slot([TT], F32R, "xs%d" % i) for i in range(2)]
            t1 = [Slot([TT], F32, "t1_%d" % i) for i in range(2)]
            t2 = [Slot([TT], F32, "t2_%d" % i) for i in range(2)]
            st = {"w": 0, "o": 0, "b": 0, "t": 0, "x": 0}
            load(pmat, pm_in.get().rearrange("a p m -> p a m"))

            def nextW():
                st["w"] += 1
                return W[st["w"] % 4]

            def nexto(r=False):
                st["o"] += 1
                return (otr if r else ot)[st["o"] % 4]

            def nextbank():
                st["b"] += 1
                return 1 + st["b"] % 7

            def nextt():
                st["t"] += 1
                return t1[st["t"] % 2], t2[st["t"] % 2]

            def nextx():
                st["x"] += 1
                return xs[st["x"] % 2]

            def loadblk(col0, ncols=512):
                w_ = nextW()
                load(w_, wi[:, :, col0:col0 + ncols], dst=w_.ap[:, :, 0:ncols])
                return w_

            def proj(w_, c0, m=128):
                bank = nextbank()
                mm_group(bank, [(w_.ap[:, k, c0:c0 + m], un.ap[:, k, :]) for k in range(NK)], [w_.buf, un.buf], m=m)
                return bank

            def tile_body(i):
                tsl = slice(i * TT, (i + 1) * TT)
                x = xt[i % 2]
                rp = rope[i % 2]
                load(x, cm(xT)[:, :, tsl])
                load(rp, ROPE[:, :, tsl].rearrange("a p t -> p a t"))
                rmsnorm(x, 8, vec[l], un, sq, rstd, TT, bank=0)

                def plain(w_, c0, dst, row0, func=None):
                    bank = proj(w_, c0)
                    o = nexto()
                    if func is None:
                        P.op("act", lambda e: e.copy(out=o.ap, in_=psum[bank][:, :]), reads=[PS[bank]], writes=[o.buf])
                    else:
                        P.op("act", lambda e: e.activation(out=o.ap, in_=psum[bank][:, :], func=func),
                             reads=[PS[bank]], writes=[o.buf])
                    store(o, dst[row0:row0 + 128, tsl])

                def gelu(w_, c0, dst, row0):
                    bank = proj(w_, c0)
                    a_, b_ = nextt()
                    o = nexto()
                    P.op("act", lambda e: e.activation(out=a_.ap, in_=psum[bank][:, :], func=AF.Square),
                         reads=[PS[bank]], writes=[a_.buf])
                    P.op("dve", lambda e: e.tensor_scalar(out=a_.ap, in0=a_.ap, scalar1=0.044715, scalar2=1.0,
                                                           op0=ALU.mult, op1=ALU.add),
                         reads=[a_.buf], writes=[a_.buf])
                    P.op("dve", lambda e: e.tensor_tensor(out=b_.ap, in0=a_.ap, in1=psum[bank][:, :], op=ALU.mult),
                         reads=[a_.buf, PS[bank]], writes=[b_.buf])
                    P.op("act", lambda e: e.activation(out=b_.ap, in_=b_.ap, func=AF.Sigmoid, scale=1.5957691216057308),
                         reads=[b_.buf], writes=[b_.buf])
                    P.op("dve", lambda e: e.tensor_tensor(out=o.ap, in0=b_.ap, in1=psum[bank][:, :], op=ALU.mult),
                         reads=[b_.buf, PS[bank]], writes=[o.buf])
                    store(o, dst[row0:row0 + 128, tsl])

                def prod(w1, w2, c0, dst, row0):
                    b1 = proj(w1, c0)
                    b2 = proj(w2, c0)
                    a_, b_ = nextt()
                    o = nexto()
                    P.op("act", lambda e: e.copy(out=a_.ap, in_=psum[b1][:, :]), reads=[PS[b1]], writes=[a_.buf])
                    P.op("dve", lambda e: e.tensor_tensor(out=o.ap, in0=a_.ap, in1=psum[b2][:, :], op=ALU.mult),
                         reads=[a_.buf, PS[b2]], writes=[o.buf])
                    store(o, dst[row0:row0 + 128, tsl])

                def roped(w_, c0, ty, dst, row0):
                    b1 = proj(w_, c0)
                    x_ = nextx()
                    P.op("act", lambda e: e.copy(out=x_.ap, in_=psum[b1][:, :]), reads=[PS[b1]], writes=[x_.buf])
                    b2 = nextbank()
                    mm_group(b2, [(pmat.ap[:, ty, :], x_.ap)], [pmat.buf, x_.buf])
                    a_, b_ = nextt()
                    o = nexto(True)
                    P.op("dve", lambda e: e.tensor_tensor(out=a_.ap, in0=f32(x_.ap), in1=rp.ap[:, 2 * ty, :], op=ALU.mult),
                         reads=[x_.buf, rp.buf], writes=[a_.buf])
                    P.op("dve", lambda e: e.tensor_tensor(out=b_.ap, in0=psum[b2][:, :], in1=rp.ap[:, 2 * ty + 1, :], op=ALU.mult),
                         reads=[PS[b2], rp.buf], writes=[b_.buf])
                    P.op("dve", lambda e: e.tensor_tensor(out=o.ap, in0=a_.ap, in1=b_.ap, op=ALU.add),
                         reads=[a_.buf, b_.buf], writes=[o.buf])
                    store(o, dst[row0:row0 + 128, tsl])

                for blk in range(2):
                    w_ = loadblk(blk * 512)
                    for c in range(4):
                        plain(w_, c * 128, RX, (blk * 4 + c) * 128)
                for blk in range(2):
                    w_ = loadblk(1024 + blk * 512)
                    for c in range(4):
                        gelu(w_, c * 128, GG, (blk * 4 + c) * 128)
                for blk in range(2):
                    w_ = loadblk(2048 + blk * 512)
                    for c in range(4):
                        plain(w_, c * 128, CB, (blk * 4 + c) * 128)
                for blk in range(2):
                    w1 = loadblk(3072 + blk * 512)
                    w2 = loadblk(4096 + blk * 512)
                    for c in range(4):
                        prod(w1, w2, c * 128, CCH, (blk * 4 + c) * 128)
                for blk in range(2):
                    w_ = loadblk(5120 + blk * 512)
                    for c in range(4):
                        roped(w_, c * 128, 0, QT, (blk * 4 + c) * 128)
                w_ = loadblk(6144)
                for g in range(2):
                    roped(w_, g * 128, 0, KT, g * 128)
                for s in range(4):
                    bank = nextbank()
                    mm_group(bank, [(un.ap[:, k, s * 128:(s + 1) * 128], w_.ap[:, k, 256:512]) for k in range(NK)],
                             [w_.buf, un.buf], n=256)
                    o = nexto(True)
                    P.op("act", lambda e, o=o, bank=bank: e.copy(out=o.ap[:, 0:256], in_=psum[bank][:, 0:256]),
                         reads=[PS[bank]], writes=[o.buf])
                    store(o, VV[i * TT + s * 128:i * TT + (s + 1) * 128, :], src=o.ap[:, 0:256])
                w_ = loadblk(6656)
                for c in range(4):
                    roped(w_, c * 128, 1, QIT, c * 128)
                load(Wk, wi[:, :, 7168:7232], dst=Wk.ap[:, :, 0:64])
                load(Wk, wi[:, :, 7168:7232], dst=Wk.ap[:, :, 64:128])
                roped(Wk, 0, 1, KIT, 0)
                load(Ww, wi[:, :, 7232:7360])
                bank = proj(Ww, 0)
                o = nexto(True)
                wsc = (8 ** -0.5) * (64 ** -0.5)
                P.op("act", lambda e, o=o, bank=bank: e.activation(out=o.ap[0:8, :], in_=psum[bank][0:8, :], func=AF.Copy, scale=wsc),
                     reads=[PS[bank]], writes=[o.buf])
                store(o, WIT[:, tsl], src=o.ap[0:8, :])
                for blk in range(6):
                    w_ = loadblk(7240 + blk * 512)
                    for c in range(4):
                        plain(w_, c * 128, GT, (blk * 4 + c) * 128, func=AF.Sigmoid)

            for i in range(NT):
                tile_body(i)

        def phase_rnn(l):
            new_phase()
            v = vec[l]
            S2 = range(2)
            bda = [Slot([128], F32R, "bda%d" % k) for k in S2]
            bdx = [Slot([128], F32R, "bdx%d" % k) for k in S2]
            c1 = [Slot([1], F32, "c1_%d" % k) for k in S2]
            tmpc = [Slot([1], F32, "tmpc%d" % k) for k in S2]
            rx = [[Slot([TT + 3], F32, "rx%d_%d" % (k, i)) for i in range(2)] for k in S2]
            gg = [[Slot([TT], F32, "gg%d" % k)] * 2 for k in S2]
            xa = [Slot([TT], F32, "xa%d" % k) for k in S2]
            xar = [Slot([TT], F32R, "xar%d" % k) for k in S2]
            r_ = [Slot([TT], F32, "r%d" % k) for k in S2]
            ig = [Slot([TT], F32, "ig%d" % k) for k in S2]
            a_ = [Slot([TT], F32, "a%d" % k) for k in S2]
            a2 = [Slot([TT], F32, "a2%d" % k) for k in S2]
            u_ = [Slot([TT], F32, "u%d" % k) for k in S2]
            hs = [[Slot([TT], F32, "h%d_%d" % (k, i)) for i in range(2)] for k in S2]
            hg = [[Slot([TT], F32R, "hg%d_%d" % (k, i)) for i in range(2)] for k in S2]
            cx = [[Slot([TT + 2], F32, "cx%d" % k)] * 2 for k in S2]
            cb = [[Slot([TT], F32, "cb%d" % k)] * 2 for k in S2]
            y = [Slot([TT], F32, "y%d" % k) for k in S2]
            yo = [[Slot([TT], F32R, "yo%d_%d" % (k, i)) for i in range(2)] for k in S2]

            def rnn_setup(c, k):
                P.op("act", lambda e: e.activation(out=bda[k].ap, in_=ones_f.ap, func=AF.Copy, scale=0.0),
                     reads=[ones_f.buf], writes=[bda[k].buf])
                P.op("act", lambda e: e.activation(out=bdx[k].ap, in_=ones_f.ap, func=AF.Copy, scale=0.0),
                     reads=[ones_f.buf], writes=[bdx[k].buf])
                for r in range(2):
                    load(bda[k], w_ga[l, 2 * c + r], eng="pool", dst=bda[k].ap[r * 64:(r + 1) * 64, r * 64:(r + 1) * 64])
                    load(bdx[k], w_gx[l, 2 * c + r], eng="pool", dst=bdx[k].ap[r * 64:(r + 1) * 64, r * 64:(r + 1) * 64])
                P.op("act", lambda e: e.activation(out=tmpc[k].ap, in_=v.ap[:, 48 + c:49 + c], func=AF.Exp, scale=-1.0),
                     reads=[v.buf], writes=[tmpc[k].buf])
                P.op("act", lambda e: e.activation(out=tmpc[k].ap, in_=tmpc[k].ap, func=AF.Ln, bias=1.0, scale=1.0),
                     reads=[tmpc[k].buf], writes=[tmpc[k].buf])
                P.op("dve", lambda e: e.tensor_scalar(out=c1[k].ap, in0=tmpc[k].ap, scalar1=-8.0, scalar2=None, op0=ALU.mult),
                     reads=[tmpc[k].buf], writes=[c1[k].buf])

            def rnn_tile(c, i, k):
                rows = slice(c * 128, (c + 1) * 128)
                tsl = slice(i * TT, (i + 1) * TT)
                x_, g_ = rx[k][i % 2], gg[k][i % 2]
                hcur, hprev, ho = hs[k][i % 2], hs[k][(i + 1) % 2], hg[k][i % 2]
                xa_, xar_, rr, ig_, aa, a2_, uu = xa[k], xar[k], r_[k], ig[k], a_[k], a2[k], u_[k]
                b1, b2 = 1 + 2 * k, 2 + 2 * k
                if i == 0:
                    P.op("dve", lambda e: e.memset(x_.ap[:, 0:3], 0.0), writes=[x_.buf])
                    load(x_, RX[rows, 0:TT], dst=x_.ap[:, 3:TT + 3])
                else:
                    load(x_, RX[rows, i * TT - 3:(i + 1) * TT])
                load(g_, GG[rows, tsl])
                cw = [v.ap[:, 56 + kk * 8 + c:57 + kk * 8 + c] for kk in range(4)]
                P.op("dve", lambda e: e.tensor_scalar(out=xa_.ap, in0=x_.ap[:, 0:TT], scalar1=cw[0],
                                                      scalar2=v.ap[:, 24 + c:25 + c], op0=ALU.mult, op1=ALU.add),
                     reads=[x_.buf, v.buf], writes=[xa_.buf])
                for kk in (1, 2, 3):
                    P.op("dve", lambda e, kk=kk: e.scalar_tensor_tensor(
                        out=xa_.ap, in0=x_.ap[:, kk:kk + TT], scalar=cw[kk], in1=xa_.ap, op0=ALU.mult, op1=ALU.add),
                        reads=[x_.buf, v.buf, xa_.buf], writes=[xa_.buf])
                P.op("act", lambda e: e.copy(out=xar_.ap, in_=xa_.ap), reads=[xa_.buf], writes=[xar_.buf])
                mm_group(b1, [(bda[k].ap, xar_.ap)], [bda[k].buf, xar_.buf])
                mm_group(b2, [(bdx[k].ap, xar_.ap)], [bdx[k].buf, xar_.buf])
                P.op("act", lambda e: e.activation(out=rr.ap, in_=psum[b1][:, :], func=AF.Sigmoid,
                                                   bias=v.ap[:, 32 + c:33 + c], scale=1.0),
                     reads=[PS[b1], v.buf], writes=[rr.buf])
                P.op("act", lambda e: e.activation(out=ig_.ap, in_=psum[b2][:, :], func=AF.Sigmoid,
                                                   bias=v.ap[:, 40 + c:41 + c], scale=1.0),
                     reads=[PS[b2], v.buf], writes=[ig_.buf])
                P.op("act", lambda e: e.activation(out=aa.ap, in_=rr.ap, func=AF.Exp, scale=c1[k].ap[:, 0:1]),
                     reads=[rr.buf, c1[k].buf], writes=[aa.buf])
                P.op("dve", lambda e: e.tensor_tensor(out=a2_.ap, in0=aa.ap, in1=aa.ap, op=ALU.mult),
                     reads=[aa.buf], writes=[a2_.buf])
                P.op("act", lambda e: e.activation(out=a2_.ap, in_=a2_.ap, func=AF.Sqrt, bias=1.0, scale=-1.0),
                     reads=[a2_.buf], writes=[a2_.buf])
                P.op("dve", lambda e: e.tensor_tensor(out=uu.ap, in0=ig_.ap, in1=xa_.ap, op=ALU.mult),
                     reads=[ig_.buf, xa_.buf], writes=[uu.buf])
                P.op("dve", lambda e: e.tensor_tensor(out=uu.ap, in0=uu.ap, in1=a2_.ap, op=ALU.mult),
                     reads=[uu.buf, a2_.buf], writes=[uu.buf])
                if i == 0:
                    P.op("dve", lambda e: e.tensor_tensor_scan(
                        out=hcur.ap, data0=aa.ap, data1=uu.ap, initial=0.0, op0=ALU.mult, op1=ALU.add),
                        reads=[aa.buf, uu.buf], writes=[hcur.buf])
                else:
                    P.op("dve", lambda e: e.tensor_tensor_scan(
                        out=hcur.ap, data0=aa.ap, data1=uu.ap, initial=hprev.ap[:, TT - 1:TT],
                        op0=ALU.mult, op1=ALU.add),
                        reads=[aa.buf, uu.buf, hprev.buf], writes=[hcur.buf])
                P.op("dve", lambda e: e.tensor_tensor(out=ho.ap, in0=hcur.ap, in1=g_.ap, op=ALU.mult),
                     reads=[hcur.buf, g_.buf], writes=[ho.buf])
                store(ho, HG[rows, tsl], eng="pool")

            def sconv_tile(c, i, k):
                rows = slice(c * 128, (c + 1) * 128)
                tsl = slice(i * TT, (i + 1) * TT)
                x_, b_, o, y_ = cx[k][i % 2], cb[k][i % 2], yo[k][i % 2], y[k]
                if i == 0:
                    P.op("dve", lambda e: e.memset(x_.ap[:, 0:2], 0.0), writes=[x_.buf])
                    load(x_, CCH[rows, 0:TT], dst=x_.ap[:, 2:TT + 2])
                else:
                    load(x_, CCH[rows, i * TT - 2:(i + 1) * TT])
                load(b_, CB[rows, tsl])
                sw = [v.ap[:, 88 + kk * 8 + c:89 + kk * 8 + c] for kk in range(3)]
                P.op("dve", lambda e: e.tensor_scalar(out=y_.ap, in0=x_.ap[:, 0:TT], scalar1=sw[0], scalar2=None,
                                                      op0=ALU.mult), reads=[x_.buf, v.buf], writes=[y_.buf])
                for kk in (1, 2):
                    P.op("dve", lambda e, kk=kk: e.scalar_tensor_tensor(
                        out=y_.ap, in0=x_.ap[:, kk:kk + TT], scalar=sw[kk], in1=y_.ap, op0=ALU.mult, op1=ALU.add),
                        reads=[x_.buf, v.buf, y_.buf], writes=[y_.buf])
                P.op("dve", lambda e: e.tensor_tensor(out=o.ap, in0=y_.ap, in1=b_.ap, op=ALU.mult),
                     reads=[y_.buf, b_.buf], writes=[o.buf])
                store(o, YB[rows, tsl], eng="pool")

            for cp_ in range(NK // 2):
                for k in S2:
                    rnn_setup(2 * cp_ + k, k)
                for i in range(NT):
                    for k in S2:
                        rnn_tile(2 * cp_ + k, i, k)
                    for k in S2:
                        sconv_tile(2 * cp_ + k, i, k)

        def phase_sconv(l):
            pass

        def phase_attn(l):
            new_phase()
            kt = Slot([2, T], F32R, "kt")
            vt = Slot([T // 128, 256], F32R, "vt")
            kit = Slot([T], F32R, "kit")
            sel = Slot([8 * 128], F32R, "sel")
            sc = Slot([T // 128, TQ], F32, "sc")
            scb = [Buf("scb%d" % i) for i in range(T // 256)]
            qt = Slot([8, TQ], F32R, "qt")
            qit = Slot([4, TQ], F32R, "qit")
            wit = [Slot([TQ], F32R, "wit")] * 2
            qp = Slot([4, TQ], F32R, "qp")
            qm = Slot([4, TQ], F32R, "qm")
            wp = [Slot([TQ], F32, "wp")] * 2
            wm = [Slot([TQ], F32, "wm")] * 2
            acc2 = Slot([2, TQ], F32, "acc2")
            rtmp = [Slot([2, TQ], F32, "rtmp%d" % i) for i in range(2)]
            lo = Slot([TQ], F32, "lo")
            mid = Slot([TQ], F32, "mid")
            tmp = Slot([TQ], F32, "tmp")
            CBN = 8
            cmpb = [Slot([CBN, TQ], BF16, "cmp%d" % i) for i in range(2)]
            ee = [Slot([2, TQ], F32, "ee%d" % i) for i in range(2)]
            pm_ = [Slot([2, TQ], F32R, "pm%d" % i) for i in range(2)]
            rden = Slot([2, TQ], F32, "rden")
            oh = [Slot([2, TQ], F32R, "oh%d" % i) for i in range(2)]
            load(kt, KT.rearrange("(g p) t -> p g t", p=128))
            load(vt, VV.rearrange("(j p) d -> p j d", p=128))
            load(kit, KIT.get())
            load(sel, sel_in.get(), eng="pool", dst=sel.ap[0:8, :])
            ascale = 128 ** -0.5
            cst_ = {"cn": 0, "en": 0, "on": 0}

            def bcast(slot, nb):
                a = slot.ap
                return bass.AP(a.tensor, a.offset, [list(a.ap[0]), [0, nb], [1, TQ]])

            def attn_q(q):
                t0 = q * TQ
                nkt = 2 * (q + 1)
                Q_, QI, WI = qt, qit, wit[q % 2]
                load(Q_, cm(QT)[:, :, t0:t0 + TQ])
                load(QI, cm(QIT)[:, :, t0:t0 + TQ])
                load(WI, WIT[:, t0:t0 + TQ], dst=WI.ap[0:8, :])
                for hh in range(8):
                    c, r = divmod(hh, 2)
                    bank = 4 + hh % 2
                    mm_group(bank, [(sel.ap[0:8, hh * 128:(hh + 1) * 128], WI.ap[0:8, :])], [sel.buf, WI.buf], n=TQ)
                    p_, m_ = wp[hh % 2], wm[hh % 2]
                    P.op("dve", lambda e, p_=p_, bank=bank: e.tensor_scalar(out=p_.ap, in0=psum[bank][:, 0:TQ], scalar1=0.0,
                                                                            scalar2=None, op0=ALU.max),
                         reads=[PS[bank]], writes=[p_.buf])
                    P.op("dve", lambda e, m_=m_, bank=bank: e.tensor_scalar(out=m_.ap, in0=psum[bank][:, 0:TQ], scalar1=0.0,
                                                                            scalar2=None, op0=ALU.min),
                         reads=[PS[bank]], writes=[m_.buf])
                    rs = slice(r * 64, (r + 1) * 64)
                    P.op("pool", lambda e, p_=p_, c=c, rs=rs: e.tensor_tensor(
                        out=qp.ap[rs, c, :], in0=f32(QI.ap)[rs, c, :], in1=p_.ap[rs, :], op=ALU.mult),
                        reads=[QI.buf, p_.buf], writes=[qp.buf])
                    P.op("pool", lambda e, m_=m_, c=c, rs=rs: e.tensor_tensor(
                        out=qm.ap[rs, c, :], in0=f32(QI.ap)[rs, c, :], in1=m_.ap[rs, :], op=ALU.mult),
                        reads=[QI.buf, m_.buf], writes=[qm.buf])
                for j in range(0, nkt, 2):
                    scj = sc.ap[:, j:j + 2, :]
                    for hh in range(8):
                        c, r = divmod(hh, 2)
                        rs = slice(r * 64, (r + 1) * 64)
                        b1, b2 = 2 * (hh % 2), 2 * (hh % 2) + 1
                        for (bk, qq) in ((b1, qp), (b2, qm)):
                            def fn(e, bk=bk, qq=qq, rs=rs, c=c, j=j):
                                inst = None
                                for jj in range(2):
                                    inst = e.matmul(psum[bk][:, jj * TQ:(jj + 1) * TQ],
                                                    lhsT=kit.ap[rs, (j + jj) * 128:(j + jj + 1) * 128],
                                                    rhs=qq.ap[rs, c, :], start=True, stop=True)
                                return inst
                            P.op("pe", fn, reads=[kit.buf, qq.buf], writes=[PS[bk]])
                        ps1 = psum[b1][:, :].rearrange("p (a b) -> p a b", a=2)
                        ps2 = psum[b2][:, :].rearrange("p (a b) -> p a b", a=2)
                        if hh == 0:
                            P.op("dve", lambda e, ps1=ps1, scj=scj: e.tensor_scalar(
                                out=scj, in0=ps1, scalar1=0.0, scalar2=None, op0=ALU.max),
                                reads=[PS[b1]], writes=[scb[j // 2]])
                        else:
                            P.op("dve", lambda e, ps1=ps1, scj=scj: e.scalar_tensor_tensor(
                                out=scj, in0=ps1, scalar=0.0, in1=scj, op0=ALU.max, op1=ALU.add),
                                reads=[PS[b1], scb[j // 2]], writes=[scb[j // 2]])
                        if hh < 2:
                            P.op("dve", lambda e, ps2=ps2, scj=scj: e.scalar_tensor_tensor(
                                out=scj, in0=ps2, scalar=0.0, in1=scj, op0=ALU.min, op1=ALU.add),
                                reads=[PS[b2], scb[j // 2]], writes=[scb[j // 2]])
                            continue
                        rt = rtmp[hh % 2]
                        P.op("act", lambda e, ps2=ps2, rt=rt: e.activation(out=rt.ap, in_=ps2, func=AF.Relu, scale=-1.0),
                             reads=[PS[b2]], writes=[rt.buf])
                        if hh == 2:
                            P.op("pool", lambda e, rt=rt: e.tensor_copy(out=acc2.ap, in_=rt.ap),
                                 reads=[rt.buf], writes=[acc2.buf])
                        else:
                            P.op("pool", lambda e, rt=rt: e.tensor_tensor(out=acc2.ap, in0=acc2.ap, in1=rt.ap, op=ALU.add),
                                 reads=[rt.buf, acc2.buf], writes=[acc2.buf])
                    P.op("dve", lambda e, scj=scj: e.tensor_tensor(out=scj, in0=scj, in1=acc2.ap, op=ALU.subtract),
                         reads=[scb[j // 2], acc2.buf], writes=[scb[j // 2]])
                    if j >= 2 * q:
                        def fsel(e, j=j, scj=scj):
                            if "r" not in fillreg:
                                fillreg["r"] = e.to_reg(-1.0e30)
                            return e.affine_select(
                                out=scj, in_=scj, pattern=[[-128, 2], [1, TQ]], compare_op=ALU.is_ge,
                                fill=fillreg["r"], base=t0 - j * 128, channel_multiplier=-1)
                        P.op("pool", fsel, reads=[scb[j // 2]], writes=[scb[j // 2]])
                P.op("dve", lambda e: e.memset(lo.ap, -8.0), writes=[lo.buf])
                for it in range(NIT):
                    dstep = 8.0 / (2 ** it)
                    P.op("dve", lambda e, dstep=dstep: e.tensor_scalar(out=mid.ap, in0=lo.ap, scalar1=dstep, scalar2=None,
                                                                      op0=ALU.add), reads=[lo.buf], writes=[mid.buf])
                    cb_ = 6 + it % 2
                    npool = 0
                    chunks = []
                    j0 = 0
                    while j0 < nkt - npool:
                        nb = min(CBN, nkt - npool - j0)
                        chunks.append(("dve", j0, nb))
                        j0 += nb
                    if npool:
                        chunks.insert(min(1, len(chunks)), ("pool", nkt - npool, npool))
                    for ci, (ceng, j0, nb) in enumerate(chunks):
                        cp = cmpb[cst_["cn"] % 2]
                        cst_["cn"] += 1
                        P.op(ceng, lambda e, cp=cp, j0=j0, nb=nb: e.tensor_tensor(
                            out=cp.ap[:, 0:nb, :], in0=sc.ap[:, j0:j0 + nb, :], in1=bcast(mid, nb), op=ALU.is_gt),
                            reads=scb[j0 // 2:(j0 + nb) // 2] + [mid.buf], writes=[cp.buf])
                        def fn(e, cp=cp, nb=nb, cb_=cb_, first=(ci == 0), last=(ci == len(chunks) - 1)):
                            inst = None
                            for jj in range(nb):
                                inst = e.matmul(psum[cb_][:, 0:TQ], lhsT=ones_b.ap, rhs=cp.ap[:, jj, :],
                                                start=(first and jj == 0), stop=(last and jj == nb - 1))
                            return inst
                        P.op("pe", fn, reads=[cp.buf, ones_b.buf], writes=[PS[cb_]])
                    P.op("dve", lambda e, cb_=cb_, dstep=dstep: e.tensor_scalar(
                        out=tmp.ap, in0=psum[cb_][:, 0:TQ], scalar1=255.5, scalar2=dstep, op0=ALU.is_ge, op1=ALU.mult),
                        reads=[PS[cb_]], writes=[tmp.buf])
                    P.op("dve", lambda e: e.tensor_tensor(out=lo.ap, in0=lo.ap, in1=tmp.ap, op=ALU.add),
                         reads=[lo.buf, tmp.buf], writes=[lo.buf])
                for j0 in range(0, nkt, CBN):
                    nb = min(CBN, nkt - j0)
                    P.op("dve", lambda e, j0=j0, nb=nb: e.tensor_tensor(
                        out=sc.ap[:, j0:j0 + nb, :], in0=sc.ap[:, j0:j0 + nb, :], in1=bcast(lo, nb), op=ALU.is_gt),
                        reads=scb[j0 // 2:(j0 + nb) // 2] + [lo.buf], writes=scb[j0 // 2:(j0 + nb) // 2])
                steps = [(hp, j) for hp in range(1 if SKIP_HEADS else 4) for j in range(nkt)]

                def issue_s(n):
                    hp, j = steps[n]
                    bs = (cst_["en"] + n) % 2
                    mm_group(bs, [(kt.ap[:, hp // 2, j * 128:(j + 1) * 128], Q_.ap[:, 2 * hp:2 * hp + 2, :])],
                             [kt.buf, Q_.buf])

                issue_s(0)
                for n, (hp, j) in enumerate(steps):
                    g = hp // 2
                    bo, bd = (2, 3) if hp % 2 == 0 else (4, 5)
                    en = cst_["en"] + n
                    bs = en % 2
                    E = ee[en % 2]
                    Pm = pm_[en % 2]
                    P.op("act", lambda e, E=E, bs=bs: e.activation(
                        out=E.ap, in_=psum[bs][:, :].rearrange("p (a b) -> p a b", a=2), func=AF.Exp, scale=ascale),
                        reads=[PS[bs]], writes=[E.buf])
                    if n + 1 < len(steps):
                        issue_s(n + 1)
                    mj = sc.ap[:, j, :]
                    mjb = bass.AP(mj.tensor, mj.offset, [list(mj.ap[0]), [0, 2], [1, TQ]])
                    P.op("dve", lambda e, E=E, Pm=Pm, mjb=mjb: e.tensor_tensor(
                        out=Pm.ap, in0=E.ap, in1=mjb, op=ALU.mult),
                        reads=[E.buf, scb[j // 2]], writes=[Pm.buf])
                    def fo(e, Pm=Pm, j=j, g=g, bo=bo):
                        return e.matmul(psum[bo][:, :], lhsT=vt.ap[:, j, g * 128:(g + 1) * 128],
                                        rhs=Pm.ap, start=(j == 0), stop=(j == nkt - 1))
                    P.op("pe", fo, reads=[vt.buf, Pm.buf], writes=[PS[bo]])
                    def fd(e, Pm=Pm, j=j, bd=bd):
                        return e.matmul(psum[bd][:, :], lhsT=ones_r.ap, rhs=Pm.ap,
                                        start=(j == 0), stop=(j == nkt - 1))
                    P.op("pe", fd, reads=[ones_r.buf, Pm.buf], writes=[PS[bd]])
                    if j == nkt - 1:
                        P.op("dve", lambda e, bd=bd: e.reciprocal(
                            out=rden.ap, in_=psum[bd][:, :].rearrange("p (a b) -> p a b", a=2)),
                            reads=[PS[bd]], writes=[rden.buf])
                        o_ = oh[cst_["on"] % 2]
                        cst_["on"] += 1
                        P.op("dve", lambda e, bo=bo, o_=o_: e.tensor_tensor(
                            out=o_.ap, in0=psum[bo][:, :].rearrange("p (a b) -> p a b", a=2), in1=rden.ap, op=ALU.mult),
                            reads=[PS[bo], rden.buf], writes=[o_.buf])
                        store(o_, AT[2 * hp * 128:(2 * hp + 2) * 128, t0:t0 + TQ].rearrange("(h p) t -> p h t", p=128))
                cst_["en"] += len(steps)

            for q in range(NQ):
                attn_q(q)

        def phase_merge(l):
            new_phase()
            xt = Slot([NK, TT], F32, "xt")
            br = [Slot([NK, TT], F32R, "br%d" % i) for i in range(2)]
            gt = [Slot([4, TT], F32, "gt%d" % i) for i in range(2)]
            mg = Slot([NK, TT], F32, "mg")
            mgr = Slot([NK, TT], F32R, "mgr")
            tmp = [Slot([TT], F32, "tmp%d" % i) for i in range(2)]
            W = [Slot([NK, 512], F32R, "W%d" % i) for i in range(4)]
            srcs = [(HG, w_ro[l]), (YB, w_so[l]), (AT, w_ao[l])]
            st = {"w": 0, "b": 0, "t": 0, "g": 0}

            def tile_body(i):
                tsl = slice(i * TT, (i + 1) * TT)
                x = xt
                load(x, cm(xT)[:, :, tsl])
                load(br[0], cm(srcs[0][0])[:, :, tsl])
                for b, (src, wmat) in enumerate(srcs):
                    s_ = br[b % 2]
                    if b + 1 < 3:
                        load(br[(b + 1) % 2], cm(srcs[b + 1][0])[:, :, tsl])
                    wr = wmat.rearrange("(k p) n -> p k n", p=128)
                    for blk in range(2):
                        w_ = W[st["w"] % 4]
                        st["w"] += 1
                        load(w_, wr[:, :, blk * 512:(blk + 1) * 512])
                        g_ = gt[st["g"] % 2]
                        st["g"] += 1
                        load(g_, cm(GT[b * D + blk * 512:b * D + (blk + 1) * 512, :])[:, :, tsl])
                        for cc in range(4):
                            c = blk * 4 + cc
                            bank = 1 + st["b"] % 6
                            st["b"] += 1
                            mm_group(bank, [(w_.ap[:, k, cc * 128:(cc + 1) * 128], s_.ap[:, k, :]) for k in range(NK)],
                                     [w_.buf, s_.buf])
                            if b == 0:
                                P.op("dve", lambda e, bank=bank, g_=g_, c=c, cc=cc: e.tensor_tensor(
                                    out=mg.ap[:, c, :], in0=psum[bank][:, :], in1=g_.ap[:, cc, :], op=ALU.mult),
                                    reads=[PS[bank], g_.buf], writes=[mg.buf])
                            else:
                                t_ = tmp[st["t"] % 2]
                                st["t"] += 1
                                dst = mg.ap[:, c, :] if b < 2 else mgr.ap[:, c, :]
                                dbuf = mg.buf if b < 2 else mgr.buf
                                P.op("dve", lambda e, bank=bank, g_=g_, c=c, cc=cc, t_=t_: e.tensor_tensor(
                                    out=t_.ap, in0=psum[bank][:, :], in1=g_.ap[:, cc, :], op=ALU.mult),
                                    reads=[PS[bank], g_.buf], writes=[t_.buf])
                                P.op("dve", lambda e, t_=t_, c=c, dst=dst: e.tensor_tensor(
                                    out=dst, in0=t_.ap, in1=mg.ap[:, c, :], op=ALU.add),
                                    reads=[t_.buf, mg.buf], writes=[dbuf])
                wr = w_o[l].rearrange("(k p) n -> p k n", p=128)
                for blk in range(2):
                    w_ = W[st["w"] % 4]
                    st["w"] += 1
                    load(w_, wr[:, :, blk * 512:(blk + 1) * 512])
                    for cc in range(4):
                        c = blk * 4 + cc
                        bank = 1 + st["b"] % 6
                        st["b"] += 1
                        mm_group(bank, [(w_.ap[:, k, cc * 128:(cc + 1) * 128], mgr.ap[:, k, :]) for k in range(NK)],
                                 [w_.buf, mgr.buf])
                        P.op("dve", lambda e, bank=bank, c=c: e.tensor_tensor(
                            out=x.ap[:, c, :], in0=psum[bank][:, :], in1=x.ap[:, c, :], op=ALU.add),
                            reads=[PS[bank], x.buf], writes=[x.buf])
                store(x, cm(xT)[:, :, tsl])

            for i in range(NT):
                tile_body(i)

        sched = []
        sched.append(("x0", phase_x0))
        sched.append(("rope", phase_rope))
        for l in range(nlayers):
            sched.append(("ffn1_%d" % l, lambda l=l: phase_ffn(l, 0)))
            sched.append(("proj_%d" % l, lambda l=l: phase_proj(l)))
            sched.append(("rnn_%d" % l, lambda l=l: phase_rnn(l)))
            sched.append(("sconv_%d" % l, lambda l=l: phase_sconv(l)))
            sched.append(("attn_%d" % l, lambda l=l: phase_attn(l)))
            sched.append(("merge_%d" % l, lambda l=l: phase_merge(l)))
            sched.append(("ffn2_%d" % l, lambda l=l: phase_ffn(l, 1)))
        sched.append(("out", phase_out))
        for name, fn in sched:
            if phases == "all" or name in phases:
                fn()
        P.barrier()
        P.build(nc)
    _USED[id(nc)] = list(used_inputs)
    return nc


def make_consts():
    theta = 500000.0
    cst = np.zeros((128, 16), np.float32)
    p = np.arange(128)
    invA = (theta ** (-(np.arange(0, 32, 2, dtype=np.float32)) / 32)).astype(np.float32)
    invB = (theta ** (-(np.arange(0, 16, 2, dtype=np.float32)) / 16)).astype(np.float32)
    cst[:32, 0] = invA[p[:32] % 16]
    cst[:, 1] = np.where(p < 16, -1.0, 1.0)
    pm = p % 64
    cst[:, 2] = np.where(pm < 16, invB[pm % 8], 0.0)
    cst[:, 3] = np.where(pm < 8, -1.0, 1.0)
    ident = np.eye(128, dtype=np.float32)
    sel = np.zeros((8, 8 * 128), np.float32)
    for h in range(8):
        sel[h, h * 128:(h + 1) * 128] = 1.0
    pm = np.zeros((2, 128, 128), np.float32)
    for i in range(16):
        pm[0, i + 16, i] = -1.0
        pm[0, i, i + 16] = 1.0
    for hb in (0, 64):
        for i in range(8):
            pm[1, hb + i + 8, hb + i] = -1.0
            pm[1, hb + i, hb + i + 8] = 1.0
    return cst, ident, sel, pm


def pack_vec(inp):
    def col(v):
        return np.ascontiguousarray(v.reshape(8, 128).T)
    vec = np.zeros((L, 128, NVEC), np.float32)
    for l in range(L):
        vec[l, :, 0:8] = col(inp["ffn1_norm"][l])
        vec[l, :, 8:16] = col(inp["mix_norm"][l])
        vec[l, :, 16:24] = col(inp["ffn2_norm"][l])
        vec[l, :, 24:32] = col(inp["rnn_conv_b"][l])
        vec[l, :, 32:40] = col(inp["rnn_gate_a_b"][l])
        vec[l, :, 40:48] = col(inp["rnn_gate_x_b"][l])
        vec[l, :, 48:56] = col(inp["rnn_lambda"][l])
        for k in range(4):
            vec[l, :, 56 + 8 * k:64 + 8 * k] = col(inp["rnn_conv_w"][l, k])
        for k in range(3):
            vec[l, :, 88 + 8 * k:96 + 8 * k] = col(inp["sconv_w"][l, k])
    fin = col(inp["final_norm"])
    return vec, fin


def make_in_maps(inp, batches, used=None, extra=None):
    cst, ident, sel, pm = make_consts()
    vec, fin = pack_vec(inp)
    shared = {k: np.ascontiguousarray(np.asarray(inp[k], dtype=np.float32)) for k in (
        "ffn1_w_gate_up", "ffn2_w_gate_up", "ffn1_w_down", "ffn2_w_down", "w_in", "rnn_gate_a_w",
        "rnn_gate_x_w", "rnn_w_out", "sconv_w_out", "attn_w_out", "w_o")}
    shared.update({"vec": vec, "fin": fin, "cst": cst, "ident": ident, "sel": sel, "pm": pm})
    maps = []
    for b in batches:
        m = dict(shared)
        m["x"] = np.ascontiguousarray(np.asarray(inp["x"][b], dtype=np.float32))
        m["positions"] = np.ascontiguousarray(np.asarray(inp["positions"][b:b + 1], dtype=np.int32))
        if extra:
            m.update(extra)
        if used is not None:
            m = {k: v for k, v in m.items() if k in used}
        maps.append(m)
    return maps


_NC_CACHE = {}


def kernel(**inputs):
    inp = {k: np.asarray(v) for k, v in inputs.items()}
    if "full" not in _NC_CACHE:
        _NC_CACHE["full"] = build_program()
    nc = _NC_CACHE["full"]
    maps = make_in_maps(inp, range(4), used=_USED[id(nc)])
    res = run_bass_kernel_spmd(nc, maps, core_ids=list(range(4)))
    return np.stack([np.asarray(r["out"], dtype=np.float32) for r in res.results], axis=0)
```

```python
import math
from contextlib import ExitStack
import numpy as np
import concourse.bass as bass
import concourse.mybir as mybir
from concourse.bass_utils import run_bass_kernel_spmd

F32 = mybir.dt.float32
BF16 = mybir.dt.bfloat16
F32R = mybir.dt.float32r
I32 = mybir.dt.int32
AF = mybir.ActivationFunctionType
ALU = mybir.AluOpType

T = 4096
D = 1024
DFF = 2816
NK = 8
NH = DFF // 128
TT = 512
NT = T // TT
TQ = 256
NQ = T // TQ
DIN = 10312
L = 2
NVEC = 112
import os
NIT = int(os.environ.get('K_NIT', '18'))
SKIP_HEADS = os.environ.get('K_SKIPH') == '1'
SKIP_IDX = os.environ.get('K_SKIPI') == '1'
ARENA_R = 29100
ARENA_F = 15200
TWO_PI = 2.0 * math.pi


class Buf:
    _n = 0

    def __init__(self, name=None):
        Buf._n += 1
        self.name = name or ("b%d" % Buf._n)
        self.w = None
        self.r = {}


class Prog:
    CE = ("pe", "act", "dve", "pool")
    ALLE = ("pe", "act", "dve", "pool", "sp")

    def __init__(self):
        self.streams = {e: [] for e in self.ALLE}
        self.cnt = {}
        self.known = {e: {} for e in self.ALLE}

    def op(self, eng, fn, reads=(), writes=(), dma_key=None):
        waits = {}

        def need(tok):
            if tok is None:
                return
            k, v = tok
            if eng == "pe" and k == "pe":
                return
            if self.known[eng].get(k, 0) >= v:
                return
            if waits.get(k, 0) < v:
                waits[k] = v

        for b in reads:
            need(b.w)
        for b in writes:
            need(b.w)
            for k, v in b.r.items():
                need((k, v))
        for k, v in waits.items():
            self.known[eng][k] = v
        key = dma_key if dma_key is not None else eng
        step = 16 if dma_key is not None else 1
        self.cnt[key] = self.cnt.get(key, 0) + step
        tok = (key, self.cnt[key])
        self.streams[eng].append((fn, sorted(waits.items()), tok, step))
        for b in reads:
            if b.r.get(key, 0) < tok[1]:
                b.r[key] = tok[1]
        for b in writes:
            b.w = tok
            b.r = {}
        return tok

    def barrier(self):
        snap = dict(self.cnt)
        for e in self.ALLE:
            waits = []
            for k, v in sorted(snap.items()):
                if e == "pe" and k == "pe":
                    continue
                if self.known[e].get(k, 0) >= v:
                    continue
                self.known[e][k] = v
                waits.append((k, v))
            if waits:
                self.streams[e].append(("wait", waits, None, 0))

    def build(self, nc):
        keys = sorted(self.cnt.keys())
        with ExitStack() as es:
            sems = {}
            for i, k in enumerate(keys):
                sems[k] = es.enter_context(nc.semaphore("s%d" % i))
            block = es.enter_context(nc.Block())
            emap = {"pe": block.tensor, "act": block.scalar, "dve": block.vector,
                    "pool": block.gpsimd, "sp": block.sync}
            for e in self.ALLE:
                stream = self.streams[e]
                if not stream:
                    continue

                def body(engine, stream=stream):
                    for fn, waits, tok, step in stream:
                        for k, v in waits:
                            engine.wait_ge(sems[k], v)
                        if fn == "wait":
                            continue
                        inst = fn(engine)
                        inst.then_inc(sems[tok[0]], step)
                emap[e](body)


class Arena:
    def __init__(self, tensor, size, base):
        self.t = tensor
        self.off = 0
        self.size = size
        self.base = base

    def alloc(self, shape, dtype=F32, name=None):
        n = 1
        for s in shape:
            n *= s
        words = n if dtype in (F32, F32R, I32) else (n + 1) // 2
        words = (words + 7) // 8 * 8
        assert self.off + words <= self.size, ("arena overflow", self.off, words, name)
        ap = self.t[:, self.off:self.off + words]
        self.off += words
        if dtype != self.base:
            ap = ap.bitcast(dtype)
        ap = ap[:, 0:n]
        if len(shape) == 2:
            ap = ap.rearrange("p (a b) -> p a b", a=shape[0], b=shape[1])
        elif len(shape) == 3:
            ap = ap.rearrange("p (a b c) -> p a b c", a=shape[0], b=shape[1], c=shape[2])
        return ap, Buf(name)


def f32(ap):
    return ap.bitcast(F32)


_USED = {}


def build_program(phases="all", dbg_out=("out",), dbg_in=(), nlayers=L):
    nc = bass.Bass("TRN2", target_bir_lowering=False)
    P = Prog()

    used_inputs = []

    class LazyIn:
        def __init__(self, name, shape, dt=F32):
            self.name, self.shape, self.dt, self._ap = name, list(shape), dt, None

        def get(self):
            if self._ap is None:
                self._ap = nc.dram_tensor(self.name, self.shape, self.dt, kind="ExternalInput").ap()
                used_inputs.append(self.name)
            return self._ap

        def __getitem__(self, idx):
            return self.get()[idx]

        def rearrange(self, *a, **k):
            return self.get().rearrange(*a, **k)

        @property
        def tensor(self):
            return self.get().tensor

    def din(name, shape, dt=F32):
        return LazyIn(name, shape, dt)

    class LazyScr(LazyIn):
        def get(self):
            if self._ap is None:
                if self.name in dbg_in:
                    kind = "ExternalInput"
                    used_inputs.append(self.name)
                elif self.name in dbg_out:
                    kind = "ExternalOutput"
                else:
                    kind = "Internal"
                self._ap = nc.dram_tensor(self.name, self.shape, self.dt, kind=kind).ap()
            return self._ap

    def dscr(name, shape, dt=F32):
        return LazyScr(name, shape, dt)

    x_in = din("x", [T, D])
    pos_in = din("positions", [1, T], I32)
    w_gu = [din("ffn1_w_gate_up", [L, D, 2 * DFF]), din("ffn2_w_gate_up", [L, D, 2 * DFF])]
    w_dn = [din("ffn1_w_down", [L, DFF, D]), din("ffn2_w_down", [L, DFF, D])]
    w_in = din("w_in", [L, D, DIN])
    w_ga = din("rnn_gate_a_w", [L, 16, 64, 64])
    w_gx = din("rnn_gate_x_w", [L, 16, 64, 64])
    w_ro = din("rnn_w_out", [L, D, D])
    w_so = din("sconv_w_out", [L, D, D])
    w_ao = din("attn_w_out", [L, D, D])
    w_o = din("w_o", [L, D, D])
    vec_in = din("vec", [L, 128, NVEC])
    fin_in = din("fin", [128, NK])
    cst_in = din("cst", [128, 16])
    ident_in = din("ident", [128, 128])
    sel_in = din("sel", [8, 8 * 128])
    pm_in = din("pm", [2, 128, 128])
    out_l = LazyScr("out", [T, D], F32)

    xT = dscr("xT", [D, T])
    RX = dscr("RX", [D, T])
    GG = dscr("GG", [D, T])
    CB = dscr("CB", [D, T])
    CCH = dscr("CCH", [D, T])
    QT = dscr("QT", [D, T])
    KT = dscr("KT", [256, T])
    VV = dscr("VV", [T, 256])
    QIT = dscr("QIT", [512, T])
    KIT = dscr("KIT", [128, T])
    WIT = dscr("WIT", [8, T])
    GT = dscr("GT", [3 * D, T])
    HG = dscr("HG", [D, T])
    YB = dscr("YB", [D, T])
    AT = dscr("AT", [D, T])
    ROPE = dscr("ROPE", [4, 128, T])

    def cm(ap, c0=None):
        return ap.rearrange("(c p) t -> p c t", p=128)

    with ExitStack() as es:
        arena_r = es.enter_context(nc.sbuf_tensor("arena_r", [128, ARENA_R], F32R))
        arena_f = es.enter_context(nc.sbuf_tensor("arena_f", [128, ARENA_F], F32))
        AR_ = Arena(arena_r, ARENA_R, F32R)
        AF_ = Arena(arena_f, ARENA_F, F32)
        psum = [es.enter_context(nc.psum_tensor("ps%d" % i, [128, 512], F32)) for i in range(8)]
        PS = [Buf("ps%d" % i) for i in range(8)]

        keyn = [0]
        fillreg = {}

        def newkey(pfx):
            keyn[0] += 1
            return "%s%d" % (pfx, keyn[0])

        class Slot:
            def __init__(self, shape, dtype=F32, name=None, region=None):
                if region is None:
                    region = "r" if dtype == F32R else "f"
                self.ap, self.buf = (AR_ if region == "r" else AF_).alloc(shape, dtype, name)
                self.lk = None
                self.sk = None

        def load(slot, src, eng="sp", dst=None):
            if slot.lk is None:
                slot.lk = newkey("L")
            d = slot.ap if dst is None else dst
            if d.dtype == F32R:
                eng = "pool"
            P.op(eng, lambda e: e.dma_start(out=d, in_=src), writes=[slot.buf], dma_key=slot.lk)

        def store(slot, dstdram, src=None, eng="sp"):
            if slot.sk is None:
                slot.sk = newkey("S")
            s = slot.ap if src is None else src
            if s.dtype == F32R:
                s = s.bitcast(F32)
            P.op(eng, lambda e: e.dma_start(out=dstdram, in_=s), reads=[slot.buf], dma_key=slot.sk)

        def mm_group(bank, items, reads, n=512, m=128):
            def fn(e):
                inst = None
                for i, (l, r) in enumerate(items):
                    inst = e.matmul(psum[bank][0:m, 0:n], lhsT=l, rhs=r, start=(i == 0), stop=(i == len(items) - 1))
                return inst
            P.op("pe", fn, reads=reads, writes=[PS[bank]])

        ones_r = Slot([128], F32R, "ones_r")
        ones_b = Slot([128], BF16, "ones_b")
        ident = Slot([128], F32, "ident")
        cst = Slot([16], F32, "cst")
        fin = Slot([NK], F32, "fin")
        vec = [Slot([NVEC], F32, "vec%d" % l) for l in range(L)]
        ones_f = Slot([128], F32, "ones_f")
        P.op("dve", lambda e: e.memset(ones_f.ap, 1.0), writes=[ones_f.buf])
        P.op("act", lambda e: e.copy(out=ones_r.ap, in_=ones_f.ap), reads=[ones_f.buf], writes=[ones_r.buf])
        P.op("dve", lambda e: e.tensor_copy(out=ones_b.ap, in_=ones_f.ap), reads=[ones_f.buf], writes=[ones_b.buf])
        load(ident, ident_in.get())
        load(cst, cst_in.get())
        load(fin, fin_in.get())
        for l in range(L):
            load(vec[l], vec_in[l])
        base_off = (AR_.off, AF_.off)
        base_key = keyn[0]

        def new_phase():
            P.barrier()
            AR_.off, AF_.off = base_off
            keyn[0] = base_key

        def rmsnorm(xt, gcol, gslot, outp, sq, rstd, width, bank=0, cs=slice(None)):
            for c in range(NK):
                s = sq[c % 2]
                P.op("act", lambda e, s=s, c=c: e.activation(out=s.ap, in_=xt.ap[:, c, cs], func=AF.Square),
                     reads=[xt.buf], writes=[s.buf])
                def fn(e, s=s, c=c):
                    return e.matmul(psum[bank][:, 0:width], lhsT=ones_r.ap, rhs=s.ap, start=(c == 0), stop=(c == NK - 1))
                P.op("pe", fn, reads=[s.buf, ones_r.buf], writes=[PS[bank]])
            P.op("act", lambda e: e.activation(out=rstd.ap, in_=psum[bank][:, 0:width], func=AF.Sqrt,
                                               bias=1e-6, scale=1.0 / D),
                 reads=[PS[bank]], writes=[rstd.buf])
            P.op("dve", lambda e: e.reciprocal(out=rstd.ap, in_=rstd.ap), reads=[rstd.buf], writes=[rstd.buf])
            for c in range(NK):
                P.op("dve", lambda e, c=c: e.scalar_tensor_tensor(
                    out=outp.ap[:, c, cs], in0=xt.ap[:, c, cs], scalar=gslot.ap[:, gcol + c:gcol + c + 1],
                    in1=rstd.ap, op0=ALU.mult, op1=ALU.mult),
                    reads=[xt.buf, rstd.buf, gslot.buf], writes=[outp.buf])

        def phase_x0():
            new_phase()
            xin = [Slot([4, D], F32, "xin")] * 2
            xo = [Slot([NK, TT], F32, "xo")] * 2
            xr = x_in.rearrange("(i s p) d -> i p s d", s=4, p=128)
            n = 0
            for i in range(NT):
                xi = xin[i % 2]
                o = xo[i % 2]
                load(xi, xr[i])
                for c in range(NK):
                    bank = n % 2
                    n += 1
                    def fn(e, xi=xi, c=c, bank=bank):
                        inst = None
                        for s in range(4):
                            inst = e.transpose(psum[bank][:, s * 128:(s + 1) * 128],
                                               xi.ap[:, s, c * 128:(c + 1) * 128], ident.ap)
                        return inst
                    P.op("pe", fn, reads=[xi.buf, ident.buf], writes=[PS[bank]])
                    if c % 2 == 0:
                        P.op("act", lambda e, o=o, c=c, bank=bank: e.copy(out=o.ap[:, c, :], in_=psum[bank][:, :]),
                             reads=[PS[bank]], writes=[o.buf])
                    else:
                        P.op("dve", lambda e, o=o, c=c, bank=bank: e.tensor_copy(out=o.ap[:, c, :], in_=psum[bank][:, :]),
                             reads=[PS[bank]], writes=[o.buf])
                store(o, cm(xT)[:, :, i * TT:(i + 1) * TT])

        def phase_out():
            new_phase()
            xt = [Slot([NK, TT], F32, "xt")] * 2
            yn = Slot([NK, TT], F32, "yn")
            sq = [Slot([TT], F32R, "sq%d" % i) for i in range(2)]
            rstd = Slot([TT], F32, "rstd")
            yo = [Slot([4, D], F32, "yo")] * 2
            orr = out_l.rearrange("(i s p) d -> i p s d", s=4, p=128)
            n = 0
            for i in range(NT):
                load(xt[i % 2], cm(xT)[:, :, i * TT:(i + 1) * TT])
                x = xt[i % 2]
                rmsnorm(x, 0, fin, yn, sq, rstd, TT, bank=0)
                o = yo[i % 2]
                for s in range(4):
                    for half in range(2):
                        bank = 1 + n % 2
                        n += 1
                        def fn(e, s=s, half=half, bank=bank):
                            inst = None
                            for cc in range(4):
                                c = half * 4 + cc
                                inst = e.transpose(psum[bank][:, cc * 128:(cc + 1) * 128],
                                                   yn.ap[:, c, s * 128:(s + 1) * 128], ident.ap)
                            return inst
                        P.op("pe", fn, reads=[yn.buf, ident.buf], writes=[PS[bank]])
                        if half == 0:
                            P.op("act", lambda e, o=o, s=s, half=half, bank=bank: e.copy(
                                out=o.ap[:, s, half * 512:(half + 1) * 512], in_=psum[bank][:, :]),
                                reads=[PS[bank]], writes=[o.buf])
                        else:
                            P.op("dve", lambda e, o=o, s=s, half=half, bank=bank: e.tensor_copy(
                                out=o.ap[:, s, half * 512:(half + 1) * 512], in_=psum[bank][:, :]),
                                reads=[PS[bank]], writes=[o.buf])
                store(o, orr[i])

        def phase_ffn(l, which):
            new_phase()
            TF = 1024
            NHH = NH // 2
            wgu = w_gu[which][l]
            wdn = w_dn[which][l]
            gcol = 0 if which == 0 else 16
            xt = Slot([NK, TF], F32, "xt")
            xn = Slot([NK, TF], F32R, "xn")
            sq = [Slot([TT], F32R, "sq%d" % i) for i in range(2)]
            rstd = Slot([TT], F32, "rstd")
            h = Slot([NHH, TF], F32R, "h")
            wg = [Slot([NK, 128], F32R, "wg%d" % i) for i in range(2)]
            wu = [Slot([NK, 128], F32R, "wu%d" % i) for i in range(2)]
            wd = [Slot([NHH, 128], F32R, "wd%d" % i) for i in range(3)]
            sg = [Slot([TT], F32, "sg%d" % i) for i in range(2)]
            wgu_r = wgu.rearrange("(k p) n -> p k n", p=128)
            wdn_r = wdn.rearrange("(j p) n -> p j n", p=128)
            cnt = {"n": 0, "d": 0}

            def tile_body(i):
                x = xt
                load(x, cm(xT)[:, :, i * TF:(i + 1) * TF])
                for s in range(2):
                    rmsnorm(x, gcol, vec[l], xn, sq, rstd, TT, bank=0, cs=slice(s * TT, (s + 1) * TT))
                for half in range(2):
                    for jj in range(NHH):
                        j = half * NHH + jj
                        g_, u_ = wg[j % 2], wu[j % 2]
                        load(g_, wgu_r[:, :, j * 128:(j + 1) * 128])
                        load(u_, wgu_r[:, :, DFF + j * 128:DFF + (j + 1) * 128])
                        for s in range(2):
                            cs = slice(s * TT, (s + 1) * TT)
                            n = cnt["n"]
                            cnt["n"] += 1
                            bg, bu = 1 + n % 2, 3 + n % 2
                            mm_group(bg, [(g_.ap[:, k, :], xn.ap[:, k, cs]) for k in range(NK)], [g_.buf, xn.buf])
                            mm_group(bu, [(u_.ap[:, k, :], xn.ap[:, k, cs]) for k in range(NK)], [u_.buf, xn.buf])
                            s_ = sg[n % 2]
                            P.op("act", lambda e, s_=s_, bg=bg: e.activation(out=s_.ap, in_=psum[bg][:, :], func=AF.Silu),
                                 reads=[PS[bg]], writes=[s_.buf])
                            P.op("dve", lambda e, s_=s_, bu=bu, jj=jj, cs=cs: e.tensor_tensor(
                                out=h.ap[:, jj, cs], in0=s_.ap, in1=psum[bu][:, :], op=ALU.mult),
                                reads=[s_.buf, PS[bu]], writes=[h.buf])
                    for c in range(NK):
                        w_ = wd[c % 3]
                        load(w_, wdn_r[:, half * NHH:(half + 1) * NHH, c * 128:(c + 1) * 128])
                        for s in range(2):
                            cs = slice(s * TT, (s + 1) * TT)
                            d = cnt["d"]
                            cnt["d"] += 1
                            bo = 5 + d % 2
                            mm_group(bo, [(w_.ap[:, jj, :], h.ap[:, jj, cs]) for jj in range(NHH)], [w_.buf, h.buf])
                            P.op("dve", lambda e, c=c, bo=bo, cs=cs: e.scalar_tensor_tensor(
                                out=x.ap[:, c, cs], in0=psum[bo][:, :], scalar=0.5, in1=x.ap[:, c, cs],
                                op0=ALU.mult, op1=ALU.add),
                                reads=[PS[bo], x.buf], writes=[x.buf])
                store(x, cm(xT)[:, :, i * TF:(i + 1) * TF])

            for i in range(T // TF):
                tile_body(i)

        def phase_rope():
            new_phase()
            posi = Slot([TT], I32, "posi")
            posf = Slot([TT], F32, "posf")
            a2 = Slot([TT], F32, "a2")
            ki_ = Slot([TT], I32, "ki")
            kf = Slot([TT], F32, "kf")
            y = Slot([TT], F32, "y")
            m = Slot([TT], F32, "m")
            tab = [Slot([TT], F32, "tab%d" % i) for i in range(2)]
            n = 0
            for i in range(NT):
                load(posi, bass.AP(pos_in.tensor, i * TT, [[0, 128], [1, TT]]), eng="pool")
                P.op("dve", lambda e: e.tensor_copy(out=posf.ap, in_=posi.ap), reads=[posi.buf], writes=[posf.buf])
                for ty in range(2):
                    for cs in range(2):
                        shift = math.pi / 2 if cs == 0 else 0.0
                        P.op("dve", lambda e, ty=ty, shift=shift: e.tensor_scalar(
                            out=a2.ap, in0=posf.ap, scalar1=cst.ap[:, 2 * ty:2 * ty + 1], scalar2=shift,
                            op0=ALU.mult, op1=ALU.add), reads=[posf.buf, cst.buf], writes=[a2.buf])
                        P.op("dve", lambda e: e.tensor_scalar(
                            out=ki_.ap, in0=a2.ap, scalar1=1.0 / TWO_PI, scalar2=None, op0=ALU.mult),
                            reads=[a2.buf], writes=[ki_.buf])
                        P.op("dve", lambda e: e.tensor_copy(out=kf.ap, in_=ki_.ap), reads=[ki_.buf], writes=[kf.buf])
                        P.op("dve", lambda e: e.scalar_tensor_tensor(
                            out=y.ap, in0=kf.ap, scalar=-TWO_PI, in1=a2.ap, op0=ALU.mult, op1=ALU.add),
                            reads=[kf.buf, a2.buf], writes=[y.buf])
                        P.op("dve", lambda e: e.tensor_scalar(
                            out=m.ap, in0=y.ap, scalar1=math.pi, scalar2=-TWO_PI, op0=ALU.is_gt, op1=ALU.mult),
                            reads=[y.buf], writes=[m.buf])
                        P.op("dve", lambda e: e.tensor_tensor(out=y.ap, in0=y.ap, in1=m.ap, op=ALU.add),
                             reads=[y.buf, m.buf], writes=[y.buf])
                        P.op("dve", lambda e: e.tensor_scalar(
                            out=m.ap, in0=y.ap, scalar1=-math.pi, scalar2=TWO_PI, op0=ALU.is_lt, op1=ALU.mult),
                            reads=[y.buf], writes=[m.buf])
                        P.op("dve", lambda e: e.tensor_tensor(out=y.ap, in0=y.ap, in1=m.ap, op=ALU.add),
                             reads=[y.buf, m.buf], writes=[y.buf])
                        P.op("dve", lambda e: e.tensor_scalar(
                            out=y.ap, in0=y.ap, scalar1=3.1415925, scalar2=-3.1415925, op0=ALU.min, op1=ALU.max),
                            reads=[y.buf], writes=[y.buf])
                        tb = tab[n % 2]
                        n += 1
                        P.op("act", lambda e, tb=tb: e.activation(out=tb.ap, in_=y.ap, func=AF.Sin),
                             reads=[y.buf], writes=[tb.buf])
                        store(tb, ROPE[2 * ty + cs, :, i * TT:(i + 1) * TT])

        def phase_proj(l):
            new_phase()
            wi = w_in[l].rearrange("(k p) n -> p k n", p=128)
            xt = [Slot([NK, TT], F32, "xt")] * 2
            un = Slot([NK, TT], F32R, "un")
            sq = [Slot([TT], F32R, "sq%d" % i) for i in range(2)]
            rstd = Slot([TT], F32, "rstd")
            rope = [Slot([4, TT], F32, "rope")] * 2
            W = [Slot([NK, 512], F32R, "W%d" % i) for i in range(4)]
            Wk = Slot([NK, 128], F32R, "Wk")
            Ww = Slot([NK, 128], F32R, "Ww")
            pmat = Slot([2, 128], F32R, "pmat")
            ot = [Slot([TT], F32, "ot%d" % i) for i in range(4)]
            otr = [Slot([TT], F32R, "otr%d" % i) for i in range(4)]
            xs = [Slot([TT], F32R, "xs%d" % i) for i in range(2)]
            t1 = [Slot([TT], F32, "t1_%d" % i) for i in range(2)]
            t2 = [Slot([TT], F32, "t2_%d" % i) for i in range(2)]
            st = {"w": 0, "o": 0, "b": 0, "t": 0, "x": 0}
            load(pmat, pm_in.get().rearrange("a p m -> p a m"))

            def nextW():
                st["w"] += 1
                return W[st["w"] % 4]

            def nexto(r=False):
                st["o"] += 1
                return (otr if r else ot)[st["o"] % 4]

            def nextbank():
                st["b"] += 1
                return 1 + st["b"] % 7

            def nextt():
                st["t"] += 1
                return t1[st["t"] % 2], t2[st["t"] % 2]

            def nextx():
                st["x"] += 1
                return xs[st["x"] % 2]

            def loadblk(col0, ncols=512):
                w_ = nextW()
                load(w_, wi[:, :, col0:col0 + ncols], dst=w_.ap[:, :, 0:ncols])
                return w_

            def proj(w_, c0, m=128):
                bank = nextbank()
                mm_group(bank, [(w_.ap[:, k, c0:c0 + m], un.ap[:, k, :]) for k in range(NK)], [w_.buf, un.buf], m=m)
                return bank

            def tile_body(i):
                tsl = slice(i * TT, (i + 1) * TT)
                x = xt[i % 2]
                rp = rope[i % 2]
                load(x, cm(xT)[:, :, tsl])
                load(rp, ROPE[:, :, tsl].rearrange("a p t -> p a t"))
                rmsnorm(x, 8, vec[l], un, sq, rstd, TT, bank=0)

                def plain(w_, c0, dst, row0, func=None):
                    bank = proj(w_, c0)
                    o = nexto()
                    if func is None:
                        P.op("act", lambda e: e.copy(out=o.ap, in_=psum[bank][:, :]), reads=[PS[bank]], writes=[o.buf])
                    else:
                        P.op("act", lambda e: e.activation(out=o.ap, in_=psum[bank][:, :], func=func),
                             reads=[PS[bank]], writes=[o.buf])
                    store(o, dst[row0:row0 + 128, tsl])

                def gelu(w_, c0, dst, row0):
                    bank = proj(w_, c0)
                    a_, b_ = nextt()
                    o = nexto()
                    P.op("act", lambda e: e.activation(out=a_.ap, in_=psum[bank][:, :], func=AF.Square),
                         reads=[PS[bank]], writes=[a_.buf])
                    P.op("dve", lambda e: e.tensor_scalar(out=a_.ap, in0=a_.ap, scalar1=0.044715, scalar2=1.0,
                                                           op0=ALU.mult, op1=ALU.add),
                         reads=[a_.buf], writes=[a_.buf])
                    P.op("dve", lambda e: e.tensor_tensor(out=b_.ap, in0=a_.ap, in1=psum[bank][:, :], op=ALU.mult),
                         reads=[a_.buf, PS[bank]], writes=[b_.buf])
                    P.op("act", lambda e: e.activation(out=b_.ap, in_=b_.ap, func=AF.Sigmoid, scale=1.5957691216057308),
                         reads=[b_.buf], writes=[b_.buf])
                    P.op("dve", lambda e: e.tensor_tensor(out=o.ap, in0=b_.ap, in1=psum[bank][:, :], op=ALU.mult),
                         reads=[b_.buf, PS[bank]], writes=[o.buf])
                    store(o, dst[row0:row0 + 128, tsl])

                def prod(w1, w2, c0, dst, row0):
                    b1 = proj(w1, c0)
                    b2 = proj(w2, c0)
                    a_, b_ = nextt()
                    o = nexto()
                    P.op("act", lambda e: e.copy(out=a_.ap, in_=psum[b1][:, :]), reads=[PS[b1]], writes=[a_.buf])
                    P.op("dve", lambda e: e.tensor_tensor(out=o.ap, in0=a_.ap, in1=psum[b2][:, :], op=ALU.mult),
                         reads=[a_.buf, PS[b2]], writes=[o.buf])
                    store(o, dst[row0:row0 + 128, tsl])

                def roped(w_, c0, ty, dst, row0):
                    b1 = proj(w_, c0)
                    x_ = nextx()
                    P.op("act", lambda e: e.copy(out=x_.ap, in_=psum[b1][:, :]), reads=[PS[b1]], writes=[x_.buf])
                    b2 = nextbank()
                    mm_group(b2, [(pmat.ap[:, ty, :], x_.ap)], [pmat.buf, x_.buf])
                    a_, b_ = nextt()
                    o = nexto(True)
                    P.op("dve", lambda e: e.tensor_tensor(out=a_.ap, in0=f32(x_.ap), in1=rp.ap[:, 2 * ty, :], op=ALU.mult),
                         reads=[x_.buf, rp.buf], writes=[a_.buf])
                    P.op("dve", lambda e: e.tensor_tensor(out=b_.ap, in0=psum[b2][:, :], in1=rp.ap[:, 2 * ty + 1, :], op=ALU.mult),
                         reads=[PS[b2], rp.buf], writes=[b_.buf])
                    P.op("dve", lambda e: e.tensor_tensor(out=o.ap, in0=a_.ap, in1=b_.ap, op=ALU.add),
                         reads=[a_.buf, b_.buf], writes=[o.buf])
                    store(o, dst[row0:row0 + 128, tsl])

                for blk in range(2):
                    w_ = loadblk(blk * 512)
                    for c in range(4):
                        plain(w_, c * 128, RX, (blk * 4 + c) * 128)
                for blk in range(2):
                    w_ = loadblk(1024 + blk * 512)
                    for c in range(4):
                        gelu(w_, c * 128, GG, (blk * 4 + c) * 128)
                for blk in range(2):
                    w_ = loadblk(2048 + blk * 512)
                    for c in range(4):
                        plain(w_, c * 128, CB, (blk * 4 + c) * 128)
                for blk in range(2):
                    w1 = loadblk(3072 + blk * 512)
                    w2 = loadblk(4096 + blk * 512)
                    for c in range(4):
                        prod(w1, w2, c * 128, CCH, (blk * 4 + c) * 128)
                for blk in range(2):
                    w_ = loadblk(5120 + blk * 512)
                    for c in range(4):
                        roped(w_, c * 128, 0, QT, (blk * 4 + c) * 128)
                w_ = loadblk(6144)
                for g in range(2):
                    roped(w_, g * 128, 0, KT, g * 128)
                for s in range(4):
                    bank = nextbank()
                    mm_group(bank, [(un.ap[:, k, s * 128:(s + 1) * 128], w_.ap[:, k, 256:512]) for k in range(NK)],
                             [w_.buf, un.buf], n=256)
                    o = nexto(True)
                    P.op("act", lambda e, o=o, bank=bank: e.copy(out=o.ap[:, 0:256], in_=psum[bank][:, 0:256]),
                         reads=[PS[bank]], writes=[o.buf])
                    store(o, VV[i * TT + s * 128:i * TT + (s + 1) * 128, :], src=o.ap[:, 0:256])
                w_ = loadblk(6656)
                for c in range(4):
                    roped(w_, c * 128, 1, QIT, c * 128)
                load(Wk, wi[:, :, 7168:7232], dst=Wk.ap[:, :, 0:64])
                load(Wk, wi[:, :, 7168:7232], dst=Wk.ap[:, :, 64:128])
                roped(Wk, 0, 1, KIT, 0)
                load(Ww, wi[:, :, 7232:7360])
                bank = proj(Ww, 0)
                o = nexto(True)
                wsc = (8 ** -0.5) * (64 ** -0.5)
                P.op("act", lambda e, o=o, bank=bank: e.activation(out=o.ap[0:8, :], in_=psum[bank][0:8, :], func=AF.Copy, scale=wsc),
                     reads=[PS[bank]], writes=[o.buf])
                store(o, WIT[:, tsl], src=o.ap[0:8, :])
                for blk in range(6):
                    w_ = loadblk(7240 + blk * 512)
                    for c in range(4):
                        plain(w_, c * 128, GT, (blk * 4 + c) * 128, func=AF.Sigmoid)

            for i in range(NT):
                tile_body(i)

        def phase_rnn(l):
            new_phase()
            v = vec[l]
            S2 = range(2)
            bda = [Slot([128], F32R, "bda%d" % k) for k in S2]
            bdx = [Slot([128], F32R, "bdx%d" % k) for k in S2]
            c1 = [Slot([1], F32, "c1_%d" % k) for k in S2]
            tmpc = [Slot([1], F32, "tmpc%d" % k) for k in S2]
            rx = [[Slot([TT + 8], F32R, "rx%d_%d" % (k, i)) for i in range(2)] for k in S2]
            gg = [[Slot([TT], F32R, "gg%d_%d" % (k, i)) for i in range(2)] for k in S2]
            xa = [Slot([TT], F32, "xa%d" % k) for k in S2]
            xar = [Slot([TT], F32R, "xar%d" % k) for k in S2]
            r_ = [Slot([TT], F32, "r%d" % k) for k in S2]
            ig = [Slot([TT], F32, "ig%d" % k) for k in S2]
            a_ = [Slot([TT], F32, "a%d" % k) for k in S2]
            a2 = [Slot([TT], F32, "a2%d" % k) for k in S2]
            u_ = [Slot([TT], F32, "u%d" % k) for k in S2]
            hs = [[Slot([TT], F32, "h%d_%d" % (k, i)) for i in range(2)] for k in S2]
            hg = [[Slot([TT], F32R, "hg%d_%d" % (k, i)) for i in range(2)] for k in S2]
            cx = [[Slot([TT + 8], F32R, "cx%d_%d" % (k, i)) for i in range(2)] for k in S2]
            cb = [[Slot([TT], F32R, "cb%d_%d" % (k, i)) for i in range(2)] for k in S2]
            y = [Slot([TT], F32, "y%d" % k) for k in S2]
            yo = [[Slot([TT], F32R, "yo%d_%d" % (k, i)) for i in range(2)] for k in S2]

            def rnn_setup(c, k):
                P.op("act", lambda e: e.activation(out=bda[k].ap, in_=ones_f.ap, func=AF.Copy, scale=0.0),
                     reads=[ones_f.buf], writes=[bda[k].buf])
                P.op("act", lambda e: e.activation(out=bdx[k].ap, in_=ones_f.ap, func=AF.Copy, scale=0.0),
                     reads=[ones_f.buf], writes=[bdx[k].buf])
                for r in range(2):
                    load(bda[k], w_ga[l, 2 * c + r], eng="pool", dst=bda[k].ap[r * 64:(r + 1) * 64, r * 64:(r + 1) * 64])
                    load(bdx[k], w_gx[l, 2 * c + r], eng="pool", dst=bdx[k].ap[r * 64:(r + 1) * 64, r * 64:(r + 1) * 64])
                P.op("act", lambda e: e.activation(out=tmpc[k].ap, in_=v.ap[:, 48 + c:49 + c], func=AF.Exp, scale=-1.0),
                     reads=[v.buf], writes=[tmpc[k].buf])
                P.op("act", lambda e: e.activation(out=tmpc[k].ap, in_=tmpc[k].ap, func=AF.Ln, bias=1.0, scale=1.0),
                     reads=[tmpc[k].buf], writes=[tmpc[k].buf])
                P.op("dve", lambda e: e.tensor_scalar(out=c1[k].ap, in0=tmpc[k].ap, scalar1=-8.0, scalar2=None, op0=ALU.mult),
                     reads=[tmpc[k].buf], writes=[c1[k].buf])

            def rnn_tile(c, i, k):
                rows = slice(c * 128, (c + 1) * 128)
                tsl = slice(i * TT, (i + 1) * TT)
                x_, g_ = rx[k][i % 2], gg[k][i % 2]
                hcur, hprev, ho = hs[k][i % 2], hs[k][(i + 1) % 2], hg[k][i % 2]
                xa_, xar_, rr, ig_, aa, a2_, uu = xa[k], xar[k], r_[k], ig[k], a_[k], a2[k], u_[k]
                b1, b2 = 1 + 2 * k, 2 + 2 * k
                if i == 0:
                    P.op("act", lambda e: e.activation(out=x_.ap[:, 0:3], in_=ones_f.ap[:, 0:3], func=AF.Copy, scale=0.0), reads=[ones_f.buf], writes=[x_.buf])
                    load(x_, RX[rows, 0:TT], dst=x_.ap[:, 3:TT + 3])
                else:
                    load(x_, RX[rows, i * TT - 3:(i + 1) * TT], dst=x_.ap[:, 0:TT + 3])
                load(g_, GG[rows, tsl])
                yield
                cw = [v.ap[:, 56 + kk * 8 + c:57 + kk * 8 + c] for kk in range(4)]
                P.op("dve", lambda e: e.tensor_scalar(out=xa_.ap, in0=f32(x_.ap)[:, 0:TT], scalar1=cw[0],
                                                      scalar2=v.ap[:, 24 + c:25 + c], op0=ALU.mult, op1=ALU.add),
                     reads=[x_.buf, v.buf], writes=[xa_.buf])
                for kk in (1, 2, 3):
                    P.op("dve", lambda e, kk=kk: e.scalar_tensor_tensor(
                        out=xa_.ap, in0=f32(x_.ap)[:, kk:kk + TT], scalar=cw[kk], in1=xa_.ap, op0=ALU.mult, op1=ALU.add),
                        reads=[x_.buf, v.buf, xa_.buf], writes=[xa_.buf])
                P.op("act", lambda e: e.copy(out=xar_.ap, in_=xa_.ap), reads=[xa_.buf], writes=[xar_.buf])
                yield
                mm_group(b1, [(bda[k].ap, xar_.ap)], [bda[k].buf, xar_.buf])
                mm_group(b2, [(bdx[k].ap, xar_.ap)], [bdx[k].buf, xar_.buf])
                yield
                P.op("act", lambda e: e.activation(out=rr.ap, in_=psum[b1][:, :], func=AF.Sigmoid,
                                                   bias=v.ap[:, 32 + c:33 + c], scale=1.0),
                     reads=[PS[b1], v.buf], writes=[rr.buf])
                P.op("act", lambda e: e.activation(out=ig_.ap, in_=psum[b2][:, :], func=AF.Sigmoid,
                                                   bias=v.ap[:, 40 + c:41 + c], scale=1.0),
                     reads=[PS[b2], v.buf], writes=[ig_.buf])
                yield
                P.op("act", lambda e: e.activation(out=aa.ap, in_=rr.ap, func=AF.Exp, scale=c1[k].ap[:, 0:1]),
                     reads=[rr.buf, c1[k].buf], writes=[aa.buf])
                yield
                P.op("dve", lambda e: e.tensor_tensor(out=a2_.ap, in0=aa.ap, in1=aa.ap, op=ALU.mult),
                     reads=[aa.buf], writes=[a2_.buf])
                P.op("act", lambda e: e.activation(out=a2_.ap, in_=a2_.ap, func=AF.Sqrt, bias=1.0, scale=-1.0),
                     reads=[a2_.buf], writes=[a2_.buf])
                yield
                P.op("dve", lambda e: e.tensor_tensor(out=uu.ap, in0=ig_.ap, in1=xa_.ap, op=ALU.mult),
                     reads=[ig_.buf, xa_.buf], writes=[uu.buf])
                P.op("dve", lambda e: e.tensor_tensor(out=uu.ap, in0=uu.ap, in1=a2_.ap, op=ALU.mult),
                     reads=[uu.buf, a2_.buf], writes=[uu.buf])
                yield
                if i == 0:
                    P.op("dve", lambda e: e.tensor_tensor_scan(
                        out=hcur.ap, data0=aa.ap, data1=uu.ap, initial=0.0, op0=ALU.mult, op1=ALU.add),
                        reads=[aa.buf, uu.buf], writes=[hcur.buf])
                else:
                    P.op("dve", lambda e: e.tensor_tensor_scan(
                        out=hcur.ap, data0=aa.ap, data1=uu.ap, initial=hprev.ap[:, TT - 1:TT],
                        op0=ALU.mult, op1=ALU.add),
                        reads=[aa.buf, uu.buf, hprev.buf], writes=[hcur.buf])
                P.op("dve", lambda e: e.tensor_tensor(out=ho.ap, in0=hcur.ap, in1=f32(g_.ap), op=ALU.mult),
                     reads=[hcur.buf, g_.buf], writes=[ho.buf])
                yield
                P.op_dummy = None
                store(ho, HG[rows, tsl])
                yield

            def sconv_tile(c, i, k):
                rows = slice(c * 128, (c + 1) * 128)
                tsl = slice(i * TT, (i + 1) * TT)
                x_, b_, o, y_ = cx[k][i % 2], cb[k][i % 2], yo[k][i % 2], y[k]
                if i == 0:
                    P.op("act", lambda e: e.activation(out=x_.ap[:, 0:2], in_=ones_f.ap[:, 0:2], func=AF.Copy, scale=0.0), reads=[ones_f.buf], writes=[x_.buf])
                    load(x_, CCH[rows, 0:TT], dst=x_.ap[:, 2:TT + 2])
                else:
                    load(x_, CCH[rows, i * TT - 2:(i + 1) * TT], dst=x_.ap[:, 0:TT + 2])
                load(b_, CB[rows, tsl])
                yield
                sw = [v.ap[:, 88 + kk * 8 + c:89 + kk * 8 + c] for kk in range(3)]
                P.op("dve", lambda e: e.tensor_scalar(out=y_.ap, in0=f32(x_.ap)[:, 0:TT], scalar1=sw[0], scalar2=None,
                                                      op0=ALU.mult), reads=[x_.buf, v.buf], writes=[y_.buf])
                for kk in (1, 2):
                    P.op("dve", lambda e, kk=kk: e.scalar_tensor_tensor(
                        out=y_.ap, in0=f32(x_.ap)[:, kk:kk + TT], scalar=sw[kk], in1=y_.ap, op0=ALU.mult, op1=ALU.add),
                        reads=[x_.buf, v.buf, y_.buf], writes=[y_.buf])
                P.op("dve", lambda e: e.tensor_tensor(out=o.ap, in0=y_.ap, in1=f32(b_.ap), op=ALU.mult),
                     reads=[y_.buf, b_.buf], writes=[o.buf])
                store(o, YB[rows, tsl])
                yield

            for cp_ in range(NK // 2):
                for k in S2:
                    rnn_setup(2 * cp_ + k, k)
                for i in range(NT):
                    gens = [rnn_tile(2 * cp_ + k, i, k) for k in S2] + [sconv_tile(2 * cp_ + k, i, k) for k in S2]
                    while gens:
                        for g_ in list(gens):
                            try:
                                next(g_)
                            except StopIteration:
                                gens.remove(g_)

        def phase_sconv(l):
            pass

        def phase_attn(l):
            new_phase()
            kt = Slot([2, T], F32R, "kt")
            vt = Slot([T // 128, 256], F32R, "vt")
            kit = Slot([T], F32R, "kit")
            sel = Slot([8 * 128], F32R, "sel")
            sc = Slot([T // 128, TQ], F32, "sc")
            scb = [Buf("scb%d" % i) for i in range(T // 256)]
            qt = Slot([8, TQ], F32R, "qt")
            qit = Slot([4, TQ], F32R, "qit")
            wit = [Slot([TQ], F32R, "wit")] * 2
            qp = Slot([4, TQ], F32R, "qp")
            qm = Slot([4, TQ], F32R, "qm")
            wp = [Slot([TQ], F32, "wp")] * 2
            wm = [Slot([TQ], F32, "wm")] * 2
            acc2 = Slot([2, TQ], F32, "acc2")
            rtmp = [Slot([2, TQ], F32, "rtmp%d" % i) for i in range(2)]
            lo = Slot([TQ], F32, "lo")
            mid = Slot([TQ], F32, "mid")
            tmp = Slot([TQ], F32, "tmp")
            CBN = 8
            cmpb = [Slot([CBN, TQ], BF16, "cmp%d" % i) for i in range(2)]
            ee = [Slot([2, TQ], F32, "ee%d" % i) for i in range(2)]
            pm_ = [Slot([2, TQ], F32R, "pm%d" % i) for i in range(2)]
            rden = Slot([2, TQ], F32, "rden")
            oh = [Slot([2, TQ], F32R, "oh%d" % i) for i in range(2)]
            load(kt, KT.rearrange("(g p) t -> p g t", p=128))
            load(vt, VV.rearrange("(j p) d -> p j d", p=128))
            load(kit, KIT.get())
            load(sel, sel_in.get(), eng="pool", dst=sel.ap[0:8, :])
            ascale = 128 ** -0.5
            cst_ = {"cn": 0, "en": 0, "on": 0}

            def bcast(slot, nb):
                a = slot.ap
                return bass.AP(a.tensor, a.offset, [list(a.ap[0]), [0, nb], [1, TQ]])

            def attn_q(q):
                t0 = q * TQ
                nkt = 2 * (q + 1)
                Q_, QI, WI = qt, qit, wit[q % 2]
                load(Q_, cm(QT)[:, :, t0:t0 + TQ])
                load(QI, cm(QIT)[:, :, t0:t0 + TQ])
                load(WI, WIT[:, t0:t0 + TQ], dst=WI.ap[0:8, :])
                for hh in range(8):
                    c, r = divmod(hh, 2)
                    bank = 4 + hh % 2
                    mm_group(bank, [(sel.ap[0:8, hh * 128:(hh + 1) * 128], WI.ap[0:8, :])], [sel.buf, WI.buf], n=TQ)
                    p_, m_ = wp[hh % 2], wm[hh % 2]
                    P.op("dve", lambda e, p_=p_, bank=bank: e.tensor_scalar(out=p_.ap, in0=psum[bank][:, 0:TQ], scalar1=0.0,
                                                                            scalar2=None, op0=ALU.max),
                         reads=[PS[bank]], writes=[p_.buf])
                    P.op("dve", lambda e, m_=m_, bank=bank: e.tensor_scalar(out=m_.ap, in0=psum[bank][:, 0:TQ], scalar1=0.0,
                                                                            scalar2=None, op0=ALU.min),
                         reads=[PS[bank]], writes=[m_.buf])
                    rs = slice(r * 64, (r + 1) * 64)
                    P.op("pool", lambda e, p_=p_, c=c, rs=rs: e.tensor_tensor(
                        out=qp.ap[rs, c, :], in0=f32(QI.ap)[rs, c, :], in1=p_.ap[rs, :], op=ALU.mult),
                        reads=[QI.buf, p_.buf], writes=[qp.buf])
                    P.op("pool", lambda e, m_=m_, c=c, rs=rs: e.tensor_tensor(
                        out=qm.ap[rs, c, :], in0=f32(QI.ap)[rs, c, :], in1=m_.ap[rs, :], op=ALU.mult),
                        reads=[QI.buf, m_.buf], writes=[qm.buf])
                for j in range(0, nkt, 2):
                    scj = sc.ap[:, j:j + 2, :]
                    for hh in range(8):
                        c, r = divmod(hh, 2)
                        rs = slice(r * 64, (r + 1) * 64)
                        b1, b2 = 2 * (hh % 2), 2 * (hh % 2) + 1
                        for (bk, qq) in ((b1, qp), (b2, qm)):
                            def fn(e, bk=bk, qq=qq, rs=rs, c=c, j=j):
                                inst = None
                                for jj in range(2):
                                    inst = e.matmul(psum[bk][:, jj * TQ:(jj + 1) * TQ],
                                                    lhsT=kit.ap[rs, (j + jj) * 128:(j + jj + 1) * 128],
                                                    rhs=qq.ap[rs, c, :], start=True, stop=True)
                                return inst
                            P.op("pe", fn, reads=[kit.buf, qq.buf], writes=[PS[bk]])
                        ps1 = psum[b1][:, :].rearrange("p (a b) -> p a b", a=2)
                        ps2 = psum[b2][:, :].rearrange("p (a b) -> p a b", a=2)
                        if hh == 0:
                            P.op("dve", lambda e, ps1=ps1, scj=scj: e.tensor_scalar(
                                out=scj, in0=ps1, scalar1=0.0, scalar2=None, op0=ALU.max),
                                reads=[PS[b1]], writes=[scb[j // 2]])
                        else:
                            P.op("dve", lambda e, ps1=ps1, scj=scj: e.scalar_tensor_tensor(
                                out=scj, in0=ps1, scalar=0.0, in1=scj, op0=ALU.max, op1=ALU.add),
                                reads=[PS[b1], scb[j // 2]], writes=[scb[j // 2]])
                        if hh < 2:
                            P.op("dve", lambda e, ps2=ps2, scj=scj: e.scalar_tensor_tensor(
                                out=scj, in0=ps2, scalar=0.0, in1=scj, op0=ALU.min, op1=ALU.add),
                                reads=[PS[b2], scb[j // 2]], writes=[scb[j // 2]])
                            continue
                        rt = rtmp[hh % 2]
                        P.op("act", lambda e, ps2=ps2, rt=rt: e.activation(out=rt.ap, in_=ps2, func=AF.Relu, scale=-1.0),
                             reads=[PS[b2]], writes=[rt.buf])
                        if hh == 2:
                            P.op("pool", lambda e, rt=rt: e.tensor_copy(out=acc2.ap, in_=rt.ap),
                                 reads=[rt.buf], writes=[acc2.buf])
                        else:
                            P.op("pool", lambda e, rt=rt: e.tensor_tensor(out=acc2.ap, in0=acc2.ap, in1=rt.ap, op=ALU.add),
                                 reads=[rt.buf, acc2.buf], writes=[acc2.buf])
                    P.op("dve", lambda e, scj=scj: e.tensor_tensor(out=scj, in0=scj, in1=acc2.ap, op=ALU.subtract),
                         reads=[scb[j // 2], acc2.buf], writes=[scb[j // 2]])
                    if j >= 2 * q:
                        def fsel(e, j=j, scj=scj):
                            if "r" not in fillreg:
                                fillreg["r"] = e.to_reg(-1.0e30)
                            return e.affine_select(
                                out=scj, in_=scj, pattern=[[-128, 2], [1, TQ]], compare_op=ALU.is_ge,
                                fill=fillreg["r"], base=t0 - j * 128, channel_multiplier=-1)
                        P.op("pool", fsel, reads=[scb[j // 2]], writes=[scb[j // 2]])
                P.op("dve", lambda e: e.memset(lo.ap, -8.0), writes=[lo.buf])
                for it in range(NIT):
                    dstep = 8.0 / (2 ** it)
                    P.op("dve", lambda e, dstep=dstep: e.tensor_scalar(out=mid.ap, in0=lo.ap, scalar1=dstep, scalar2=None,
                                                                      op0=ALU.add), reads=[lo.buf], writes=[mid.buf])
                    cb_ = 6 + it % 2
                    npool = 0
                    chunks = []
                    j0 = 0
                    while j0 < nkt - npool:
                        nb = min(CBN, nkt - npool - j0)
                        chunks.append(("dve", j0, nb))
                        j0 += nb
                    if npool:
                        chunks.insert(min(1, len(chunks)), ("pool", nkt - npool, npool))
                    for ci, (ceng, j0, nb) in enumerate(chunks):
                        cp = cmpb[cst_["cn"] % 2]
                        cst_["cn"] += 1
                        P.op(ceng, lambda e, cp=cp, j0=j0, nb=nb: e.tensor_tensor(
                            out=cp.ap[:, 0:nb, :], in0=sc.ap[:, j0:j0 + nb, :], in1=bcast(mid, nb), op=ALU.is_gt),
                            reads=scb[j0 // 2:(j0 + nb) // 2] + [mid.buf], writes=[cp.buf])
                        def fn(e, cp=cp, nb=nb, cb_=cb_, first=(ci == 0), last=(ci == len(chunks) - 1)):
                            inst = None
                            for jj in range(nb):
                                inst = e.matmul(psum[cb_][:, 0:TQ], lhsT=ones_b.ap, rhs=cp.ap[:, jj, :],
                                                start=(first and jj == 0), stop=(last and jj == nb - 1))
                            return inst
                        P.op("pe", fn, reads=[cp.buf, ones_b.buf], writes=[PS[cb_]])
                    P.op("dve", lambda e, cb_=cb_, dstep=dstep: e.tensor_scalar(
                        out=tmp.ap, in0=psum[cb_][:, 0:TQ], scalar1=255.5, scalar2=dstep, op0=ALU.is_ge, op1=ALU.mult),
                        reads=[PS[cb_]], writes=[tmp.buf])
                    P.op("dve", lambda e: e.tensor_tensor(out=lo.ap, in0=lo.ap, in1=tmp.ap, op=ALU.add),
                         reads=[lo.buf, tmp.buf], writes=[lo.buf])
                for j0 in range(0, nkt, CBN):
                    nb = min(CBN, nkt - j0)
                    P.op("dve", lambda e, j0=j0, nb=nb: e.tensor_tensor(
                        out=sc.ap[:, j0:j0 + nb, :], in0=sc.ap[:, j0:j0 + nb, :], in1=bcast(lo, nb), op=ALU.is_gt),
                        reads=scb[j0 // 2:(j0 + nb) // 2] + [lo.buf], writes=scb[j0 // 2:(j0 + nb) // 2])
                steps = [(hp, j) for hp in range(1 if SKIP_HEADS else 4) for j in range(nkt)]

                def issue_s(n):
                    hp, j = steps[n]
                    bs = (cst_["en"] + n) % 2
                    mm_group(bs, [(kt.ap[:, hp // 2, j * 128:(j + 1) * 128], Q_.ap[:, 2 * hp:2 * hp + 2, :])],
                             [kt.buf, Q_.buf])

                issue_s(0)
                for n, (hp, j) in enumerate(steps):
                    g = hp // 2
                    bo, bd = (2, 3) if hp % 2 == 0 else (4, 5)
                    en = cst_["en"] + n
                    bs = en % 2
                    E = ee[en % 2]
                    Pm = pm_[en % 2]
                    P.op("act", lambda e, E=E, bs=bs: e.activation(
                        out=E.ap, in_=psum[bs][:, :].rearrange("p (a b) -> p a b", a=2), func=AF.Exp, scale=ascale),
                        reads=[PS[bs]], writes=[E.buf])
                    if n + 1 < len(steps):
                        issue_s(n + 1)
                    mj = sc.ap[:, j, :]
                    mjb = bass.AP(mj.tensor, mj.offset, [list(mj.ap[0]), [0, 2], [1, TQ]])
                    P.op("dve", lambda e, E=E, Pm=Pm, mjb=mjb: e.tensor_tensor(
                        out=Pm.ap, in0=E.ap, in1=mjb, op=ALU.mult),
                        reads=[E.buf, scb[j // 2]], writes=[Pm.buf])
                    def fo(e, Pm=Pm, j=j, g=g, bo=bo):
                        return e.matmul(psum[bo][:, :], lhsT=vt.ap[:, j, g * 128:(g + 1) * 128],
                                        rhs=Pm.ap, start=(j == 0), stop=(j == nkt - 1))
                    P.op("pe", fo, reads=[vt.buf, Pm.buf], writes=[PS[bo]])
                    def fd(e, Pm=Pm, j=j, bd=bd):
                        return e.matmul(psum[bd][:, :], lhsT=ones_r.ap, rhs=Pm.ap,
                                        start=(j == 0), stop=(j == nkt - 1))
                    P.op("pe", fd, reads=[ones_r.buf, Pm.buf], writes=[PS[bd]])
                    if j == nkt - 1:
                        P.op("dve", lambda e, bd=bd: e.reciprocal(
                            out=rden.ap, in_=psum[bd][:, :].rearrange("p (a b) -> p a b", a=2)),
                            reads=[PS[bd]], writes=[rden.buf])
                        o_ = oh[cst_["on"] % 2]
                        cst_["on"] += 1
                        P.op("dve", lambda e, bo=bo, o_=o_: e.tensor_tensor(
                            out=o_.ap, in0=psum[bo][:, :].rearrange("p (a b) -> p a b", a=2), in1=rden.ap, op=ALU.mult),
                            reads=[PS[bo], rden.buf], writes=[o_.buf])
                        store(o_, AT[2 * hp * 128:(2 * hp + 2) * 128, t0:t0 + TQ].rearrange("(h p) t -> p h t", p=128))
                cst_["en"] += len(steps)

            for q in range(NQ):
                attn_q(q)

        def phase_merge(l):
            new_phase()
            xt = Slot([NK, TT], F32, "xt")
            br = [Slot([NK, TT], F32R, "br%d" % i) for i in range(2)]
            gt = [Slot([4, TT], F32, "gt%d" % i) for i in range(2)]
            mg = Slot([NK, TT], F32, "mg")
            mgr = Slot([NK, TT], F32R, "mgr")
            tmp = [Slot([TT], F32, "tmp%d" % i) for i in range(2)]
            W = [Slot([NK, 512], F32R, "W%d" % i) for i in range(4)]
            srcs = [(HG, w_ro[l]), (YB, w_so[l]), (AT, w_ao[l])]
            st = {"w": 0, "b": 0, "t": 0, "g": 0}

            def tile_body(i):
                tsl = slice(i * TT, (i + 1) * TT)
                x = xt
                load(x, cm(xT)[:, :, tsl])
                load(br[0], cm(srcs[0][0])[:, :, tsl])
                for b, (src, wmat) in enumerate(srcs):
                    s_ = br[b % 2]
                    if b + 1 < 3:
                        load(br[(b + 1) % 2], cm(srcs[b + 1][0])[:, :, tsl])
                    wr = wmat.rearrange("(k p) n -> p k n", p=128)
                    for blk in range(2):
                        w_ = W[st["w"] % 4]
                        st["w"] += 1
                        load(w_, wr[:, :, blk * 512:(blk + 1) * 512])
                        g_ = gt[st["g"] % 2]
                        st["g"] += 1
                        load(g_, cm(GT[b * D + blk * 512:b * D + (blk + 1) * 512, :])[:, :, tsl])
                        for cc in range(4):
                            c = blk * 4 + cc
                            bank = 1 + st["b"] % 6
                            st["b"] += 1
                            mm_group(bank, [(w_.ap[:, k, cc * 128:(cc + 1) * 128], s_.ap[:, k, :]) for k in range(NK)],
                                     [w_.buf, s_.buf])
                            if b == 0:
                                P.op("dve", lambda e, bank=bank, g_=g_, c=c, cc=cc: e.tensor_tensor(
                                    out=mg.ap[:, c, :], in0=psum[bank][:, :], in1=g_.ap[:, cc, :], op=ALU.mult),
                                    reads=[PS[bank], g_.buf], writes=[mg.buf])
                            else:
                                t_ = tmp[st["t"] % 2]
                                st["t"] += 1
                                dst = mg.ap[:, c, :] if b < 2 else mgr.ap[:, c, :]
                                dbuf = mg.buf if b < 2 else mgr.buf
                                P.op("dve", lambda e, bank=bank, g_=g_, c=c, cc=cc, t_=t_: e.tensor_tensor(
                                    out=t_.ap, in0=psum[bank][:, :], in1=g_.ap[:, cc, :], op=ALU.mult),
                                    reads=[PS[bank], g_.buf], writes=[t_.buf])
                                P.op("dve", lambda e, t_=t_, c=c, dst=dst: e.tensor_tensor(
                                    out=dst, in0=t_.ap, in1=mg.ap[:, c, :], op=ALU.add),
                                    reads=[t_.buf, mg.buf], writes=[dbuf])
                wr = w_o[l].rearrange("(k p) n -> p k n", p=128)
                for blk in range(2):
                    w_ = W[st["w"] % 4]
                    st["w"] += 1
                    load(w_, wr[:, :, blk * 512:(blk + 1) * 512])
                    for cc in range(4):
                        c = blk * 4 + cc
                        bank = 1 + st["b"] % 6
                        st["b"] += 1
                        mm_group(bank, [(w_.ap[:, k, cc * 128:(cc + 1) * 128], mgr.ap[:, k, :]) for k in range(NK)],
                                 [w_.buf, mgr.buf])
                        P.op("dve", lambda e, bank=bank, c=c: e.tensor_tensor(
                            out=x.ap[:, c, :], in0=psum[bank][:, :], in1=x.ap[:, c, :], op=ALU.add),
                            reads=[PS[bank], x.buf], writes=[x.buf])
                store(x, cm(xT)[:, :, tsl])

            for i in range(NT):
                tile_body(i)

        sched = []
        sched.append(("x0", phase_x0))
        sched.append(("rope", phase_rope))
        for l in range(nlayers):
            sched.append(("ffn1_%d" % l, lambda l=l: phase_ffn(l, 0)))
            sched.append(("proj_%d" % l, lambda l=l: phase_proj(l)))
            sched.append(("rnn_%d" % l, lambda l=l: phase_rnn(l)))
            sched.append(("sconv_%d" % l, lambda l=l: phase_sconv(l)))
            sched.append(("attn_%d" % l, lambda l=l: phase_attn(l)))
            sched.append(("merge_%d" % l, lambda l=l: phase_merge(l)))
            sched.append(("ffn2_%d" % l, lambda l=l: phase_ffn(l, 1)))
        sched.append(("out", phase_out))
        for name, fn in sched:
            if phases == "all" or name in phases:
                fn()
        P.barrier()
        P.build(nc)
    _USED[id(nc)] = list(used_inputs)
    return nc


def make_consts():
    theta = 500000.0
    cst = np.zeros((128, 16), np.float32)
    p = np.arange(128)
    invA = (theta ** (-(np.arange(0, 32, 2, dtype=np.float32)) / 32)).astype(np.float32)
    invB = (theta ** (-(np.arange(0, 16, 2, dtype=np.float32)) / 16)).astype(np.float32)
    cst[:32, 0] = invA[p[:32] % 16]
    cst[:, 1] = np.where(p < 16, -1.0, 1.0)
    pm = p % 64
    cst[:, 2] = np.where(pm < 16, invB[pm % 8], 0.0)
    cst[:, 3] = np.where(pm < 8, -1.0, 1.0)
    ident = np.eye(128, dtype=np.float32)
    sel = np.zeros((8, 8 * 128), np.float32)
    for h in range(8):
        sel[h, h * 128:(h + 1) * 128] = 1.0
    pm = np.zeros((2, 128, 128), np.float32)
    for i in range(16):
        pm[0, i + 16, i] = -1.0
        pm[0, i, i + 16] = 1.0
    for hb in (0, 64):
        for i in range(8):
            pm[1, hb + i + 8, hb + i] = -1.0
            pm[1, hb + i, hb + i + 8] = 1.0
    return cst, ident, sel, pm


def pack_vec(inp):
    def col(v):
        return np.ascontiguousarray(v.reshape(8, 128).T)
    vec = np.zeros((L, 128, NVEC), np.float32)
    for l in range(L):
        vec[l, :, 0:8] = col(inp["ffn1_norm"][l])
        vec[l, :, 8:16] = col(inp["mix_norm"][l])
        vec[l, :, 16:24] = col(inp["ffn2_norm"][l])
        vec[l, :, 24:32] = col(inp["rnn_conv_b"][l])
        vec[l, :, 32:40] = col(inp["rnn_gate_a_b"][l])
        vec[l, :, 40:48] = col(inp["rnn_gate_x_b"][l])
        vec[l, :, 48:56] = col(inp["rnn_lambda"][l])
        for k in range(4):
            vec[l, :, 56 + 8 * k:64 + 8 * k] = col(inp["rnn_conv_w"][l, k])
        for k in range(3):
            vec[l, :, 88 + 8 * k:96 + 8 * k] = col(inp["sconv_w"][l, k])
    fin = col(inp["final_norm"])
    return vec, fin


def make_in_maps(inp, batches, used=None, extra=None):
    cst, ident, sel, pm = make_consts()
    vec, fin = pack_vec(inp)
    shared = {k: np.ascontiguousarray(np.asarray(inp[k], dtype=np.float32)) for k in (
        "ffn1_w_gate_up", "ffn2_w_gate_up", "ffn1_w_down", "ffn2_w_down", "w_in", "rnn_gate_a_w",
        "rnn_gate_x_w", "rnn_w_out", "sconv_w_out", "attn_w_out", "w_o")}
    shared.update({"vec": vec, "fin": fin, "cst": cst, "ident": ident, "sel": sel, "pm": pm})
    maps = []
    for b in batches:
        m = dict(shared)
        m["x"] = np.ascontiguousarray(np.asarray(inp["x"][b], dtype=np.float32))
        m["positions"] = np.ascontiguousarray(np.asarray(inp["positions"][b:b + 1], dtype=np.int32))
        if extra:
            m.update(extra)
        if used is not None:
            m = {k: v for k, v in m.items() if k in used}
        maps.append(m)
    return maps


_NC_CACHE = {}


def kernel(**inputs):
    inp = {k: np.asarray(v) for k, v in inputs.items()}
    if "full" not in _NC_CACHE:
        _NC_CACHE["full"] = build_program()
    nc = _NC_CACHE["full"]
    maps = make_in_maps(inp, range(4), used=_USED[id(nc)])
    res = run_bass_kernel_spmd(nc, maps, core_ids=list(range(4)))
    return np.stack([np.asarray(r["out"], dtype=np.float32) for r in res.results], axis=0)
```
